# Optimizing a Trainium2 kernel written in Bass

```python
import jax, jax.numpy as jnp
from jax import lax
import numpy as np

D_MODEL = 1024
BATCH = 4
SEQ = 8192
DEPTH = 4

CTX_LEN = 256
GRID_W = 64
HEAD_DIM = 64
N_HEADS = D_MODEL // HEAD_DIM
NA_HEADS = N_HEADS // 2
WA_HEADS = N_HEADS - NA_HEADS
WA_KV_HEADS = max(1, WA_HEADS // 4)
NA_KH = 8
NA_KW = 16
WA_WINDOW = 128
WA_BLOCK = 128
D_FF = ((8 * D_MODEL // 3 + 127) // 128) * 128
ROPE_BASE = 10000.0
N_MOD = 9
MACARON_W = 0.5
NORM_EPS = 1e-6
NEG_INF = -1e30
NA_W = NA_HEADS * HEAD_DIM
WA_W = WA_HEADS * HEAD_DIM
WA_KV_W = WA_KV_HEADS * HEAD_DIM
Q_W = NA_W + WA_W
IN_W = Q_W + 2 * NA_W + 2 * WA_KV_W

kernel_name = "hybrid_natten_swa_macaron_dit_block"


def rms_norm(x, g):
    xf = x.astype(jnp.float32)
    y = xf * lax.rsqrt(jnp.mean(xf * xf, axis=-1, keepdims=True) + NORM_EPS)
    return (y * g.astype(jnp.float32)).astype(x.dtype)


def modulate(h, shift, scale):
    return h * (1 + scale) + shift


def ada_params(cond, w_ada, b_ada):
    m = jax.nn.silu(cond) @ w_ada + b_ada
    return m.reshape(m.shape[:-1] + (N_MOD, D_MODEL))


def swiglu_half_step(x, g, mod, j, w_up, w_down):
    h = modulate(rms_norm(x, g), mod[..., j, :], mod[..., j + 1, :])
    gt, up = jnp.split(h @ w_up, 2, axis=-1)
    return x + MACARON_W * mod[..., j + 2, :] * ((jax.nn.silu(gt) * up) @ w_down)


def heads(t):
    return t.reshape(t.shape[:-1] + (t.shape[-1] // HEAD_DIM, HEAD_DIM))


def axial_rope_tables(n_tokens):
    rot = HEAD_DIM // 2
    inv_freq = ROPE_BASE ** (-jnp.arange(0, rot, 2, dtype=jnp.float32) / rot)
    t = jnp.arange(n_tokens)
    row = (t // GRID_W).astype(jnp.float32)
    col = (t % GRID_W).astype(jnp.float32)
    ang = jnp.stack([row[:, None] * inv_freq, col[:, None] * inv_freq], axis=1)
    return jnp.cos(ang), jnp.sin(ang)


def apply_axial_rope(x, cos, sin):
    B, S, H, Dh = x.shape
    xr = x.reshape(B, S, H, 2, 2, Dh // 4)
    x1, x2 = xr[..., 0, :], xr[..., 1, :]
    c = cos[:, None].astype(x.dtype)
    s = sin[:, None].astype(x.dtype)
    out = jnp.stack([x1 * c - x2 * s, x2 * c + x1 * s], axis=-2)
    return out.reshape(B, S, H, Dh)


def context_self_attention(q, k, v, sink):
    B, C, Hq, Dh = q.shape
    Hkv = k.shape[2]
    G = Hq // Hkv
    qg = (q * Dh ** -0.5).reshape(B, C, Hkv, G, Dh)
    s = jnp.einsum('bqkgd,bckd->bkgqc', qg, k).astype(jnp.float32)
    if sink is not None:
        sink_col = jnp.broadcast_to(sink.astype(jnp.float32).reshape(1, Hkv, G, 1, 1), s.shape[:-1] + (1,))
        s = jnp.concatenate([s, sink_col], axis=-1)
    p = jax.nn.softmax(s, axis=-1)[..., :C].astype(v.dtype)
    out = jnp.einsum('bkgqc,bckd->bqkgd', p, v)
    return out.reshape(B, C, Hq * Dh)


def neighbourhood_attention(q, k, v, k_ctx, v_ctx, rpb):
    B, S, H, Dh = q.shape
    rows = S // GRID_W
    kh = min(NA_KH, rows)
    n_nb = kh * NA_KW
    qg = (q * Dh ** -0.5).reshape(B, rows, GRID_W, H, Dh)
    kg = k.reshape(B, rows, GRID_W, H, Dh)
    vg = v.reshape(B, rows, GRID_W, H, Dh)
    col = jnp.arange(GRID_W)
    c0 = jnp.clip(col - NA_KW // 2, 0, GRID_W - NA_KW)
    col_idx = c0[:, None] + jnp.arange(NA_KW)[None, :]
    dc = col_idx - col[:, None] + (NA_KW - 1)
    rpb32 = rpb.astype(jnp.float32)

    def one_row(r):
        r0 = jnp.clip(r - kh // 2, 0, rows - kh)
        q_r = lax.dynamic_index_in_dim(qg, r, axis=1, keepdims=False)
        k_rows = lax.dynamic_slice_in_dim(kg, r0, kh, axis=1)
        v_rows = lax.dynamic_slice_in_dim(vg, r0, kh, axis=1)
        k_win = k_rows[:, :, col_idx]
        v_win = v_rows[:, :, col_idx]
        dr = r0 + jnp.arange(kh) - r + (NA_KH - 1)
        bias = rpb32[:, dr[:, None, None], dc[None, :, :]].transpose(0, 2, 1, 3)
        s_nb = jnp.einsum('bqhd,biqjhd->bhqij', q_r, k_win).astype(jnp.float32) + bias[None]
        s_ctx = jnp.einsum('bqhd,bchd->bhqc', q_r, k_ctx).astype(jnp.float32)
        logits = jnp.concatenate([s_nb.reshape(B, H, GRID_W, n_nb), s_ctx], axis=-1)
        p = jax.nn.softmax(logits, axis=-1).astype(v.dtype)
        p_nb = p[..., :n_nb].reshape(B, H, GRID_W, kh, NA_KW)
        return (jnp.einsum('bhqij,biqjhd->bqhd', p_nb, v_win)
                + jnp.einsum('bhqc,bchd->bqhd', p[..., n_nb:], v_ctx))

    out = lax.map(one_row, jnp.arange(rows))
    return out.transpose(1, 0, 2, 3, 4).reshape(B, S, H * Dh)


def windowed_gqa_attention(q, k, v, k_ctx, v_ctx, sink):
    B, S, Hq, Dh = q.shape
    Hkv = k.shape[2]
    G = Hq // Hkv
    nb = S // WA_BLOCK
    n_loc = 3 * WA_BLOCK
    qb = (q * Dh ** -0.5).reshape(B, nb, WA_BLOCK, Hkv, G, Dh)
    pad = ((0, 0), (WA_BLOCK, WA_BLOCK), (0, 0), (0, 0))
    kp = jnp.pad(k, pad)
    vp = jnp.pad(v, pad)
    q_off = jnp.arange(WA_BLOCK)
    k_off = jnp.arange(n_loc) - WA_BLOCK
    band = jnp.abs(k_off[None, :] - q_off[:, None]) <= WA_WINDOW
    sink_col = jnp.broadcast_to(sink.astype(jnp.float32).reshape(1, Hkv, G, 1, 1), (B, Hkv, G, WA_BLOCK, 1))

    def one_block(i):
        q_i = lax.dynamic_index_in_dim(qb, i, axis=1, keepdims=False)
        k_i = lax.dynamic_slice_in_dim(kp, i * WA_BLOCK, n_loc, axis=1)
        v_i = lax.dynamic_slice_in_dim(vp, i * WA_BLOCK, n_loc, axis=1)
        k_pos = i * WA_BLOCK + k_off
        valid = band & ((k_pos >= 0) & (k_pos < S))[None, :]
        s_loc = jnp.einsum('bqkgd,bjkd->bkgqj', q_i, k_i).astype(jnp.float32)
        s_loc = jnp.where(valid, s_loc, NEG_INF)
        s_ctx = jnp.einsum('bqkgd,bckd->bkgqc', q_i, k_ctx).astype(jnp.float32)
        p = jax.nn.softmax(jnp.concatenate([s_loc, s_ctx, sink_col], axis=-1), axis=-1).astype(v.dtype)
        out = (jnp.einsum('bkgqj,bjkd->bqkgd', p[..., :n_loc], v_i)
               + jnp.einsum('bkgqc,bckd->bqkgd', p[..., n_loc:-1], v_ctx))
        return out.reshape(B, WA_BLOCK, Hq * Dh)

    out = lax.map(one_block, jnp.arange(nb))
    return out.transpose(1, 0, 2, 3).reshape(B, S, Hq * Dh)


def split_kv(p):
    o1, o2, o3 = NA_W, 2 * NA_W, 2 * NA_W + WA_KV_W
    return p[..., :o1], p[..., o1:o2], p[..., o2:o3], p[..., o3:]


def setup_inputs(seed: int = 0) -> dict:
    key = jax.random.key(seed)
    ks = jax.random.split(key, 14)
    f32 = jnp.float32

    def nrm(k, shape, scale):
        return jax.random.normal(k, shape, f32) * scale

    return {
        "x": nrm(ks[0], (BATCH, SEQ, D_MODEL), 1.0),
        "c": nrm(ks[1], (BATCH, D_MODEL), 1.0),
        "ctx": nrm(ks[2], (BATCH, CTX_LEN, D_MODEL), 1.0),
        "c_ctx": nrm(ks[3], (D_MODEL,), 1.0),
        "w_ada": nrm(ks[4], (DEPTH, D_MODEL, N_MOD * D_MODEL), 0.5 * D_MODEL ** -0.5),
        "b_ada": nrm(ks[5], (DEPTH, N_MOD * D_MODEL), 0.02),
        "norm_g": 1.0 + nrm(ks[6], (DEPTH, 3, D_MODEL), 0.02),
        "w_ffn_up": nrm(ks[7], (DEPTH, 2, D_MODEL, 2 * D_FF), D_MODEL ** -0.5),
        "w_ffn_down": nrm(ks[8], (DEPTH, 2, D_FF, D_MODEL), D_FF ** -0.5),
        "w_in": nrm(ks[9], (DEPTH, D_MODEL, IN_W), D_MODEL ** -0.5),
        "w_out": nrm(ks[10], (DEPTH, Q_W, D_MODEL), Q_W ** -0.5),
        "qk_norm_g": 1.0 + nrm(ks[11], (DEPTH, 4, HEAD_DIM), 0.02),
        "na_rpb": nrm(ks[12], (DEPTH, NA_HEADS, 2 * NA_KH - 1, 2 * NA_KW - 1), 0.1),
        "wa_sink": nrm(ks[13], (DEPTH, WA_HEADS), 0.5),
    }


def reference(x, c, ctx, c_ctx, w_ada, b_ada, norm_g, w_ffn_up, w_ffn_down, w_in, w_out,
              qk_norm_g, na_rpb, wa_sink):
    S = x.shape[1]
    cos, sin = axial_rope_tables(S)
    h_ctx = ctx
    for l in range(DEPTH):
        last = l == DEPTH - 1
        mod_x = ada_params(c, w_ada[l], b_ada[l])[:, None]
        mod_c = ada_params(c_ctx, w_ada[l], b_ada[l])[None, None]

        x = swiglu_half_step(x, norm_g[l, 0], mod_x, 0, w_ffn_up[l, 0], w_ffn_down[l, 0])
        h_ctx = swiglu_half_step(h_ctx, norm_g[l, 0], mod_c, 0, w_ffn_up[l, 0], w_ffn_down[l, 0])

        hx = modulate(rms_norm(x, norm_g[l, 1]), mod_x[..., 3, :], mod_x[..., 4, :])
        hc = modulate(rms_norm(h_ctx, norm_g[l, 1]), mod_c[..., 3, :], mod_c[..., 4, :])
        px = hx @ w_in[l]
        pc = hc @ (w_in[l, :, Q_W:] if last else w_in[l])
        pc_kv = pc[..., -(IN_W - Q_W):]

        qa = rms_norm(heads(px[..., :NA_W]), qk_norm_g[l, 0])
        qb = rms_norm(heads(px[..., NA_W:Q_W]), qk_norm_g[l, 2])
        ka, va, kb, vb = split_kv(px[..., Q_W:])
        ka = rms_norm(heads(ka), qk_norm_g[l, 1])
        kb = rms_norm(heads(kb), qk_norm_g[l, 3])
        va, vb = heads(va), heads(vb)
        qb = apply_axial_rope(qb, cos, sin)
        kb = apply_axial_rope(kb, cos, sin)

        ka_c, va_c, kb_c, vb_c = split_kv(pc_kv)
        ka_c = rms_norm(heads(ka_c), qk_norm_g[l, 1])
        kb_c = rms_norm(heads(kb_c), qk_norm_g[l, 3])
        va_c, vb_c = heads(va_c), heads(vb_c)

        out_a = neighbourhood_attention(qa, ka, va, ka_c, va_c, na_rpb[l])
        out_b = windowed_gqa_attention(qb, kb, vb, kb_c, vb_c, wa_sink[l])
        x = x + mod_x[..., 5, :] * (jnp.concatenate([out_a, out_b], axis=-1) @ w_out[l])

        if not last:
            qa_c = rms_norm(heads(pc[..., :NA_W]), qk_norm_g[l, 0])
            qb_c = rms_norm(heads(pc[..., NA_W:Q_W]), qk_norm_g[l, 2])
            out_a_c = context_self_attention(qa_c, ka_c, va_c, None)
            out_b_c = context_self_attention(qb_c, kb_c, vb_c, wa_sink[l])
            h_ctx = h_ctx + mod_c[..., 5, :] * (jnp.concatenate([out_a_c, out_b_c], axis=-1) @ w_out[l])

        x = swiglu_half_step(x, norm_g[l, 2], mod_x, 6, w_ffn_up[l, 1], w_ffn_down[l, 1])
        if not last:
            h_ctx = swiglu_half_step(h_ctx, norm_g[l, 2], mod_c, 6, w_ffn_up[l, 1], w_ffn_down[l, 1])
    return x
```

```python
import numpy as np
import ml_dtypes
from contextlib import ExitStack
import concourse.bass as bass
import concourse.mybir as mybir
from concourse.bass_utils import run_bass_kernel_spmd

F32 = mybir.dt.float32
BF16 = mybir.dt.bfloat16
AF = mybir.ActivationFunctionType
ALU = mybir.AluOpType
AX = mybir.AxisListType

D = 1024
DFF = 2816
NFC = 22
INW = 2304
DEPTH = 4
SEQ = 8192
NLAT = 40
NT_ALL = 42
CT0 = 40
NOUT = [38, 36, 34, 32]
NKV = [40, 38, 36, 34]
EPS = 1e-6
MASKV = -30000.0
GT = 4
SAME_ENG_SYNC = True


class Prod:
    __slots__ = ("sem", "val")

    def __init__(self, sem):
        self.sem = sem
        self.val = 0


class Res:
    __slots__ = ("name", "w", "r")

    def __init__(self, name):
        self.name = name
        self.w = {}
        self.r = {}


class Eng:
    def __init__(self, nc, name, h, ndma=0):
        self.nc = nc
        self.name = name
        self.h = h
        self.prod = Prod(nc.alloc_semaphore("e_" + name))
        self.seen = {}
        self.dsem = [Prod(nc.alloc_semaphore("d_%s_%d" % (name, i))) for i in range(ndma)]
        self.di = 0
        self.nwait = 0
        self.nins = 0

    def need(self, deps):
        for p, v in deps.items():
            if p is self.prod and not SAME_ENG_SYNC:
                continue
            if self.seen.get(p, 0) >= v:
                continue
            self.h.wait_ge(p.sem, v)
            self.nwait += 1
            self.seen[p] = v


def _merge(d, s):
    for p, v in s.items():
        if d.get(p, 0) < v:
            d[p] = v


def _deps(reads, writes, pwrites):
    deps = {}
    for r in reads:
        _merge(deps, r.w)
    for w in writes:
        _merge(deps, w.w)
        _merge(deps, w.r)
    for w in pwrites:
        _merge(deps, w.w)
        _merge(deps, w.r)
    return deps


def _commit(p, v, reads, writes, pwrites):
    for r in reads:
        if r.r.get(p, 0) < v:
            r.r[p] = v
    for w in writes:
        w.w = {p: v}
        w.r = {}
    for w in pwrites:
        if w.w.get(p, 0) < v:
            w.w[p] = v


def op(e, fn, reads=(), writes=(), pwrites=(), pe_inorder=False):
    deps = _deps(reads, writes, pwrites)
    if pe_inorder:
        deps.pop(e.prod, None)
    e.need(deps)
    ins = fn()
    e.prod.val += 1
    ins.then_inc(e.prod.sem, 1)
    e.nins += 1
    _commit(e.prod, e.prod.val, reads, writes, pwrites)


def dma(q, out, in_, reads=(), writes=(), pwrites=(), **kw):
    deps = _deps(reads, writes, pwrites)
    q.need(deps)
    s = q.dsem[q.di % len(q.dsem)]
    q.di += 1
    if s.val > 0 and q.seen.get(s, 0) < s.val:
        q.h.wait_ge(s.sem, s.val)
        q.seen[s] = s.val
    q.h.dma_start(out=out, in_=in_, **kw).then_inc(s.sem, 16)
    s.val += 16
    _commit(s, s.val, reads, writes, pwrites)
    return s


def lat_groups(n):
    gs = []
    t = 0
    while t < n:
        k = min(GT, n - t)
        gs.append(list(range(t, t + k)))
        t += k
    return gs


def build_program(stop_after=None, dumps=(), btest=None, bparts=('na', 'wa', 'proj')):
    nc = bass.Bass("TRN2", target_bir_lowering=False, dynamic_dma_scratch_size=8192)

    def din(name, shape, dt=F32, big=False):
        if big and btest is not None:
            return None
        return nc.dram_tensor(name, list(shape), dt, kind="ExternalInput").ap()

    def dscr(name, shape, dt, ext=False):
        if ext and btest is not None:
            return nc.dram_tensor("in_" + name, list(shape), dt, kind="ExternalInput").ap()
        return nc.dram_tensor(name, list(shape), dt).ap()

    xin = din("xin", [NLAT * 128, D], big=True)
    ctxin = din("ctxin", [256, D], big=True)
    cvec = din("cvec", [2, D], big=True)
    w_ada = din("w_ada", [DEPTH, D, 9 * D], big=True)
    b_ada = din("b_ada", [DEPTH, 9 * D], big=True)
    norm_g = din("norm_g", [DEPTH, 3, D], big=True)
    w_up = din("w_up", [DEPTH, 2, D, 2 * DFF], big=True)
    w_dn = din("w_dn", [DEPTH, 2, DFF, D], big=True)
    w_in = din("w_in", [DEPTH, D, INW], big=True)
    w_out = din("w_out", [DEPTH, D, D], big=True)
    qkg = din("qkg", [DEPTH, 4, 64])
    sinkp = din("sinkp", [DEPTH, 8])
    nab = din("nab", [DEPTH, 13, 8, 128, 128])
    rope = din("rope", [NLAT * 128, 128], big=True)
    wam = din("wam", [2, 128, 128], BF16)
    identf_d = din("identf_in", [128, 128])
    yout = nc.dram_tensor("yout", [32 * 128, D], F32, kind="ExternalOutput").ap()

    wupb = dscr("wupb", [DEPTH, 2, NFC, 128, 8, 256], BF16)
    wdnb = dscr("wdnb", [DEPTH, 2, NFC, 128, D], BF16)
    winb = dscr("winb", [DEPTH, 128, 8, INW], BF16)
    woutb = dscr("woutb", [DEPTH, 128, 8, D], BF16, ext=True)
    modtab = dscr("modtab", [DEPTH, 2, 9, D], F32, ext=True)
    xres = dscr("xres", [NT_ALL * 128, D], F32, ext=True)
    qT_d = dscr("qT_d", [2, NT_ALL, 128, 8, 128], BF16, ext=True)
    kT_d = dscr("kT_d", [2, NT_ALL, 128, 6, 128], BF16, ext=True)
    v_d = dscr("v_d", [2, NT_ALL, 128, 12, 128], BF16, ext=True)

    xres_w = xres
    if btest is not None:
        xres_w = nc.dram_tensor("xres_out", [NT_ALL * 128, D], F32, kind="ExternalOutput").ap()
    dump_out = {}
    for name in dumps:
        src = {"xres": xres, "qT_d": qT_d, "kT_d": kT_d, "v_d": v_d, "modtab": modtab}[name]
        dump_out[name] = (nc.dram_tensor("dump_" + name, list(src.shape), src.dtype, kind="ExternalOutput").ap(), src)

    es = ExitStack()
    with es:
        def sb(name, shape, dt=F32):
            return es.enter_context(nc.sbuf_tensor(name, list(shape), dt))

        pe = Eng(nc, "pe", nc.tensor)
        act = Eng(nc, "act", nc.scalar)
        dve = Eng(nc, "dve", nc.vector)
        pool = Eng(nc, "pool", nc.gpsimd, ndma=6)
        sp = Eng(nc, "sp", nc.sync, ndma=32)

        psum = es.enter_context(nc.psum_tensor("psum", [128, 8, 512], F32))
        bankres = [Res("bank%d" % i) for i in range(8)]

        r_wup = [[Res("wup%d%d" % (l, f)) for f in range(2)] for l in range(DEPTH)]
        r_wdn = [[Res("wdn%d%d" % (l, f)) for f in range(2)] for l in range(DEPTH)]
        r_win = [Res("win%d" % l) for l in range(DEPTH)]
        r_wout = [Res("wout%d" % l) for l in range(DEPTH)]
        r_modtab = [Res("modtab%d" % l) for l in range(DEPTH)]
        r_xres = Res("xres")
        r_qkv = [Res("qkv0"), Res("qkv1")]

        identf = sb("identf", [128, 128])
        r_ident = Res("ident")
        wam_sb = sb("wam_sb", [128, 2, 128], BF16)
        r_wam = Res("wam")
        esink = sb("esink", [128, 8])
        r_esink = Res("esink")
        gcols = sb("gcols", [128, 2])
        gbc = sb("gbc", [128, 2, 64])
        r_g = Res("g")

        block = es.enter_context(nc.Block())

        def program(_eng):
            def cast_up(l, f):
                for c in range(NFC):
                    for half in range(2):
                        col0 = half * DFF + c * 128
                        dma(pool, wupb[l, f, c][:, :, half * 128:(half + 1) * 128],
                            w_up[l, f][:, col0:col0 + 128].rearrange("(kc p) n -> p kc n", p=128),
                            pwrites=[r_wup[l][f]])

            def cast_dn(l, f):
                for c in range(NFC):
                    dma(pool, wdnb[l, f, c], w_dn[l, f][c * 128:(c + 1) * 128, :], pwrites=[r_wdn[l][f]])

            def cast_in(l):
                for kc in range(8):
                    for hh in range(2):
                        dma(pool, winb[l][:, kc, hh * 1152:(hh + 1) * 1152],
                            w_in[l][kc * 128:(kc + 1) * 128, hh * 1152:(hh + 1) * 1152], pwrites=[r_win[l]])

            def cast_out(l):
                for kc in range(8):
                    dma(pool, woutb[l][:, kc, :], w_out[l][kc * 128:(kc + 1) * 128, :], pwrites=[r_wout[l]])

            for l in range(DEPTH if btest is None else 0):
                cast_up(l, 0)
                cast_dn(l, 0)
                cast_in(l)
                cast_out(l)
                cast_up(l, 1)
                cast_dn(l, 1)

            dma(sp, identf[:], identf_d, writes=[r_ident])
            dma(sp, wam_sb[:], wam.rearrange("r k q -> k r q"), writes=[r_wam])

            def emit_modtabs():
                with ExitStack() as ms:
                    def msb(name, shape, dt=F32):
                        return ms.enter_context(nc.sbuf_tensor(name, list(shape), dt))
                    ccol = msb("ccol", [128, 8, 2])
                    scol = msb("scol", [128, 8, 2], BF16)
                    wst = [msb("wst%d" % i, [128, 8, 512]) for i in range(2)]
                    wbf = [msb("wbf%d" % i, [128, 8, 512], BF16) for i in range(2)]
                    bch = [msb("bch%d" % i, [2, 512]) for i in range(2)]
                    gch = [msb("gch%d" % i, [2, 512]) for i in range(2)]
                    och = [msb("och%d" % i, [2, 512]) for i in range(2)]
                    r_ccol, r_scol = Res("ccol"), Res("scol")
                    r_wst = [Res("wst0"), Res("wst1")]
                    r_wbf = [Res("wbf0"), Res("wbf1")]
                    r_bch = [Res("bch0"), Res("bch1")]
                    r_gch = [Res("gch0"), Res("gch1")]
                    r_och = [Res("och0"), Res("och1")]
                    for w_ in range(2):
                        dma(sp, ccol[:, :, w_], cvec[w_, :].rearrange("(kc p) -> p kc", p=128), pwrites=[r_ccol],
                            allow_slow_non_contiguous=True)
                    op(act, lambda: nc.scalar.activation(out=scol[:], in_=ccol[:], func=AF.Silu),
                       reads=[r_ccol], writes=[r_scol])
                    it = 0
                    for l in range(DEPTH):
                        for cc in range(18):
                            s = it % 2
                            m, half = cc // 2, cc % 2
                            j, kind = m // 3, m % 3
                            dma(sp, wst[s][:], w_ada[l][:, cc * 512:(cc + 1) * 512].rearrange("(kc p) n -> p kc n", p=128),
                                writes=[r_wst[s]])
                            dma(sp, bch[s][:], b_ada[l:l + 1, cc * 512:(cc + 1) * 512].partition_broadcast(2), writes=[r_bch[s]])
                            if kind == 1:
                                dma(sp, gch[s][:], norm_g[l, j:j + 1, half * 512:(half + 1) * 512].partition_broadcast(2),
                                    writes=[r_gch[s]])
                            op(dve, lambda: nc.vector.tensor_copy(out=wbf[s][:], in_=wst[s][:]),
                               reads=[r_wst[s]], writes=[r_wbf[s]])
                            bk = it % 8

                            def mm():
                                for kc in range(8):
                                    ins = nc.tensor.matmul(psum[0:2, bk, :], lhsT=scol[:, kc, :], rhs=wbf[s][:, kc, :],
                                                           start=(kc == 0), stop=(kc == 7))
                                return ins
                            op(pe, mm, reads=[r_scol, r_wbf[s]], writes=[bankres[bk]], pe_inorder=True)
                            if kind == 0:
                                op(dve, lambda: nc.vector.tensor_tensor(out=och[s][:], in0=psum[0:2, bk, :], in1=bch[s][:], op=ALU.add),
                                   reads=[bankres[bk], r_bch[s]], writes=[r_och[s]])
                            elif kind == 1:
                                op(dve, lambda: nc.vector.scalar_tensor_tensor(out=och[s][:], in0=psum[0:2, bk, :], scalar=1.0,
                                                                               in1=bch[s][:], op0=ALU.add, op1=ALU.add),
                                   reads=[bankres[bk], r_bch[s]], writes=[r_och[s]])
                                op(dve, lambda: nc.vector.tensor_tensor(out=och[s][:], in0=och[s][:], in1=gch[s][:], op=ALU.mult),
                                   reads=[r_gch[s], r_och[s]], writes=[r_och[s]])
                            else:
                                gsc = 1.0 if j == 1 else 0.5
                                op(dve, lambda: nc.vector.tensor_tensor(out=och[s][:], in0=psum[0:2, bk, :], in1=bch[s][:], op=ALU.add),
                                   reads=[bankres[bk], r_bch[s]], writes=[r_och[s]])
                                if gsc != 1.0:
                                    op(dve, lambda: nc.vector.tensor_scalar(out=och[s][:], in0=och[s][:], scalar1=gsc, scalar2=None, op0=ALU.mult),
                                       reads=[r_och[s]], writes=[r_och[s]])
                            dma(sp, modtab[l, :, m, half * 512:(half + 1) * 512], och[s][:], reads=[r_och[s]],
                                pwrites=[r_modtab[l]])
                            it += 1

            if btest is None:
                emit_modtabs()

            def load_layer_consts(l):
                for hh in range(2):
                    dma(sp, gcols[hh * 64:(hh + 1) * 64, 0:1], qkg[l, 0:1, :].rearrange("o d -> d o"), pwrites=[r_g],
                        allow_slow_non_contiguous=True)
                    dma(sp, gcols[hh * 64:(hh + 1) * 64, 1:2], qkg[l, 1:2, :].rearrange("o d -> d o"), pwrites=[r_g],
                        allow_slow_non_contiguous=True)
                dma(sp, gbc[:, 0, :], qkg[l, 2:3, :].partition_broadcast(128), pwrites=[r_g])
                dma(sp, gbc[:, 1, :], qkg[l, 3:4, :].partition_broadcast(128), pwrites=[r_g])
                op(dve, lambda: nc.vector.tensor_scalar(out=gcols[:, 0:1], in0=gcols[:, 0:1], scalar1=0.125, scalar2=None,
                                                        op0=ALU.mult), reads=[r_g], pwrites=[r_g])
                op(dve, lambda: nc.vector.tensor_scalar(out=gbc[:, 0, :], in0=gbc[:, 0, :], scalar1=0.125, scalar2=None,
                                                        op0=ALU.mult), reads=[r_g], pwrites=[r_g])
                dma(sp, esink[:], sinkp[l:l + 1, :].partition_broadcast(128), writes=[r_esink])

            def barrier():
                allp = {}
                for e in (pe, act, dve):
                    if e.prod.val:
                        allp[e.prod] = e.prod.val
                for sd in sp.dsem:
                    if sd.val:
                        allp[sd] = sd.val
                for e in (pe, act, dve, sp):
                    e.need(dict(allp))

            def pass_F(l):
                do_ffn2 = l >= 1
                do_ffn1 = l <= DEPTH - 1
                n_lat = NKV[l] if l <= DEPTH - 1 else NOUT[DEPTH - 1]
                groups = [("x", g) for g in lat_groups(n_lat)]
                if l <= DEPTH - 1:
                    groups.append(("c", [CT0, CT0 + 1]))
                par = l % 2
                barrier()
                with ExitStack() as fs:
                    def fsb(name, shape, dt=F32):
                        return fs.enter_context(nc.sbuf_tensor("%s_F%d" % (name, l), list(shape), dt))
                    NXB = 2
                    xg = [fsb("xg%d" % i, [128, GT, D]) for i in range(NXB)]
                    r_xg = [[Res("xg%d_%d" % (i, t)) for t in range(GT)] for i in range(NXB)]
                    hnT = fsb("hnT", [128, 8, GT * 128], BF16)
                    r_hnT = [Res("hnT%d" % t) for t in range(GT)]
                    hidT = fsb("hidT", [128, NFC, GT * 128], BF16)
                    r_hid = Res("hidT")
                    NUP = 3
                    wup = [fsb("wup%d" % i, [128, 8, 256], BF16) for i in range(NUP)]
                    r_wups = [Res("wups%d" % i) for i in range(NUP)]
                    wdn = fsb("wdn", [128, NFC, D], BF16)
                    r_wdns = Res("wdns")
                    NWI = 2
                    win = [fsb("win%d" % i, [128, 8, 512], BF16) for i in range(NWI)]
                    r_wins = [Res("wins%d" % i) for i in range(NWI)]
                    xn = [fsb("xn%d" % i, [128, D]) for i in range(2)]
                    r_xn = [Res("xn0"), Res("xn1")]
                    mt = [fsb("mt%d" % i, [128, 512]) for i in range(2)]
                    r_mt = [Res("mt0"), Res("mt1")]
                    mtc = [0]
                    stat = fsb("stat", [128, 64])
                    r_stat = Res("stat")
                    sg = [fsb("sg%d" % i, [128, 512], BF16) for i in range(2)]
                    r_sg = [Res("sg0"), Res("sg1")]
                    tmpy = [fsb("tmpy%d" % i, [128, 512]) for i in range(2)]
                    r_tmpy = [Res("tmpy0"), Res("tmpy1")]
                    mcols = fsb("mcols", [128, 2, 6, 8])
                    mgate = fsb("mgate", [128, 2, D])
                    r_mod = Res("mod")
                    r_mgate = Res("mgate")
                    sq = [fsb("sq%d" % i, [128, 512]) for i in range(1)]
                    r_sq = [Res("sq0")]
                    NQN = 3
                    qn = [fsb("qn%d" % i, [128, 512]) for i in range(NQN)]
                    r_qn = [Res("qn%d" % i) for i in range(NQN)]
                    qg = fsb("qg", [128, 512])
                    r_qg = Res("qg")
                    t1 = fsb("t1", [128, 512])
                    r_t1 = Res("t1")
                    tu = fsb("tu", [128, 256])
                    r_tu = Res("tu")
                    NQR = 3
                    qrs = [fsb("qr%d" % i, [128, 512]) for i in range(NQR)]
                    r_qrs = [Res("qr%d" % i) for i in range(NQR)]
                    qrc = [0]
                    kdups = [fsb("kdup%d" % i, [128, 2, 128]) for i in range(NQR)]
                    r_kdups = [Res("kdup%d" % i) for i in range(NQR)]
                    kdc = [0]
                    ssq = fsb("ssq", [128, 32])
                    r_ssq = Res("ssq")
                    rsq = fsb("rsq", [128, 32])
                    r_rsq = Res("rsq")
                    NST = GT
                    qTs = [fsb("qTs%d" % i, [128, 8, 128], BF16) for i in range(NST)]
                    kTs = [fsb("kTs%d" % i, [128, 6, 128], BF16) for i in range(NST)]
                    vs = [fsb("vs%d" % i, [128, 12, 128], BF16) for i in range(NST)]
                    r_qTs = [Res("qTs%d" % i) for i in range(NST)]
                    r_kTs = [Res("kTs%d" % i) for i in range(NST)]
                    r_vs = [Res("vs%d" % i) for i in range(NST)]
                    ropet = fsb("ropet", [128, GT, 128])
                    r_rope = Res("rope")

                    for i in range(NST):
                        op(dve, lambda: nc.vector.memset(vs[i][:], 1.0), writes=[r_vs[i]])

                    for w in range(2):
                        specs = []
                        if do_ffn2:
                            specs.append((0, l - 1, 2))
                        if do_ffn1:
                            specs.append((1, l, 0))
                            specs.append((2, l, 1))
                        for slot, ll, j in specs:
                            dma(sp, mcols[:, w, 2 * slot, :], modtab[ll, w, 3 * j + 1, :].rearrange("(kc p) -> p kc", p=128),
                                reads=[r_modtab[ll]], pwrites=[r_mod], allow_slow_non_contiguous=True)
                            dma(sp, mcols[:, w, 2 * slot + 1, :], modtab[ll, w, 3 * j, :].rearrange("(kc p) -> p kc", p=128),
                                reads=[r_modtab[ll]], pwrites=[r_mod], allow_slow_non_contiguous=True)

                    def load_mgate(w):
                        if do_ffn2:
                            dma(sp, mgate[:, 0, :], modtab[l - 1, w, 8:9, :].partition_broadcast(128),
                                reads=[r_modtab[l - 1]], pwrites=[r_mgate])
                        if do_ffn1:
                            dma(sp, mgate[:, 1, :], modtab[l, w, 2:3, :].partition_broadcast(128),
                                reads=[r_modtab[l]], pwrites=[r_mgate])
                    load_mgate(0)
                    if do_ffn1:
                        load_layer_consts(l)

                    ringpos = [0]

                    def nextbank():
                        b = ringpos[0] % 8
                        ringpos[0] += 1
                        return b

                    upc = [0]
                    winc = [0]
                    stc = [0]
                    xnc = [0]
                    sgc = [0]
                    tyc = [0]
                    sqc = [0]
                    qnc = [0]

                    def norm_transpose(gi, tl, nt, w, slot):
                        xb = gi % NXB
                        xt = xg[xb][:, tl, :]
                        rx = r_xg[xb][tl]
                        c0 = tl * 2
                        op(act, lambda: nc.scalar.activation(out=sq[0][:].bitcast(BF16), in_=xt, func=AF.Square, accum_out=stat[:, c0:c0 + 1]),
                           reads=[rx], writes=[r_sq[0]], pwrites=[r_stat])
                        op(act, lambda: nc.scalar.activation(out=stat[:, c0 + 1:c0 + 2], in_=stat[:, c0:c0 + 1], func=AF.Ln,
                                                             scale=1.0 / D, bias=EPS),
                           reads=[r_stat], pwrites=[r_stat])
                        op(act, lambda: nc.scalar.activation(out=stat[:, c0 + 1:c0 + 2], in_=stat[:, c0 + 1:c0 + 2], func=AF.Exp,
                                                             scale=-0.5),
                           reads=[r_stat], pwrites=[r_stat])
                        xs = xnc[0] % 2
                        xnc[0] += 1
                        op(dve, lambda: nc.vector.tensor_scalar(out=xn[xs][:], in0=xt, scalar1=stat[:, c0 + 1:c0 + 2], scalar2=None,
                                                                op0=ALU.mult),
                           reads=[rx, r_stat], writes=[r_xn[xs]])
                        for hb in range(2):
                            bk = nextbank()

                            def tr():
                                for i in range(4):
                                    kc = hb * 4 + i
                                    ins = nc.tensor.transpose(psum[:, bk, i * 128:(i + 1) * 128], xn[xs][:, kc * 128:(kc + 1) * 128], identf[:])
                                return ins
                            op(pe, tr, reads=[r_xn[xs], r_ident], writes=[bankres[bk]], pe_inorder=True)
                            pv = psum[:, bk, :].rearrange("p (k t) -> p k t", k=4)
                            ov = hnT[:, hb * 4:(hb + 1) * 4, tl * 128:(tl + 1) * 128]
                            gcol = mcols[:, w, 2 * slot, hb * 4:(hb + 1) * 4].unsqueeze(2).to_broadcast([128, 4, 128])
                            scolb = mcols[:, w, 2 * slot + 1, hb * 4:(hb + 1) * 4].unsqueeze(2).to_broadcast([128, 4, 128])
                            mi = mtc[0] % 2
                            mtc[0] += 1
                            xv = mt[mi][:].rearrange("p (k t) -> p k t", k=4)
                            op(dve, lambda: nc.vector.tensor_tensor(out=xv, in0=pv, in1=gcol, op=ALU.mult),
                               reads=[bankres[bk], r_mod], writes=[r_mt[mi]])
                            op(dve, lambda: nc.vector.tensor_tensor(out=ov, in0=xv, in1=scolb, op=ALU.add),
                               reads=[r_mt[mi], r_mod], pwrites=[r_hnT[tl]])

                    up_pref = {}

                    def load_up(ll, f, c):
                        s = upc[0] % NUP
                        upc[0] += 1
                        dma(sp, wup[s][:], wupb[ll, f, c], reads=[r_wup[ll][f]], writes=[r_wups[s]])
                        return s

                    def ffn(gi, tiles, w, slot, gslot, ll, f, nxt=None):
                        nt = len(tiles)
                        ntok = nt * 128
                        xb = gi % NXB
                        pre = up_pref.pop((ll, f), None)
                        if pre is None:
                            pre = [load_up(ll, f, c) for c in range(min(NUP - 1, NFC))]
                        slots = list(pre)
                        for fc in range(NFC):
                            if fc + NUP - 1 < NFC:
                                slots.append(load_up(ll, f, fc + NUP - 1))
                            if fc < 11:
                                c0, c1 = 2 * fc, 2 * fc + 2
                                dma(sp, wdn[:, c0:c1, :], wdnb[ll, f, c0:c1].rearrange("c p n -> p c n"),
                                    reads=[r_wdn[ll][f]], pwrites=[r_wdns])
                            s = slots[fc]
                            bg, bu = nextbank(), nextbank()

                            def mmg():
                                for kc in range(8):
                                    ins = nc.tensor.matmul(psum[:, bg, 0:ntok], lhsT=wup[s][:, kc, 0:128], rhs=hnT[:, kc, 0:ntok],
                                                           start=(kc == 0), stop=(kc == 7))
                                return ins

                            def mmu():
                                for kc in range(8):
                                    ins = nc.tensor.matmul(psum[:, bu, 0:ntok], lhsT=wup[s][:, kc, 128:256], rhs=hnT[:, kc, 0:ntok],
                                                           start=(kc == 0), stop=(kc == 7))
                                return ins
                            op(pe, mmg, reads=[r_wups[s]] + r_hnT[:nt], writes=[bankres[bg]], pe_inorder=True)
                            op(pe, mmu, reads=[r_wups[s]] + r_hnT[:nt], writes=[bankres[bu]], pe_inorder=True)
                            si = sgc[0] % 2
                            sgc[0] += 1
                            op(act, lambda: nc.scalar.activation(out=sg[si][:, 0:ntok], in_=psum[:, bg, 0:ntok], func=AF.Silu),
                               reads=[bankres[bg]], writes=[r_sg[si]])
                            op(dve, lambda: nc.vector.tensor_tensor(out=hidT[:, fc, 0:ntok], in0=psum[:, bu, 0:ntok], in1=sg[si][:, 0:ntok],
                                                                    op=ALU.mult),
                               reads=[bankres[bu], r_sg[si]], pwrites=[r_hid])
                        if nxt is not None:
                            up_pref[nxt] = [load_up(nxt[0], nxt[1], c) for c in range(min(NUP - 1, NFC))]
                        for tl in range(nt):
                            for nh in range(2):
                                bk = nextbank()

                                def mmd():
                                    for fc in range(NFC):
                                        ins = nc.tensor.matmul(psum[:, bk, :], lhsT=hidT[:, fc, tl * 128:(tl + 1) * 128],
                                                               rhs=wdn[:, fc, nh * 512:(nh + 1) * 512],
                                                               start=(fc == 0), stop=(fc == NFC - 1))
                                    return ins
                                op(pe, mmd, reads=[r_hid, r_wdns], writes=[bankres[bk]], pe_inorder=True)
                                ti = tyc[0] % 2
                                tyc[0] += 1
                                op(dve, lambda: nc.vector.tensor_tensor(out=tmpy[ti][:], in0=psum[:, bk, :],
                                                                        in1=mgate[:, gslot, nh * 512:(nh + 1) * 512], op=ALU.mult),
                                   reads=[bankres[bk], r_mgate], writes=[r_tmpy[ti]])
                                xsl = xg[xb][:, tl, nh * 512:(nh + 1) * 512]
                                op(dve, lambda: nc.vector.tensor_tensor(out=xsl, in0=xsl, in1=tmpy[ti][:], op=ALU.add),
                                   reads=[r_tmpy[ti]], pwrites=[r_xg[xb][tl]])

                    def inproj(gi, tiles, w):
                        nt = len(tiles)
                        xb = gi % NXB
                        if w == 0:
                            r0 = tiles[0] * 128
                            dma(sp, ropet[:, 0:nt, :], rope[r0:r0 + nt * 128, :].rearrange("(t p) c -> p t c", p=128), writes=[r_rope])
                        import collections
                        tails = collections.deque()
                        LAGF = 2
                        sts = []
                        for tl in range(nt):
                            sti = stc[0] % NST
                            stc[0] += 1
                            sts.append(sti)
                        for cc in range(5):
                            ncol = 512 if cc < 4 else 256
                            wsl = winc[0] % NWI
                            winc[0] += 1
                            dma(sp, win[wsl][:, :, 0:ncol], winb[l][:, :, cc * 512:cc * 512 + ncol], reads=[r_win[l]], writes=[r_wins[wsl]])
                            for tl in range(nt):
                                sti = sts[tl]
                                bk = nextbank()

                                def mmi():
                                    for kc in range(8):
                                        ins = nc.tensor.matmul(psum[:, bk, 0:ncol], lhsT=hnT[:, kc, tl * 128:(tl + 1) * 128],
                                                               rhs=win[wsl][:, kc, 0:ncol], start=(kc == 0), stop=(kc == 7))
                                    return ins
                                op(pe, mmi, reads=[r_hnT[tl], r_wins[wsl]], writes=[bankres[bk]], pe_inorder=True)
                                P = psum[:, bk, :]
                                if cc == 3:
                                    Pv = P.rearrange("p (i two d) -> p i two d", i=4, two=2)
                                    vv = vs[sti][:, 0:8, :].rearrange("p (i two) d -> p i two d", two=2)
                                    op(act, lambda: nc.scalar.activation(out=vv[:, :, 0, 0:64], in_=Pv[:, :, 0, :], func=AF.Copy),
                                       reads=[bankres[bk]], pwrites=[r_vs[sti]])
                                    op(act, lambda: nc.scalar.activation(out=vv[:, :, 1, 64:128], in_=Pv[:, :, 1, :], func=AF.Copy),
                                       reads=[bankres[bk]], pwrites=[r_vs[sti]])
                                    continue
                                nh = 8 if cc < 4 else 2
                                ncn = nh * 64
                                so = cc * 8 if cc < 3 else 24
                                if cc == 4:
                                    op(act, lambda: nc.scalar.activation(out=vs[sti][:, 8:10, 0:64], in_=P[:, 128:256].rearrange("p (h d) -> p h d", h=2), func=AF.Copy),
                                       reads=[bankres[bk]], pwrites=[r_vs[sti]])
                                    op(act, lambda: nc.scalar.activation(out=vs[sti][:, 10:12, 64:128], in_=P[:, 128:256].rearrange("p (h d) -> p h d", h=2), func=AF.Copy),
                                       reads=[bankres[bk]], pwrites=[r_vs[sti]])
                                sqi = 0
                                sqc[0] += 1
                                op(act, lambda: nc.scalar.activation(out=sq[sqi][:, 0:ncn], in_=P[:, 0:ncn], func=AF.Square),
                                   reads=[bankres[bk]], writes=[r_sq[sqi]])
                                op(dve, lambda: nc.vector.tensor_reduce(out=ssq[:, so:so + nh], in_=sq[sqi][:, 0:ncn].rearrange("p (h d) -> p h d", h=nh),
                                                                        axis=AX.X, op=ALU.add),
                                   reads=[r_sq[sqi]], pwrites=[r_ssq])
                                op(act, lambda: nc.scalar.activation(out=rsq[:, so:so + nh], in_=ssq[:, so:so + nh], func=AF.Ln, scale=1.0 / 64, bias=EPS),
                                   reads=[r_ssq], pwrites=[r_rsq])
                                op(act, lambda: nc.scalar.activation(out=rsq[:, so:so + nh], in_=rsq[:, so:so + nh], func=AF.Exp, scale=-0.5),
                                   reads=[r_rsq], pwrites=[r_rsq])
                                qi = qnc[0] % NQN
                                qnc[0] += 1
                                op(dve, lambda: nc.vector.tensor_tensor(out=qn[qi][:, 0:ncn].rearrange("p (h d) -> p h d", h=nh),
                                                                        in0=P[:, 0:ncn].rearrange("p (h d) -> p h d", h=nh),
                                                                        in1=rsq[:, so:so + nh].unsqueeze(2).to_broadcast([128, nh, 64]), op=ALU.mult),
                                   reads=[bankres[bk], r_rsq], writes=[r_qn[qi]])
                                if cc in (0, 2):
                                    def tail_a(qi=qi, sti=sti, cc=cc):
                                        b2 = nextbank()

                                        def tr():
                                            for i in range(4):
                                                ins = nc.tensor.transpose(psum[:, b2, i * 128:(i + 1) * 128], qn[qi][:, i * 128:(i + 1) * 128], identf[:])
                                            return ins
                                        op(pe, tr, reads=[r_qn[qi], r_ident], writes=[bankres[b2]], pe_inorder=True)
                                        if cc == 0:
                                            op(act, lambda: nc.scalar.activation(out=qTs[sti][:, 0:4, :], in_=psum[:, b2, :].rearrange("p (k t) -> p k t", k=4),
                                                                                 func=AF.Identity, scale=gcols[:, 0:1]),
                                               reads=[bankres[b2], r_g], pwrites=[r_qTs[sti]])
                                        else:
                                            op(act, lambda: nc.scalar.activation(out=kTs[sti][:, 0:4, :], in_=psum[:, b2, :].rearrange("p (k t) -> p k t", k=4),
                                                                                 func=AF.Identity, scale=gcols[:, 1:2]),
                                               reads=[bankres[b2], r_g], pwrites=[r_kTs[sti]])
                                    tails.append(tail_a)
                                    while len(tails) > LAGF:
                                        tails.popleft()()
                                    continue
                                gi_ = 0 if cc == 1 else 1
                                qri = qrc[0] % NQR
                                qrc[0] += 1
                                qr = qrs[qri]
                                r_qr = r_qrs[qri]
                                if w == 0:
                                    gout, r_gout = qg, r_qg
                                else:
                                    gout, r_gout = qr, r_qr
                                qgv = gout[:, 0:ncn].rearrange("p (h d) -> p h d", h=nh)
                                op(dve, lambda: nc.vector.tensor_tensor(out=qgv, in0=qn[qi][:, 0:ncn].rearrange("p (h d) -> p h d", h=nh),
                                                                        in1=gbc[:, gi_, :].unsqueeze(1).to_broadcast([128, nh, 64]), op=ALU.mult),
                                   reads=[r_qn[qi], r_g], writes=[r_gout])
                                if w == 0:
                                    cosb = ropet[:, tl, 0:64].unsqueeze(1).to_broadcast([128, nh, 64])
                                    op(dve, lambda: nc.vector.tensor_tensor(out=t1[:, 0:ncn].rearrange("p (h d) -> p h d", h=nh), in0=qgv, in1=cosb, op=ALU.mult),
                                       reads=[r_qg, r_rope], writes=[r_t1])
                                    q5 = qg[:, 0:ncn].rearrange("p (h a t f) -> p h a t f", h=nh, a=2, t=2, f=16)
                                    t5 = t1[:, 0:ncn].rearrange("p (h a t f) -> p h a t f", h=nh, a=2, t=2, f=16)
                                    r5 = qr[:, 0:ncn].rearrange("p (h a t f) -> p h a t f", h=nh, a=2, t=2, f=16)
                                    u4 = tu[:, 0:ncn // 2].rearrange("p (h a f) -> p h a f", h=nh, a=2, f=16)
                                    for half in range(2):
                                        sinb = ropet[:, tl, 64 + 32 * half:96 + 32 * half].rearrange("p (a f) -> p a f", a=2).unsqueeze(1).to_broadcast([128, nh, 2, 16])
                                        op(dve, lambda: nc.vector.tensor_tensor(out=u4, in0=q5[:, :, :, 1 - half, :], in1=sinb, op=ALU.mult),
                                           reads=[r_qg, r_rope], writes=[r_tu])
                                        op(dve, lambda: nc.vector.tensor_tensor(out=r5[:, :, :, half, :], in0=t5[:, :, :, half, :], in1=u4, op=ALU.add),
                                           reads=[r_t1, r_tu], pwrites=[r_qr])
                                src = qr
                                r_src = r_qr
                                if cc == 1:
                                    def tail_b(src=src, r_src=r_src, sti=sti):
                                        b2 = nextbank()

                                        def tr():
                                            for i in range(4):
                                                ins = nc.tensor.transpose(psum[:, b2, i * 128:(i + 1) * 128], src[:, i * 128:(i + 1) * 128], identf[:])
                                            return ins
                                        op(pe, tr, reads=[r_src, r_ident], writes=[bankres[b2]], pe_inorder=True)
                                        op(act, lambda: nc.scalar.activation(out=qTs[sti][:, 4:8, :], in_=psum[:, b2, :].rearrange("p (k t) -> p k t", k=4), func=AF.Copy),
                                           reads=[bankres[b2]], pwrites=[r_qTs[sti]])
                                    tails.append(tail_b)
                                else:
                                    kdi = kdc[0] % NQR
                                    kdc[0] += 1
                                    kdup = kdups[kdi]
                                    r_kdup = r_kdups[kdi]
                                    for dd in range(2):
                                        op(dve, lambda: nc.vector.tensor_copy(out=kdup[:, :, dd * 64:(dd + 1) * 64], in_=src[:, 0:128].rearrange("p (h d) -> p h d", h=2)),
                                           reads=[r_src], pwrites=[r_kdup])

                                    def tail_k(kdup=kdup, r_kdup=r_kdup, sti=sti):
                                        b2 = nextbank()

                                        def tr():
                                            for i in range(2):
                                                ins = nc.tensor.transpose(psum[:, b2, i * 128:(i + 1) * 128], kdup[:, i, :], identf[:])
                                            return ins
                                        op(pe, tr, reads=[r_kdup, r_ident], writes=[bankres[b2]], pe_inorder=True)
                                        op(act, lambda: nc.scalar.activation(out=kTs[sti][:, 4:6, :], in_=psum[:, b2, 0:256].rearrange("p (k t) -> p k t", k=2), func=AF.Copy),
                                           reads=[bankres[b2]], pwrites=[r_kTs[sti]])
                                    tails.append(tail_k)
                                while len(tails) > LAGF:
                                    tails.popleft()()
                        while tails:
                            tails.popleft()()
                        for tl in range(nt):
                            sti = sts[tl]
                            tg = tiles[tl]
                            dma(sp, qT_d[par, tg], qTs[sti][:], reads=[r_qTs[sti]], pwrites=[r_qkv[par]])
                            dma(sp, kT_d[par, tg], kTs[sti][:], reads=[r_kTs[sti]], pwrites=[r_qkv[par]])
                            dma(sp, v_d[par, tg], vs[sti][:], reads=[r_vs[sti]], pwrites=[r_qkv[par]])

                    def load_x(gi, typ, tiles):
                        xb = gi % NXB
                        nt = len(tiles)
                        if l == 0:
                            src = xin if typ == "x" else ctxin
                            r0 = tiles[0] * 128 if typ == "x" else 0
                            dma(sp, xg[xb][:, 0:nt, :], src[r0:r0 + nt * 128, :].rearrange("(t p) d -> p t d", p=128), writes=r_xg[xb][:nt])
                        else:
                            r0 = tiles[0] * 128
                            dma(sp, xg[xb][:, 0:nt, :], xres[r0:r0 + nt * 128, :].rearrange("(t p) d -> p t d", p=128),
                                reads=[r_xres], writes=r_xg[xb][:nt])

                    def stages_of(typ):
                        st = []
                        if do_ffn2 and not (typ == "c" and l - 1 >= DEPTH - 1):
                            st.append(("ffn", 0, 0, l - 1, 1))
                        if do_ffn1:
                            st.append(("ffn", 1, 1, l, 0))
                            st.append(("inproj", 2))
                        return st
                    ffn_calls = []
                    for gi_, (typ_, _t) in enumerate(groups):
                        for si__, st_ in enumerate(stages_of(typ_)):
                            if st_[0] == "ffn":
                                ffn_calls.append((gi_, si__, (st_[3], st_[4])))
                    nxt_spec = {}
                    for i_ in range(len(ffn_calls) - 1):
                        nxt_spec[(ffn_calls[i_][0], ffn_calls[i_][1])] = ffn_calls[i_ + 1][2]

                    load_x(0, groups[0][0], groups[0][1])
                    for gi, (typ, tiles) in enumerate(groups):
                        nt = len(tiles)
                        w = 0 if typ == "x" else 1
                        xb = gi % NXB
                        if w == 1:
                            load_mgate(1)
                        if gi + 1 < len(groups):
                            load_x(gi + 1, groups[gi + 1][0], groups[gi + 1][1])
                        stages = stages_of(typ)
                        for tl in range(nt):
                            norm_transpose(gi, tl, nt, w, stages[0][1])
                        for si_, st in enumerate(stages):
                            if st[0] == "ffn":
                                nxt = stages[si_ + 1] if si_ + 1 < len(stages) else None
                                ffn(gi, tiles, w, st[1], st[2], st[3], st[4], nxt=nxt_spec.get((gi, si_)))
                                if nxt is not None:
                                    for tl in range(nt):
                                        norm_transpose(gi, tl, nt, w, nxt[1])
                            else:
                                inproj(gi, tiles, w)
                        if l <= DEPTH - 1:
                            r0 = tiles[0] * 128
                            dma(sp, xres[r0:r0 + nt * 128, :].rearrange("(t p) d -> p t d", p=128), xg[xb][:, 0:nt, :],
                                reads=r_xg[xb][:nt], pwrites=[r_xres])
                        else:
                            r0 = tiles[0] * 128
                            dma(sp, yout[r0:r0 + nt * 128, :].rearrange("(t p) d -> p t d", p=128), xg[xb][:, 0:nt, :],
                                reads=r_xg[xb][:nt], pwrites=[r_yout])

            r_yout = Res("yout")

            def pass_B(l):
                par = l % 2
                last = l == DEPTH - 1
                groups = [("x", g) for g in lat_groups(NOUT[l])]
                if not last:
                    groups.append(("c", [CT0, CT0 + 1]))
                nkv = NKV[l]
                barrier()
                with ExitStack() as bs:
                    def bsb(name, shape, dt=F32):
                        return bs.enter_context(nc.sbuf_tensor("%s_B%d" % (name, l), list(shape), dt))
                    xg = bsb("bxg", [128, GT, D])
                    r_xg = [Res("bxg%d" % t) for t in range(GT)]
                    QT = bsb("QT", [128, GT, 8, 128], BF16)
                    r_QT = Res("QT")
                    KT = bsb("KT", [128, 8, 6, 128], BF16)
                    r_KT = Res("KT")
                    VW = bsb("VW", [128, 8, 12, 128], BF16)
                    r_VW = Res("VW")
                    KTc = bsb("KTc", [128, 2, 6, 128], BF16)
                    VWc = bsb("VWc", [128, 2, 12, 128], BF16)
                    r_ctxkv = Res("ctxkv")
                    EB = bsb("EB", [128, 13 * 8, 128], BF16)
                    r_EB = Res("EB")
                    ebst = [bsb("ebst%d" % i, [128, 13, 128]) for i in range(2)]
                    r_ebst = [Res("ebst0"), Res("ebst1")]
                    wo = bsb("wo", [128, 8, D], BF16)
                    r_wo = Res("wo")
                    gate = bsb("bgate", [128, 2, D])
                    r_gate = Res("bgate")
                    PTn = [bsb("PTn%d" % i, [128, 7 * 128], BF16) for i in range(3)]
                    r_PTn = [Res("PTn%d" % i) for i in range(3)]
                    NPTW = 10
                    PTw = [bsb("PTw%d" % i, [128, 512], BF16) for i in range(NPTW)]
                    r_PTw = [Res("PTw%d" % i) for i in range(NPTW)]
                    aT = bsb("aT", [128, GT, 8, 128], BF16)
                    r_aT = [Res("aT%d" % t) for t in range(GT)]
                    rden = [bsb("rden%d" % i, [128, 1024]) for i in range(2)]
                    r_rden = [Res("rden0"), Res("rden1")]
                    tmpy = [bsb("btmpy%d" % i, [128, 512]) for i in range(2)]
                    r_tmpy = [Res("btmpy0"), Res("btmpy1")]
                    esk = bsb("esk", [128, 8])
                    r_esk = Res("esk")

                    dma(sp, wo[:], woutb[l], reads=[r_wout[l]], writes=[r_wo])
                    for w in range(2):
                        dma(sp, gate[:, w, :], modtab[l, w, 5:6, :].partition_broadcast(128), reads=[r_modtab[l]], pwrites=[r_gate])
                    dma(sp, KTc[:], kT_d[par, CT0:CT0 + 2].rearrange("t p k q -> p t k q"), reads=[r_qkv[par]], pwrites=[r_ctxkv])
                    dma(sp, VWc[:], v_d[par, CT0:CT0 + 2].rearrange("t p h d -> p t h d"), reads=[r_qkv[par]], pwrites=[r_ctxkv])
                    op(act, lambda: nc.scalar.activation(out=esk[:], in_=esink[:], func=AF.Exp), reads=[r_esink], writes=[r_esk])
                    for h in range(8):
                        s = h % 2
                        dma(sp, ebst[s][:], nab[l, :, h].rearrange("s k q -> k s q"), writes=[r_ebst[s]])
                        op(act, lambda: nc.scalar.activation(out=EB[:].rearrange("p (s h) q -> p s h q", h=8)[:, :, h, :], in_=ebst[s][:], func=AF.Exp),
                           reads=[r_ebst[s]], pwrites=[r_EB])
                    EBv = EB[:].rearrange("p (s h) q -> p s h q", h=8)

                    sring = [0]
                    ptn_c = [0]
                    ptw_c = [0]
                    rd_c = [0]
                    ty_c = [0]
                    odw_c = [0]

                    ring1 = [0]

                    def alloc_pair():
                        if ring1[0] % 2 == 1:
                            ring1[0] += 1
                        b = ring1[0] % 4
                        ring1[0] += 2
                        return b

                    def alloc_one():
                        b = ring1[0] % 4
                        ring1[0] += 1
                        return b

                    import collections
                    defer = collections.deque()
                    LAGB = 1

                    def sched(s1, s2):
                        if s1 is not None:
                            s1()
                        if s2 is not None:
                            defer.append(s2)
                        while len(defer) > LAGB:
                            defer.popleft()()

                    def flush():
                        while defer:
                            defer.popleft()()

                    def attn_tile2(typ, tl, tg, wt0):
                        if typ == "x":
                            if tg == 0:
                                na_units = [("l", tg + r - wt0, 5 + r) for r in range(0, 4)]
                            elif tg == 1:
                                na_units = [("l", tg + r - wt0, 9 + (r + 1)) for r in range(-1, 3)]
                            else:
                                na_units = [("l", tg + r - wt0, r + 2) for r in range(-2, 3)]
                            na_units += [("c", 0, None), ("c", 1, None)]
                            wa_units = []
                            for r in (-1, 0, 1):
                                if 0 <= tg + r < nkv:
                                    wa_units.append(("l", tg + r - wt0, r))
                            wa_units += [("c", 0, None), ("c", 1, None)]
                        else:
                            na_units = [("c", 0, None), ("c", 1, None)]
                            wa_units = [("c", 0, None), ("c", 1, None)]

                        def kt_ap(u, pi, base):
                            src = KT if u[0] == "l" else KTc
                            return src[base:base + 64, u[1], pi, :]

                        def v_ap(u, hidx):
                            src = VW if u[0] == "l" else VWc
                            return src[:, u[1], hidx, :]
                        nu = len(na_units)
                        nl = sum(1 for u in na_units if u[0] == "l")

                        def na_item(h):
                            pi, base = h // 2, 64 * (h % 2)
                            st = {}

                            def s1():
                                b0 = alloc_pair()
                                S = psum[:, b0:b0 + 2, :].rearrange("p b n -> p (b n)")

                                def mms():
                                    for ui, u in enumerate(na_units):
                                        ins = nc.tensor.matmul(S[:, ui * 128:(ui + 1) * 128], lhsT=kt_ap(u, pi, base),
                                                               rhs=QT[base:base + 64, tl, pi, :], start=True, stop=True)
                                    return ins
                                op(pe, mms, reads=[r_KT, r_ctxkv, r_QT], writes=[bankres[b0], bankres[b0 + 1]], pe_inorder=True)
                                pn = ptn_c[0] % 3
                                ptn_c[0] += 1
                                st["pn"] = pn
                                op(act, lambda: nc.scalar.activation(out=PTn[pn][:, 0:nu * 128], in_=S[:, 0:nu * 128], func=AF.Exp),
                                   reads=[bankres[b0], bankres[b0 + 1]], writes=[r_PTn[pn]])
                                if nl > 0:
                                    e0 = na_units[0][2]
                                    pv = PTn[pn][:, 0:nl * 128].rearrange("p (u q) -> p u q", u=nl)
                                    op(dve, lambda: nc.vector.tensor_tensor(out=pv, in0=pv, in1=EBv[:, e0:e0 + nl, h, :], op=ALU.mult),
                                       reads=[r_EB], pwrites=[r_PTn[pn]])

                            def s2():
                                pn = st["pn"]
                                ob = 4 + h // 4

                                def mmo():
                                    for ui, u in enumerate(na_units):
                                        ins = nc.tensor.matmul(psum[:, ob, (h % 4) * 128:(h % 4 + 1) * 128], lhsT=v_ap(u, h),
                                                               rhs=PTn[pn][:, ui * 128:(ui + 1) * 128], start=(ui == 0), stop=(ui == nu - 1))
                                    return ins
                                if h % 4 == 0:
                                    op(pe, mmo, reads=[r_PTn[pn], r_VW, r_ctxkv], writes=[bankres[ob]], pe_inorder=True)
                                else:
                                    op(pe, mmo, reads=[r_PTn[pn], r_VW, r_ctxkv], pwrites=[bankres[ob]], pe_inorder=True)
                                if h == 7:
                                    ri = rd_c[0] % 2
                                    rd_c[0] += 1
                                    od = psum[:, 4:6, :].rearrange("p b n -> p (b n)")
                                    op(dve, lambda: nc.vector.reciprocal(out=rden[ri][:, :], in_=od[:, :]),
                                       reads=[bankres[4], bankres[5]], writes=[r_rden[ri]])
                                    odv = od.rearrange("p (i two q) -> p i two q", i=4, two=2)
                                    rdv = rden[ri][:, :].rearrange("p (i two q) -> p i two q", i=4, two=2)
                                    op(dve, lambda: nc.vector.tensor_tensor(out=aT[0:64, tl, 0:4, :], in0=odv[0:64, :, 0, :], in1=rdv[64:128, :, 0, :], op=ALU.mult),
                                       reads=[bankres[4], bankres[5], r_rden[ri]], pwrites=[r_aT[tl]])
                                    op(dve, lambda: nc.vector.tensor_tensor(out=aT[64:128, tl, 0:4, :], in0=odv[64:128, :, 1, :], in1=rdv[0:64, :, 1, :], op=ALU.mult),
                                       reads=[bankres[4], bankres[5], r_rden[ri]], pwrites=[r_aT[tl]])
                            return s1, s2

                        for h in range(8 if "na" in bparts else 0):
                            sched(*na_item(h))

                        nuw = len(wa_units)

                        def wa_item(kv, u0, pws):
                            us = wa_units[u0:u0 + 2]
                            lastpair = u0 + 2 >= nuw

                            def s1():
                                b0 = alloc_pair()

                                def mms():
                                    for j, u in enumerate(us):
                                        nc.tensor.matmul(psum[:, b0, j * 256:(j + 1) * 256], lhsT=kt_ap(u, 4 + kv, 0),
                                                         rhs=QT[0:64, tl, 4 + 2 * kv:6 + 2 * kv, :].rearrange("p a q -> p (a q)"), start=True, stop=True)
                                        ins = nc.tensor.matmul(psum[:, b0 + 1, j * 256:(j + 1) * 256], lhsT=kt_ap(u, 4 + kv, 64),
                                                               rhs=QT[64:128, tl, 4 + 2 * kv:6 + 2 * kv, :].rearrange("p a q -> p (a q)"), start=True, stop=True)
                                    return ins
                                op(pe, mms, reads=[r_KT, r_ctxkv, r_QT], writes=[bankres[b0], bankres[b0 + 1]], pe_inorder=True)
                                for j, u in enumerate(us):
                                    pw = ptw_c[0] % NPTW
                                    ptw_c[0] += 1
                                    pws.append(pw)
                                    op(act, lambda: nc.scalar.activation(out=PTw[pw][:].rearrange("p (b n) -> p b n", b=2),
                                                                         in_=psum[:, b0:b0 + 2, j * 256:(j + 1) * 256], func=AF.Exp),
                                       reads=[bankres[b0], bankres[b0 + 1]], writes=[r_PTw[pw]])
                                    if u[0] == "l" and u[2] != 0:
                                        mi = 0 if u[2] < 0 else 1
                                        pv = PTw[pw][:].rearrange("p (s q) -> p s q", s=4)
                                        op(dve, lambda: nc.vector.tensor_tensor(out=pv, in0=pv, in1=wam_sb[:, mi, :].unsqueeze(1).to_broadcast([128, 4, 128]), op=ALU.mult),
                                           reads=[r_wam], pwrites=[r_PTw[pw]])

                            def s2():
                                ob = 6 + (odw_c[0] % 2)
                                odw_c[0] += 1

                                def mmo():
                                    for ui, u in enumerate(wa_units):
                                        nc.tensor.matmul(psum[:, ob, 0:256], lhsT=v_ap(u, 8 + kv), rhs=PTw[pws[ui]][:, 0:256],
                                                         start=(ui == 0), stop=(ui == nuw - 1), skip_group_check=True)
                                        ins = nc.tensor.matmul(psum[:, ob, 256:512], lhsT=v_ap(u, 10 + kv), rhs=PTw[pws[ui]][:, 256:512],
                                                               start=False, stop=(ui == nuw - 1), skip_group_check=True)
                                    return ins
                                op(pe, mmo, reads=[r_PTw[p] for p in pws] + [r_VW, r_ctxkv], writes=[bankres[ob]], pe_inorder=True)
                                ri = rd_c[0] % 2
                                rd_c[0] += 1
                                op(dve, lambda: nc.vector.tensor_tensor(out=rden[ri][64:128, 0:256].rearrange("p (s q) -> p s q", s=2),
                                                                        in0=psum[64:128, ob, 0:256].rearrange("p (s q) -> p s q", s=2),
                                                                        in1=esk[64:128, 4 * kv:4 * kv + 2].unsqueeze(2).to_broadcast([64, 2, 128]), op=ALU.add),
                                   reads=[bankres[ob], r_esk], pwrites=[r_rden[ri]])
                                op(dve, lambda: nc.vector.tensor_tensor(out=rden[ri][0:64, 256:512].rearrange("p (s q) -> p s q", s=2),
                                                                        in0=psum[0:64, ob, 256:512].rearrange("p (s q) -> p s q", s=2),
                                                                        in1=esk[0:64, 4 * kv + 2:4 * kv + 4].unsqueeze(2).to_broadcast([64, 2, 128]), op=ALU.add),
                                   reads=[bankres[ob], r_esk], pwrites=[r_rden[ri]])
                                op(dve, lambda: nc.vector.reciprocal(out=rden[ri][64:128, 0:256], in_=rden[ri][64:128, 0:256]),
                                   reads=[r_rden[ri]], pwrites=[r_rden[ri]])
                                op(dve, lambda: nc.vector.reciprocal(out=rden[ri][0:64, 256:512], in_=rden[ri][0:64, 256:512]),
                                   reads=[r_rden[ri]], pwrites=[r_rden[ri]])
                                op(dve, lambda: nc.vector.tensor_tensor(out=aT[0:64, tl, 4 + 2 * kv:6 + 2 * kv, :].rearrange("p s q -> p (s q)"),
                                                                        in0=psum[0:64, ob, 0:256], in1=rden[ri][64:128, 0:256], op=ALU.mult),
                                   reads=[bankres[ob], r_rden[ri]], pwrites=[r_aT[tl]])
                                op(dve, lambda: nc.vector.tensor_tensor(out=aT[64:128, tl, 4 + 2 * kv:6 + 2 * kv, :].rearrange("p s q -> p (s q)"),
                                                                        in0=psum[64:128, ob, 256:512], in1=rden[ri][0:64, 256:512], op=ALU.mult),
                                   reads=[bankres[ob], r_rden[ri]], pwrites=[r_aT[tl]])
                            return s1, (s2 if lastpair else None)

                        for kv in range(2 if "wa" in bparts else 0):
                            pws = []
                            for u0 in range(0, nuw, 2):
                                sched(*wa_item(kv, u0, pws))

                    def outproj(typ, tl):
                        w = 0 if typ == "x" else 1
                        for nh in range(2):
                            bk = alloc_one()

                            def mmp():
                                for h in range(8):
                                    ins = nc.tensor.matmul(psum[:, bk, :], lhsT=aT[:, tl, h, :], rhs=wo[:, h, nh * 512:(nh + 1) * 512],
                                                           start=(h == 0), stop=(h == 7))
                                return ins
                            op(pe, mmp, reads=[r_aT[tl], r_wo], writes=[bankres[bk]], pe_inorder=True)
                            ti = ty_c[0] % 2
                            ty_c[0] += 1
                            op(dve, lambda: nc.vector.tensor_tensor(out=tmpy[ti][:], in0=psum[:, bk, :], in1=gate[:, w, nh * 512:(nh + 1) * 512], op=ALU.mult),
                               reads=[bankres[bk], r_gate], writes=[r_tmpy[ti]])
                            xsl = xg[:, tl, nh * 512:(nh + 1) * 512]
                            op(dve, lambda: nc.vector.tensor_tensor(out=xsl, in0=xsl, in1=tmpy[ti][:], op=ALU.add),
                               reads=[r_tmpy[ti]], pwrites=[r_xg[tl]])

                    for gi, (typ, tiles) in enumerate(groups):
                        nt = len(tiles)
                        r0 = tiles[0] * 128
                        dma(sp, xg[:, 0:nt, :], xres[r0:r0 + nt * 128, :].rearrange("(t p) d -> p t d", p=128),
                            reads=[r_xres], writes=r_xg[:nt])
                        dma(sp, QT[:, 0:nt], qT_d[par, tiles[0]:tiles[0] + nt].rearrange("t p k q -> p t k q"),
                            reads=[r_qkv[par]], writes=[r_QT])
                        wt0 = 0
                        if typ == "x":
                            wt0 = max(0, tiles[0] - 2)
                            wt1 = min(nkv, tiles[-1] + 3)
                            if tiles[0] == 0:
                                wt1 = min(nkv, max(wt1, 4))
                            nw = wt1 - wt0
                            assert nw <= 8
                            dma(sp, KT[:, 0:nw], kT_d[par, wt0:wt1].rearrange("t p k q -> p t k q"), reads=[r_qkv[par]], writes=[r_KT])
                            dma(sp, VW[:, 0:nw], v_d[par, wt0:wt1].rearrange("t p h d -> p t h d"), reads=[r_qkv[par]], writes=[r_VW])
                        for tl in range(nt):
                            attn_tile2(typ, tl, tiles[tl], wt0)
                            if "proj" in bparts:
                                sched(None, (lambda typ_=typ, tl_=tl: outproj(typ_, tl_)))
                        flush()
                        dma(sp, xres_w[r0:r0 + nt * 128, :].rearrange("(t p) d -> p t d", p=128), xg[:, 0:nt, :],
                            reads=r_xg[:nt], pwrites=[r_xres])

            seq = []
            for l in range(DEPTH):
                seq.append(("F", l))
                seq.append(("B", l))
            seq.append(("F", DEPTH))
            if btest is not None:
                load_layer_consts(btest)
                seq = [("B", btest)]
            for kind, l in seq:
                if kind == "F":
                    pass_F(l)
                else:
                    pass_B(l)
                if stop_after == (kind, l):
                    break

            r_dump = Res("dump")
            for name, (dst, src) in dump_out.items():
                rr = {"xres": [r_xres], "qT_d": r_qkv, "kT_d": r_qkv, "v_d": r_qkv, "modtab": r_modtab}[name]
                if len(src.shape) == 2:
                    nrow = src.shape[0]
                    step = 1024
                    for a in range(0, nrow, step):
                        b = min(nrow, a + step)
                        dma(sp, dst[a:b], src[a:b], reads=rr, pwrites=[r_dump])
                else:
                    for a in range(src.shape[0]):
                        if len(src.shape) >= 5:
                            for b in range(src.shape[1]):
                                dma(sp, dst[a, b], src[a, b], reads=rr, pwrites=[r_dump])
                        else:
                            dma(sp, dst[a], src[a], reads=rr, pwrites=[r_dump])
            final = {}
            _merge(final, r_yout.w)
            _merge(final, r_dump.w)
            _merge(final, r_xres.w)
            for p, v in final.items():
                nc.sync.wait_ge(p.sem, v)
            for s in pool.dsem:
                if s.val:
                    nc.gpsimd.wait_ge(s.sem, s.val)
            stats = dict(pe=pe.nins, act=act.nins, dve=dve.nins, waits=pe.nwait + act.nwait + dve.nwait + sp.nwait + pool.nwait,
                         dmas=sp.di + pool.di)
            print("program stats:", stats, "sbuf_remaining", getattr(nc, "sbuf_bytes_remaining", None))

        block.sync(program)
    return nc


def _rope_table(parity):
    rot = 32
    inv_freq = (10000.0 ** (-np.arange(0, rot, 2, dtype=np.float32) / np.float32(rot))).astype(np.float32)
    tl = np.arange(NLAT * 128)
    tg = tl if parity == 0 else (SEQ - 1 - tl)
    row = (tg // 64).astype(np.float32)
    col = (tg % 64).astype(np.float32)
    ang = np.stack([row[:, None] * inv_freq, col[:, None] * inv_freq], axis=1).astype(np.float32)
    c, s = np.cos(ang).astype(np.float32), np.sin(ang).astype(np.float32)
    tab = np.zeros((NLAT * 128, 128), np.float32)
    c64 = np.stack([c, c], axis=2)
    tab[:, 0:64] = c64.reshape(-1, 64)
    tab[:, 64:96] = (-s).reshape(-1, 32)
    tab[:, 96:128] = s.reshape(-1, 32)
    return tab


def _na_bias_tables(rpb, parity):
    L = rpb.shape[0]
    out = np.full((L, 13, 8, 128, 128), MASKV, np.float32)
    pats = [(10, r, r + 2) for r in range(-2, 3)] + [(0, r, 5 + r) for r in range(0, 4)] + [(1, r, 10 + r) for r in range(-1, 3)]
    idx = np.arange(128)
    for (j, rel, slot) in pats:
        ql = idx[None, :]
        kl = idx[:, None]
        qr_l = 2 * j + ql // 64
        qc_l = ql % 64
        kr_l = 2 * (j + rel) + kl // 64
        kc_l = kl % 64
        if parity == 0:
            qr, qc, kr, kc = qr_l, qc_l, kr_l, kc_l
        else:
            qr, qc, kr, kc = 127 - qr_l, 63 - qc_l, 127 - kr_l, 63 - kc_l
        r0 = np.clip(qr - 4, 0, 120)
        c0 = np.clip(qc - 8, 0, 48)
        valid = (kr >= r0) & (kr < r0 + 8) & (kc >= c0) & (kc < c0 + 16)
        valid = np.broadcast_to(valid, (128, 128))
        dr = np.clip(kr - qr + 7, 0, 14)
        dc = np.clip(kc - qc + 15, 0, 30)
        dr = np.broadcast_to(dr, (128, 128))
        dc = np.broadcast_to(dc, (128, 128))
        g = rpb[:, :, dr, dc]
        out[:, slot] = np.where(valid[None, None], g, np.float32(MASKV))
    return out


_NC_CACHE = {}


def make_in_maps(x, c, ctx, c_ctx, w_ada, b_ada, norm_g, w_ffn_up, w_ffn_down, w_in, w_out, qk_norm_g, na_rpb, wa_sink):
    f = lambda a: np.ascontiguousarray(np.asarray(a, dtype=np.float32))
    x, c, ctx, c_ctx = f(x), f(c), f(ctx), f(c_ctx)
    shared = dict(w_ada=f(w_ada), b_ada=f(b_ada), norm_g=f(norm_g), w_up=f(w_ffn_up), w_dn=f(w_ffn_down), w_in=f(w_in),
                  w_out=f(w_out), qkg=f(qk_norm_g), identf_in=np.eye(128, dtype=np.float32))
    sink = f(wa_sink)
    shared["sinkp"] = np.ascontiguousarray(sink[:, [0, 2, 1, 3, 4, 6, 5, 7]])
    idx = np.arange(128)
    wam = np.stack([(idx[:, None] >= idx[None, :]), (idx[:, None] <= idx[None, :])]).astype(np.float32).astype(ml_dtypes.bfloat16)
    shared["wam"] = wam
    rpb = f(na_rpb)
    tabs = [(_rope_table(p), _na_bias_tables(rpb, p)) for p in range(2)]
    in_maps = []
    for core in range(8):
        b, p = core // 2, core % 2
        xs = x[b] if p == 0 else x[b, ::-1]
        m = dict(shared)
        m["xin"] = np.ascontiguousarray(xs[:NLAT * 128])
        m["ctxin"] = np.ascontiguousarray(ctx[b])
        m["cvec"] = np.ascontiguousarray(np.stack([c[b], c_ctx]))
        m["rope"] = tabs[p][0]
        m["nab"] = tabs[p][1]
        in_maps.append(m)
    return in_maps


def kernel(x, c, ctx, c_ctx, w_ada, b_ada, norm_g, w_ffn_up, w_ffn_down, w_in, w_out, qk_norm_g, na_rpb, wa_sink):
    in_maps = make_in_maps(x, c, ctx, c_ctx, w_ada, b_ada, norm_g, w_ffn_up, w_ffn_down, w_in, w_out, qk_norm_g, na_rpb, wa_sink)
    if "nc" not in _NC_CACHE:
        _NC_CACHE["nc"] = build_program()
    nc = _NC_CACHE["nc"]
    res = run_bass_kernel_spmd(nc, in_maps, core_ids=list(range(8)))
    out = np.empty((4, SEQ, D), np.float32)
    for core in range(8):
        b, p = core // 2, core % 2
        y = np.asarray(res.results[core]["yout"], dtype=np.float32)
        if p == 0:
            out[b, 0:4096] = y
        else:
            out[b, 4096:] = y[::-1]
    return out
```

```python
import numpy as np
import ml_dtypes
from contextlib import ExitStack
import concourse.bass as bass
import concourse.mybir as mybir
from concourse.bass_utils import run_bass_kernel_spmd

F32 = mybir.dt.float32
BF16 = mybir.dt.bfloat16
AF = mybir.ActivationFunctionType
ALU = mybir.AluOpType
AX = mybir.AxisListType

D = 1024
DFF = 2816
NFC = 22
INW = 2304
DEPTH = 4
SEQ = 8192
NLAT = 40
NT_ALL = 42
CT0 = 40
NOUT = [38, 36, 34, 32]
NKV = [40, 38, 36, 34]
EPS = 1e-6
MASKV = -30000.0
GT = 4
SAME_ENG_SYNC = True


class Prod:
    __slots__ = ("sem", "val")

    def __init__(self, sem):
        self.sem = sem
        self.val = 0


class Res:
    __slots__ = ("name", "w", "r")

    def __init__(self, name):
        self.name = name
        self.w = {}
        self.r = {}


class Eng:
    def __init__(self, nc, name, h, ndma=0):
        self.nc = nc
        self.name = name
        self.h = h
        self.prod = Prod(nc.alloc_semaphore("e_" + name))
        self.seen = {}
        self.dsem = [Prod(nc.alloc_semaphore("d_%s_%d" % (name, i))) for i in range(ndma)]
        self.di = 0
        self.nwait = 0
        self.nins = 0

    def need(self, deps):
        for p, v in deps.items():
            if p is self.prod and not SAME_ENG_SYNC:
                continue
            if self.seen.get(p, 0) >= v:
                continue
            self.h.wait_ge(p.sem, v)
            self.nwait += 1
            self.seen[p] = v


def _merge(d, s):
    for p, v in s.items():
        if d.get(p, 0) < v:
            d[p] = v


def _deps(reads, writes, pwrites):
    deps = {}
    for r in reads:
        _merge(deps, r.w)
    for w in writes:
        _merge(deps, w.w)
        _merge(deps, w.r)
    for w in pwrites:
        _merge(deps, w.w)
        _merge(deps, w.r)
    return deps


def _commit(p, v, reads, writes, pwrites):
    for r in reads:
        if r.r.get(p, 0) < v:
            r.r[p] = v
    for w in writes:
        w.w = {p: v}
        w.r = {}
    for w in pwrites:
        if w.w.get(p, 0) < v:
            w.w[p] = v


def op(e, fn, reads=(), writes=(), pwrites=(), pe_inorder=False):
    deps = _deps(reads, writes, pwrites)
    if pe_inorder:
        deps.pop(e.prod, None)
    e.need(deps)
    ins = fn()
    e.prod.val += 1
    ins.then_inc(e.prod.sem, 1)
    e.nins += 1
    _commit(e.prod, e.prod.val, reads, writes, pwrites)


def dma(q, out, in_, reads=(), writes=(), pwrites=(), **kw):
    deps = _deps(reads, writes, pwrites)
    q.need(deps)
    s = q.dsem[q.di % len(q.dsem)]
    q.di += 1
    if s.val > 0 and q.seen.get(s, 0) < s.val:
        q.h.wait_ge(s.sem, s.val)
        q.seen[s] = s.val
    q.h.dma_start(out=out, in_=in_, **kw).then_inc(s.sem, 16)
    s.val += 16
    _commit(s, s.val, reads, writes, pwrites)
    return s


def lat_groups(n):
    gs = []
    t = 0
    while t < n:
        k = min(GT, n - t)
        gs.append(list(range(t, t + k)))
        t += k
    return gs


def build_program(stop_after=None, dumps=(), btest=None, bparts=('na', 'wa', 'proj')):
    nc = bass.Bass("TRN2", target_bir_lowering=False, dynamic_dma_scratch_size=8192)

    def din(name, shape, dt=F32, big=False):
        if big and btest is not None:
            return None
        return nc.dram_tensor(name, list(shape), dt, kind="ExternalInput").ap()

    def dscr(name, shape, dt, ext=False):
        if ext and btest is not None:
            return nc.dram_tensor("in_" + name, list(shape), dt, kind="ExternalInput").ap()
        return nc.dram_tensor(name, list(shape), dt).ap()

    xin = din("xin", [NLAT * 128, D], big=True)
    ctxin = din("ctxin", [256, D], big=True)
    cvec = din("cvec", [2, D], big=True)
    w_ada = din("w_ada", [DEPTH, D, 9 * D], big=True)
    b_ada = din("b_ada", [DEPTH, 9 * D], big=True)
    norm_g = din("norm_g", [DEPTH, 3, D], big=True)
    w_up = din("w_up", [DEPTH, 2, D, 2 * DFF], big=True)
    w_dn = din("w_dn", [DEPTH, 2, DFF, D], big=True)
    w_in = din("w_in", [DEPTH, D, INW], big=True)
    w_out = din("w_out", [DEPTH, D, D], big=True)
    qkg = din("qkg", [DEPTH, 4, 64])
    sinkp = din("sinkp", [DEPTH, 8])
    nab = din("nab", [DEPTH, 13, 8, 128, 128])
    rope = din("rope", [NLAT * 128, 128], big=True)
    wam = din("wam", [2, 128, 128], BF16)
    identf_d = din("identf_in", [128, 128])
    yout = nc.dram_tensor("yout", [32 * 128, D], F32, kind="ExternalOutput").ap()

    wupb = dscr("wupb", [DEPTH, 2, NFC, 128, 8, 256], BF16)
    wdnb = dscr("wdnb", [DEPTH, 2, NFC, 128, D], BF16)
    winb = dscr("winb", [DEPTH, 128, 8, INW], BF16)
    woutb = dscr("woutb", [DEPTH, 128, 8, D], BF16, ext=True)
    modtab = dscr("modtab", [DEPTH, 2, 9, D], F32, ext=True)
    xres = dscr("xres", [NT_ALL * 128, D], F32, ext=True)
    qT_d = dscr("qT_d", [2, NT_ALL, 128, 8, 128], BF16, ext=True)
    kT_d = dscr("kT_d", [2, NT_ALL, 128, 6, 128], BF16, ext=True)
    v_d = dscr("v_d", [2, NT_ALL, 128, 12, 128], BF16, ext=True)

    xres_w = xres
    if btest is not None:
        xres_w = nc.dram_tensor("xres_out", [NT_ALL * 128, D], F32, kind="ExternalOutput").ap()
    dump_out = {}
    for name in dumps:
        src = {"xres": xres, "qT_d": qT_d, "kT_d": kT_d, "v_d": v_d, "modtab": modtab}[name]
        dump_out[name] = (nc.dram_tensor("dump_" + name, list(src.shape), src.dtype, kind="ExternalOutput").ap(), src)

    es = ExitStack()
    with es:
        def sb(name, shape, dt=F32):
            return es.enter_context(nc.sbuf_tensor(name, list(shape), dt))

        pe = Eng(nc, "pe", nc.tensor)
        act = Eng(nc, "act", nc.scalar)
        dve = Eng(nc, "dve", nc.vector)
        pool = Eng(nc, "pool", nc.gpsimd, ndma=6)
        sp = Eng(nc, "sp", nc.sync, ndma=32)

        psum = es.enter_context(nc.psum_tensor("psum", [128, 8, 512], F32))
        bankres = [Res("bank%d" % i) for i in range(8)]

        r_wup = [[Res("wup%d%d" % (l, f)) for f in range(2)] for l in range(DEPTH)]
        r_wdn = [[Res("wdn%d%d" % (l, f)) for f in range(2)] for l in range(DEPTH)]
        r_win = [Res("win%d" % l) for l in range(DEPTH)]
        r_wout = [Res("wout%d" % l) for l in range(DEPTH)]
        r_modtab = [Res("modtab%d" % l) for l in range(DEPTH)]
        r_xres = Res("xres")
        r_qkv = [Res("qkv0"), Res("qkv1")]

        identf = sb("identf", [128, 128])
        r_ident = Res("ident")
        wam_sb = sb("wam_sb", [128, 2, 128], BF16)
        r_wam = Res("wam")
        esink = sb("esink", [128, 8])
        r_esink = Res("esink")
        gcols = sb("gcols", [128, 2])
        gbc = sb("gbc", [128, 2, 64])
        r_g = Res("g")

        block = es.enter_context(nc.Block())

        def program(_eng):
            def cast_up(l, f):
                for c in range(NFC):
                    for half in range(2):
                        col0 = half * DFF + c * 128
                        dma(pool, wupb[l, f, c][:, :, half * 128:(half + 1) * 128],
                            w_up[l, f][:, col0:col0 + 128].rearrange("(kc p) n -> p kc n", p=128),
                            pwrites=[r_wup[l][f]])

            def cast_dn(l, f):
                for c in range(NFC):
                    dma(pool, wdnb[l, f, c], w_dn[l, f][c * 128:(c + 1) * 128, :], pwrites=[r_wdn[l][f]])

            def cast_in(l):
                for kc in range(8):
                    for hh in range(2):
                        dma(pool, winb[l][:, kc, hh * 1152:(hh + 1) * 1152],
                            w_in[l][kc * 128:(kc + 1) * 128, hh * 1152:(hh + 1) * 1152], pwrites=[r_win[l]])

            def cast_out(l):
                for kc in range(8):
                    dma(pool, woutb[l][:, kc, :], w_out[l][kc * 128:(kc + 1) * 128, :], pwrites=[r_wout[l]])

            for l in range(DEPTH if btest is None else 0):
                cast_up(l, 0)
                cast_dn(l, 0)
                cast_in(l)
                cast_out(l)
                cast_up(l, 1)
                cast_dn(l, 1)

            dma(sp, identf[:], identf_d, writes=[r_ident])
            dma(sp, wam_sb[:], wam.rearrange("r k q -> k r q"), writes=[r_wam])

            def emit_modtabs():
                with ExitStack() as ms:
                    def msb(name, shape, dt=F32):
                        return ms.enter_context(nc.sbuf_tensor(name, list(shape), dt))
                    ccol = msb("ccol", [128, 8, 2])
                    scol = msb("scol", [128, 8, 2], BF16)
                    wst = [msb("wst%d" % i, [128, 8, 512]) for i in range(2)]
                    wbf = [msb("wbf%d" % i, [128, 8, 512], BF16) for i in range(2)]
                    bch = [msb("bch%d" % i, [2, 512]) for i in range(2)]
                    gch = [msb("gch%d" % i, [2, 512]) for i in range(2)]
                    och = [msb("och%d" % i, [2, 512]) for i in range(2)]
                    r_ccol, r_scol = Res("ccol"), Res("scol")
                    r_wst = [Res("wst0"), Res("wst1")]
                    r_wbf = [Res("wbf0"), Res("wbf1")]
                    r_bch = [Res("bch0"), Res("bch1")]
                    r_gch = [Res("gch0"), Res("gch1")]
                    r_och = [Res("och0"), Res("och1")]
                    for w_ in range(2):
                        dma(sp, ccol[:, :, w_], cvec[w_, :].rearrange("(kc p) -> p kc", p=128), pwrites=[r_ccol],
                            allow_slow_non_contiguous=True)
                    op(act, lambda: nc.scalar.activation(out=scol[:], in_=ccol[:], func=AF.Silu),
                       reads=[r_ccol], writes=[r_scol])
                    it = 0
                    for l in range(DEPTH):
                        for cc in range(18):
                            s = it % 2
                            m, half = cc // 2, cc % 2
                            j, kind = m // 3, m % 3
                            dma(sp, wst[s][:], w_ada[l][:, cc * 512:(cc + 1) * 512].rearrange("(kc p) n -> p kc n", p=128),
                                writes=[r_wst[s]])
                            dma(sp, bch[s][:], b_ada[l:l + 1, cc * 512:(cc + 1) * 512].partition_broadcast(2), writes=[r_bch[s]])
                            if kind == 1:
                                dma(sp, gch[s][:], norm_g[l, j:j + 1, half * 512:(half + 1) * 512].partition_broadcast(2),
                                    writes=[r_gch[s]])
                            op(dve, lambda: nc.vector.tensor_copy(out=wbf[s][:], in_=wst[s][:]),
                               reads=[r_wst[s]], writes=[r_wbf[s]])
                            bk = it % 8

                            def mm():
                                for kc in range(8):
                                    ins = nc.tensor.matmul(psum[0:2, bk, :], lhsT=scol[:, kc, :], rhs=wbf[s][:, kc, :],
                                                           start=(kc == 0), stop=(kc == 7))
                                return ins
                            op(pe, mm, reads=[r_scol, r_wbf[s]], writes=[bankres[bk]], pe_inorder=True)
                            if kind == 0:
                                op(dve, lambda: nc.vector.tensor_tensor(out=och[s][:], in0=psum[0:2, bk, :], in1=bch[s][:], op=ALU.add),
                                   reads=[bankres[bk], r_bch[s]], writes=[r_och[s]])
                            elif kind == 1:
                                op(dve, lambda: nc.vector.scalar_tensor_tensor(out=och[s][:], in0=psum[0:2, bk, :], scalar=1.0,
                                                                               in1=bch[s][:], op0=ALU.add, op1=ALU.add),
                                   reads=[bankres[bk], r_bch[s]], writes=[r_och[s]])
                                op(dve, lambda: nc.vector.tensor_tensor(out=och[s][:], in0=och[s][:], in1=gch[s][:], op=ALU.mult),
                                   reads=[r_gch[s], r_och[s]], writes=[r_och[s]])
                            else:
                                gsc = 1.0 if j == 1 else 0.5
                                op(dve, lambda: nc.vector.tensor_tensor(out=och[s][:], in0=psum[0:2, bk, :], in1=bch[s][:], op=ALU.add),
                                   reads=[bankres[bk], r_bch[s]], writes=[r_och[s]])
                                if gsc != 1.0:
                                    op(dve, lambda: nc.vector.tensor_scalar(out=och[s][:], in0=och[s][:], scalar1=gsc, scalar2=None, op0=ALU.mult),
                                       reads=[r_och[s]], writes=[r_och[s]])
                            dma(sp, modtab[l, :, m, half * 512:(half + 1) * 512], och[s][:], reads=[r_och[s]],
                                pwrites=[r_modtab[l]])
                            it += 1

            if btest is None:
                emit_modtabs()

            def load_layer_consts(l):
                for hh in range(2):
                    dma(sp, gcols[hh * 64:(hh + 1) * 64, 0:1], qkg[l, 0:1, :].rearrange("o d -> d o"), pwrites=[r_g],
                        allow_slow_non_contiguous=True)
                    dma(sp, gcols[hh * 64:(hh + 1) * 64, 1:2], qkg[l, 1:2, :].rearrange("o d -> d o"), pwrites=[r_g],
                        allow_slow_non_contiguous=True)
                dma(sp, gbc[:, 0, :], qkg[l, 2:3, :].partition_broadcast(128), pwrites=[r_g])
                dma(sp, gbc[:, 1, :], qkg[l, 3:4, :].partition_broadcast(128), pwrites=[r_g])
                op(dve, lambda: nc.vector.tensor_scalar(out=gcols[:, 0:1], in0=gcols[:, 0:1], scalar1=0.125, scalar2=None,
                                                        op0=ALU.mult), reads=[r_g], pwrites=[r_g])
                op(dve, lambda: nc.vector.tensor_scalar(out=gbc[:, 0, :], in0=gbc[:, 0, :], scalar1=0.125, scalar2=None,
                                                        op0=ALU.mult), reads=[r_g], pwrites=[r_g])
                dma(sp, esink[:], sinkp[l:l + 1, :].partition_broadcast(128), writes=[r_esink])

            def barrier():
                allp = {}
                for e in (pe, act, dve):
                    if e.prod.val:
                        allp[e.prod] = e.prod.val
                for sd in sp.dsem:
                    if sd.val:
                        allp[sd] = sd.val
                for e in (pe, act, dve, sp):
                    e.need(dict(allp))

            def pass_F(l):
                do_ffn2 = l >= 1
                do_ffn1 = l <= DEPTH - 1
                n_lat = NKV[l] if l <= DEPTH - 1 else NOUT[DEPTH - 1]
                groups = [("x", g) for g in lat_groups(n_lat)]
                if l <= DEPTH - 1:
                    groups.append(("c", [CT0, CT0 + 1]))
                par = l % 2
                barrier()
                with ExitStack() as fs:
                    def fsb(name, shape, dt=F32):
                        return fs.enter_context(nc.sbuf_tensor("%s_F%d" % (name, l), list(shape), dt))
                    NXB = 2
                    xg = [fsb("xg%d" % i, [128, GT, D]) for i in range(NXB)]
                    r_xg = [[Res("xg%d_%d" % (i, t)) for t in range(GT)] for i in range(NXB)]
                    hnT = fsb("hnT", [128, 8, GT * 128], BF16)
                    r_hnT = [Res("hnT%d" % t) for t in range(GT)]
                    hidT = fsb("hidT", [128, NFC, GT * 128], BF16)
                    r_hid = Res("hidT")
                    NUP = 3
                    wup = [fsb("wup%d" % i, [128, 8, 256], BF16) for i in range(NUP)]
                    r_wups = [Res("wups%d" % i) for i in range(NUP)]
                    wdn = fsb("wdn", [128, NFC, D], BF16)
                    r_wdns = Res("wdns")
                    NWI = 2
                    win = [fsb("win%d" % i, [128, 8, 512], BF16) for i in range(NWI)]
                    r_wins = [Res("wins%d" % i) for i in range(NWI)]
                    xn = [fsb("xn%d" % i, [128, D]) for i in range(2)]
                    r_xn = [Res("xn0"), Res("xn1")]
                    mt = [fsb("mt%d" % i, [128, 512]) for i in range(2)]
                    r_mt = [Res("mt0"), Res("mt1")]
                    mtc = [0]
                    stat = fsb("stat", [128, 64])
                    r_stat = Res("stat")
                    sg = [fsb("sg%d" % i, [128, 512], BF16) for i in range(2)]
                    r_sg = [Res("sg0"), Res("sg1")]
                    tmpy = [fsb("tmpy%d" % i, [128, 512]) for i in range(2)]
                    r_tmpy = [Res("tmpy0"), Res("tmpy1")]
                    mcols = fsb("mcols", [128, 2, 6, 8])
                    mgate = fsb("mgate", [128, 2, D])
                    r_mod = Res("mod")
                    r_mgate = Res("mgate")
                    sq = [fsb("sq%d" % i, [128, 512]) for i in range(1)]
                    r_sq = [Res("sq0")]
                    NQN = 3
                    qn = [fsb("qn%d" % i, [128, 512]) for i in range(NQN)]
                    r_qn = [Res("qn%d" % i) for i in range(NQN)]
                    qg = fsb("qg", [128, 512])
                    r_qg = Res("qg")
                    t1 = fsb("t1", [128, 512])
                    r_t1 = Res("t1")
                    tu = fsb("tu", [128, 256])
                    r_tu = Res("tu")
                    NQR = 3
                    qrs = [fsb("qr%d" % i, [128, 512]) for i in range(NQR)]
                    r_qrs = [Res("qr%d" % i) for i in range(NQR)]
                    qrc = [0]
                    kdups = [fsb("kdup%d" % i, [128, 2, 128]) for i in range(NQR)]
                    r_kdups = [Res("kdup%d" % i) for i in range(NQR)]
                    kdc = [0]
                    ssq = fsb("ssq", [128, 32])
                    r_ssq = Res("ssq")
                    rsq = fsb("rsq", [128, 32])
                    r_rsq = Res("rsq")
                    NST = GT
                    qTs = [fsb("qTs%d" % i, [128, 8, 128], BF16) for i in range(NST)]
                    kTs = [fsb("kTs%d" % i, [128, 6, 128], BF16) for i in range(NST)]
                    vs = [fsb("vs%d" % i, [128, 12, 128], BF16) for i in range(NST)]
                    r_qTs = [Res("qTs%d" % i) for i in range(NST)]
                    r_kTs = [Res("kTs%d" % i) for i in range(NST)]
                    r_vs = [Res("vs%d" % i) for i in range(NST)]
                    ropet = fsb("ropet", [128, GT, 128])
                    r_rope = Res("rope")

                    for i in range(NST):
                        op(dve, lambda: nc.vector.memset(vs[i][:], 1.0), writes=[r_vs[i]])

                    for w in range(2):
                        specs = []
                        if do_ffn2:
                            specs.append((0, l - 1, 2))
                        if do_ffn1:
                            specs.append((1, l, 0))
                            specs.append((2, l, 1))
                        for slot, ll, j in specs:
                            dma(sp, mcols[:, w, 2 * slot, :], modtab[ll, w, 3 * j + 1, :].rearrange("(kc p) -> p kc", p=128),
                                reads=[r_modtab[ll]], pwrites=[r_mod], allow_slow_non_contiguous=True)
                            dma(sp, mcols[:, w, 2 * slot + 1, :], modtab[ll, w, 3 * j, :].rearrange("(kc p) -> p kc", p=128),
                                reads=[r_modtab[ll]], pwrites=[r_mod], allow_slow_non_contiguous=True)

                    def load_mgate(w):
                        if do_ffn2:
                            dma(sp, mgate[:, 0, :], modtab[l - 1, w, 8:9, :].partition_broadcast(128),
                                reads=[r_modtab[l - 1]], pwrites=[r_mgate])
                        if do_ffn1:
                            dma(sp, mgate[:, 1, :], modtab[l, w, 2:3, :].partition_broadcast(128),
                                reads=[r_modtab[l]], pwrites=[r_mgate])
                    load_mgate(0)
                    if do_ffn1:
                        load_layer_consts(l)

                    ringpos = [0]

                    def nextbank():
                        b = ringpos[0] % 8
                        ringpos[0] += 1
                        return b

                    upc = [0]
                    winc = [0]
                    stc = [0]
                    xnc = [0]
                    sgc = [0]
                    tyc = [0]
                    sqc = [0]
                    qnc = [0]

                    def norm_transpose(gi, tl, nt, w, slot):
                        xb = gi % NXB
                        xt = xg[xb][:, tl, :]
                        rx = r_xg[xb][tl]
                        c0 = tl * 2
                        op(act, lambda: nc.scalar.activation(out=sq[0][:].bitcast(BF16), in_=xt, func=AF.Square, accum_out=stat[:, c0:c0 + 1]),
                           reads=[rx], writes=[r_sq[0]], pwrites=[r_stat])
                        op(act, lambda: nc.scalar.activation(out=stat[:, c0 + 1:c0 + 2], in_=stat[:, c0:c0 + 1], func=AF.Ln,
                                                             scale=1.0 / D, bias=EPS),
                           reads=[r_stat], pwrites=[r_stat])
                        op(act, lambda: nc.scalar.activation(out=stat[:, c0 + 1:c0 + 2], in_=stat[:, c0 + 1:c0 + 2], func=AF.Exp,
                                                             scale=-0.5),
                           reads=[r_stat], pwrites=[r_stat])
                        xs = xnc[0] % 2
                        xnc[0] += 1
                        op(dve, lambda: nc.vector.tensor_scalar(out=xn[xs][:], in0=xt, scalar1=stat[:, c0 + 1:c0 + 2], scalar2=None,
                                                                op0=ALU.mult),
                           reads=[rx, r_stat], writes=[r_xn[xs]])
                        for hb in range(2):
                            bk = nextbank()

                            def tr():
                                for i in range(4):
                                    kc = hb * 4 + i
                                    ins = nc.tensor.transpose(psum[:, bk, i * 128:(i + 1) * 128], xn[xs][:, kc * 128:(kc + 1) * 128], identf[:])
                                return ins
                            op(pe, tr, reads=[r_xn[xs], r_ident], writes=[bankres[bk]], pe_inorder=True)
                            pv = psum[:, bk, :].rearrange("p (k t) -> p k t", k=4)
                            ov = hnT[:, hb * 4:(hb + 1) * 4, tl * 128:(tl + 1) * 128]
                            gcol = mcols[:, w, 2 * slot, hb * 4:(hb + 1) * 4].unsqueeze(2).to_broadcast([128, 4, 128])
                            scolb = mcols[:, w, 2 * slot + 1, hb * 4:(hb + 1) * 4].unsqueeze(2).to_broadcast([128, 4, 128])
                            mi = mtc[0] % 2
                            mtc[0] += 1
                            xv = mt[mi][:].rearrange("p (k t) -> p k t", k=4)
                            op(dve, lambda: nc.vector.tensor_tensor(out=xv, in0=pv, in1=gcol, op=ALU.mult),
                               reads=[bankres[bk], r_mod], writes=[r_mt[mi]])
                            op(dve, lambda: nc.vector.tensor_tensor(out=ov, in0=xv, in1=scolb, op=ALU.add),
                               reads=[r_mt[mi], r_mod], pwrites=[r_hnT[tl]])

                    up_pref = {}

                    def load_up(ll, f, c):
                        s = upc[0] % NUP
                        upc[0] += 1
                        dma(sp, wup[s][:], wupb[ll, f, c], reads=[r_wup[ll][f]], writes=[r_wups[s]])
                        return s

                    def ffn(gi, tiles, w, slot, gslot, ll, f, nxt=None):
                        nt = len(tiles)
                        ntok = nt * 128
                        xb = gi % NXB
                        pre = up_pref.pop((ll, f), None)
                        if pre is None:
                            pre = [load_up(ll, f, c) for c in range(min(NUP - 1, NFC))]
                        slots = list(pre)
                        for fc in range(NFC):
                            if fc + NUP - 1 < NFC:
                                slots.append(load_up(ll, f, fc + NUP - 1))
                            if fc < 11:
                                c0, c1 = 2 * fc, 2 * fc + 2
                                dma(sp, wdn[:, c0:c1, :], wdnb[ll, f, c0:c1].rearrange("c p n -> p c n"),
                                    reads=[r_wdn[ll][f]], pwrites=[r_wdns])
                            s = slots[fc]
                            bg, bu = nextbank(), nextbank()

                            def mmg():
                                for kc in range(8):
                                    ins = nc.tensor.matmul(psum[:, bg, 0:ntok], lhsT=wup[s][:, kc, 0:128], rhs=hnT[:, kc, 0:ntok],
                                                           start=(kc == 0), stop=(kc == 7))
                                return ins

                            def mmu():
                                for kc in range(8):
                                    ins = nc.tensor.matmul(psum[:, bu, 0:ntok], lhsT=wup[s][:, kc, 128:256], rhs=hnT[:, kc, 0:ntok],
                                                           start=(kc == 0), stop=(kc == 7))
                                return ins
                            op(pe, mmg, reads=[r_wups[s]] + r_hnT[:nt], writes=[bankres[bg]], pe_inorder=True)
                            op(pe, mmu, reads=[r_wups[s]] + r_hnT[:nt], writes=[bankres[bu]], pe_inorder=True)
                            si = sgc[0] % 2
                            sgc[0] += 1
                            op(act, lambda: nc.scalar.activation(out=sg[si][:, 0:ntok], in_=psum[:, bg, 0:ntok], func=AF.Silu),
                               reads=[bankres[bg]], writes=[r_sg[si]])
                            op(dve, lambda: nc.vector.tensor_tensor(out=hidT[:, fc, 0:ntok], in0=psum[:, bu, 0:ntok], in1=sg[si][:, 0:ntok],
                                                                    op=ALU.mult),
                               reads=[bankres[bu], r_sg[si]], pwrites=[r_hid])
                        if nxt is not None:
                            up_pref[nxt] = [load_up(nxt[0], nxt[1], c) for c in range(min(NUP - 1, NFC))]
                        for tl in range(nt):
                            for nh in range(2):
                                bk = nextbank()

                                def mmd():
                                    for fc in range(NFC):
                                        ins = nc.tensor.matmul(psum[:, bk, :], lhsT=hidT[:, fc, tl * 128:(tl + 1) * 128],
                                                               rhs=wdn[:, fc, nh * 512:(nh + 1) * 512],
                                                               start=(fc == 0), stop=(fc == NFC - 1))
                                    return ins
                                op(pe, mmd, reads=[r_hid, r_wdns], writes=[bankres[bk]], pe_inorder=True)
                                ti = tyc[0] % 2
                                tyc[0] += 1
                                op(dve, lambda: nc.vector.tensor_tensor(out=tmpy[ti][:], in0=psum[:, bk, :],
                                                                        in1=mgate[:, gslot, nh * 512:(nh + 1) * 512], op=ALU.mult),
                                   reads=[bankres[bk], r_mgate], writes=[r_tmpy[ti]])
                                xsl = xg[xb][:, tl, nh * 512:(nh + 1) * 512]
                                op(dve, lambda: nc.vector.tensor_tensor(out=xsl, in0=xsl, in1=tmpy[ti][:], op=ALU.add),
                                   reads=[r_tmpy[ti]], pwrites=[r_xg[xb][tl]])

                    def inproj(gi, tiles, w):
                        nt = len(tiles)
                        xb = gi % NXB
                        if w == 0:
                            r0 = tiles[0] * 128
                            dma(sp, ropet[:, 0:nt, :], rope[r0:r0 + nt * 128, :].rearrange("(t p) c -> p t c", p=128), writes=[r_rope])
                        import collections
                        tails = collections.deque()
                        LAGF = 2
                        sts = []
                        for tl in range(nt):
                            sti = stc[0] % NST
                            stc[0] += 1
                            sts.append(sti)
                        for cc in range(5):
                            ncol = 512 if cc < 4 else 256
                            wsl = winc[0] % NWI
                            winc[0] += 1
                            dma(sp, win[wsl][:, :, 0:ncol], winb[l][:, :, cc * 512:cc * 512 + ncol], reads=[r_win[l]], writes=[r_wins[wsl]])
                            for tl in range(nt):
                                sti = sts[tl]
                                bk = nextbank()

                                def mmi():
                                    for kc in range(8):
                                        ins = nc.tensor.matmul(psum[:, bk, 0:ncol], lhsT=hnT[:, kc, tl * 128:(tl + 1) * 128],
                                                               rhs=win[wsl][:, kc, 0:ncol], start=(kc == 0), stop=(kc == 7))
                                    return ins
                                op(pe, mmi, reads=[r_hnT[tl], r_wins[wsl]], writes=[bankres[bk]], pe_inorder=True)
                                P = psum[:, bk, :]
                                if cc == 3:
                                    Pv = P.rearrange("p (i two d) -> p i two d", i=4, two=2)
                                    vv = vs[sti][:, 0:8, :].rearrange("p (i two) d -> p i two d", two=2)
                                    op(act, lambda: nc.scalar.activation(out=vv[:, :, 0, 0:64], in_=Pv[:, :, 0, :], func=AF.Copy),
                                       reads=[bankres[bk]], pwrites=[r_vs[sti]])
                                    op(act, lambda: nc.scalar.activation(out=vv[:, :, 1, 64:128], in_=Pv[:, :, 1, :], func=AF.Copy),
                                       reads=[bankres[bk]], pwrites=[r_vs[sti]])
                                    continue
                                nh = 8 if cc < 4 else 2
                                ncn = nh * 64
                                so = cc * 8 if cc < 3 else 24
                                if cc == 4:
                                    op(act, lambda: nc.scalar.activation(out=vs[sti][:, 8:10, 0:64], in_=P[:, 128:256].rearrange("p (h d) -> p h d", h=2), func=AF.Copy),
                                       reads=[bankres[bk]], pwrites=[r_vs[sti]])
                                    op(act, lambda: nc.scalar.activation(out=vs[sti][:, 10:12, 64:128], in_=P[:, 128:256].rearrange("p (h d) -> p h d", h=2), func=AF.Copy),
                                       reads=[bankres[bk]], pwrites=[r_vs[sti]])
                                sqi = 0
                                sqc[0] += 1
                                op(act, lambda: nc.scalar.activation(out=sq[sqi][:, 0:ncn], in_=P[:, 0:ncn], func=AF.Square),
                                   reads=[bankres[bk]], writes=[r_sq[sqi]])
                                op(dve, lambda: nc.vector.tensor_reduce(out=ssq[:, so:so + nh], in_=sq[sqi][:, 0:ncn].rearrange("p (h d) -> p h d", h=nh),
                                                                        axis=AX.X, op=ALU.add),
                                   reads=[r_sq[sqi]], pwrites=[r_ssq])
                                op(act, lambda: nc.scalar.activation(out=rsq[:, so:so + nh], in_=ssq[:, so:so + nh], func=AF.Ln, scale=1.0 / 64, bias=EPS),
                                   reads=[r_ssq], pwrites=[r_rsq])
                                op(act, lambda: nc.scalar.activation(out=rsq[:, so:so + nh], in_=rsq[:, so:so + nh], func=AF.Exp, scale=-0.5),
                                   reads=[r_rsq], pwrites=[r_rsq])
                                qi = qnc[0] % NQN
                                qnc[0] += 1
                                op(dve, lambda: nc.vector.tensor_tensor(out=qn[qi][:, 0:ncn].rearrange("p (h d) -> p h d", h=nh),
                                                                        in0=P[:, 0:ncn].rearrange("p (h d) -> p h d", h=nh),
                                                                        in1=rsq[:, so:so + nh].unsqueeze(2).to_broadcast([128, nh, 64]), op=ALU.mult),
                                   reads=[bankres[bk], r_rsq], writes=[r_qn[qi]])
                                if cc in (0, 2):
                                    def tail_a(qi=qi, sti=sti, cc=cc):
                                        b2 = nextbank()

                                        def tr():
                                            for i in range(4):
                                                ins = nc.tensor.transpose(psum[:, b2, i * 128:(i + 1) * 128], qn[qi][:, i * 128:(i + 1) * 128], identf[:])
                                            return ins
                                        op(pe, tr, reads=[r_qn[qi], r_ident], writes=[bankres[b2]], pe_inorder=True)
                                        if cc == 0:
                                            op(act, lambda: nc.scalar.activation(out=qTs[sti][:, 0:4, :], in_=psum[:, b2, :].rearrange("p (k t) -> p k t", k=4),
                                                                                 func=AF.Identity, scale=gcols[:, 0:1]),
                                               reads=[bankres[b2], r_g], pwrites=[r_qTs[sti]])
                                        else:
                                            op(act, lambda: nc.scalar.activation(out=kTs[sti][:, 0:4, :], in_=psum[:, b2, :].rearrange("p (k t) -> p k t", k=4),
                                                                                 func=AF.Identity, scale=gcols[:, 1:2]),
                                               reads=[bankres[b2], r_g], pwrites=[r_kTs[sti]])
                                    tails.append(tail_a)
                                    while len(tails) > LAGF:
                                        tails.popleft()()
                                    continue
                                gi_ = 0 if cc == 1 else 1
                                qri = qrc[0] % NQR
                                qrc[0] += 1
                                qr = qrs[qri]
                                r_qr = r_qrs[qri]
                                if w == 0:
                                    gout, r_gout = qg, r_qg
                                else:
                                    gout, r_gout = qr, r_qr
                                qgv = gout[:, 0:ncn].rearrange("p (h d) -> p h d", h=nh)
                                op(dve, lambda: nc.vector.tensor_tensor(out=qgv, in0=qn[qi][:, 0:ncn].rearrange("p (h d) -> p h d", h=nh),
                                                                        in1=gbc[:, gi_, :].unsqueeze(1).to_broadcast([128, nh, 64]), op=ALU.mult),
                                   reads=[r_qn[qi], r_g], writes=[r_gout])
                                if w == 0:
                                    cosb = ropet[:, tl, 0:64].unsqueeze(1).to_broadcast([128, nh, 64])
                                    op(dve, lambda: nc.vector.tensor_tensor(out=t1[:, 0:ncn].rearrange("p (h d) -> p h d", h=nh), in0=qgv, in1=cosb, op=ALU.mult),
                                       reads=[r_qg, r_rope], writes=[r_t1])
                                    q5 = qg[:, 0:ncn].rearrange("p (h a t f) -> p h a t f", h=nh, a=2, t=2, f=16)
                                    t5 = t1[:, 0:ncn].rearrange("p (h a t f) -> p h a t f", h=nh, a=2, t=2, f=16)
                                    r5 = qr[:, 0:ncn].rearrange("p (h a t f) -> p h a t f", h=nh, a=2, t=2, f=16)
                                    u4 = tu[:, 0:ncn // 2].rearrange("p (h a f) -> p h a f", h=nh, a=2, f=16)
                                    for half in range(2):
                                        sinb = ropet[:, tl, 64 + 32 * half:96 + 32 * half].rearrange("p (a f) -> p a f", a=2).unsqueeze(1).to_broadcast([128, nh, 2, 16])
                                        op(dve, lambda: nc.vector.tensor_tensor(out=u4, in0=q5[:, :, :, 1 - half, :], in1=sinb, op=ALU.mult),
                                           reads=[r_qg, r_rope], writes=[r_tu])
                                        op(dve, lambda: nc.vector.tensor_tensor(out=r5[:, :, :, half, :], in0=t5[:, :, :, half, :], in1=u4, op=ALU.add),
                                           reads=[r_t1, r_tu], pwrites=[r_qr])
                                src = qr
                                r_src = r_qr
                                if cc == 1:
                                    def tail_b(src=src, r_src=r_src, sti=sti):
                                        b2 = nextbank()

                                        def tr():
                                            for i in range(4):
                                                ins = nc.tensor.transpose(psum[:, b2, i * 128:(i + 1) * 128], src[:, i * 128:(i + 1) * 128], identf[:])
                                            return ins
                                        op(pe, tr, reads=[r_src, r_ident], writes=[bankres[b2]], pe_inorder=True)
                                        op(act, lambda: nc.scalar.activation(out=qTs[sti][:, 4:8, :], in_=psum[:, b2, :].rearrange("p (k t) -> p k t", k=4), func=AF.Copy),
                                           reads=[bankres[b2]], pwrites=[r_qTs[sti]])
                                    tails.append(tail_b)
                                else:
                                    kdi = kdc[0] % NQR
                                    kdc[0] += 1
                                    kdup = kdups[kdi]
                                    r_kdup = r_kdups[kdi]
                                    for dd in range(2):
                                        op(dve, lambda: nc.vector.tensor_copy(out=kdup[:, :, dd * 64:(dd + 1) * 64], in_=src[:, 0:128].rearrange("p (h d) -> p h d", h=2)),
                                           reads=[r_src], pwrites=[r_kdup])

                                    def tail_k(kdup=kdup, r_kdup=r_kdup, sti=sti):
                                        b2 = nextbank()

                                        def tr():
                                            for i in range(2):
                                                ins = nc.tensor.transpose(psum[:, b2, i * 128:(i + 1) * 128], kdup[:, i, :], identf[:])
                                            return ins
                                        op(pe, tr, reads=[r_kdup, r_ident], writes=[bankres[b2]], pe_inorder=True)
                                        op(act, lambda: nc.scalar.activation(out=kTs[sti][:, 4:6, :], in_=psum[:, b2, 0:256].rearrange("p (k t) -> p k t", k=2), func=AF.Copy),
                                           reads=[bankres[b2]], pwrites=[r_kTs[sti]])
                                    tails.append(tail_k)
                                while len(tails) > LAGF:
                                    tails.popleft()()
                        while tails:
                            tails.popleft()()
                        for tl in range(nt):
                            sti = sts[tl]
                            tg = tiles[tl]
                            dma(sp, qT_d[par, tg], qTs[sti][:], reads=[r_qTs[sti]], pwrites=[r_qkv[par]])
                            dma(sp, kT_d[par, tg], kTs[sti][:], reads=[r_kTs[sti]], pwrites=[r_qkv[par]])
                            dma(sp, v_d[par, tg], vs[sti][:], reads=[r_vs[sti]], pwrites=[r_qkv[par]])

                    def load_x(gi, typ, tiles):
                        xb = gi % NXB
                        nt = len(tiles)
                        if l == 0:
                            src = xin if typ == "x" else ctxin
                            r0 = tiles[0] * 128 if typ == "x" else 0
                            dma(sp, xg[xb][:, 0:nt, :], src[r0:r0 + nt * 128, :].rearrange("(t p) d -> p t d", p=128), writes=r_xg[xb][:nt])
                        else:
                            r0 = tiles[0] * 128
                            dma(sp, xg[xb][:, 0:nt, :], xres[r0:r0 + nt * 128, :].rearrange("(t p) d -> p t d", p=128),
                                reads=[r_xres], writes=r_xg[xb][:nt])

                    def stages_of(typ):
                        st = []
                        if do_ffn2 and not (typ == "c" and l - 1 >= DEPTH - 1):
                            st.append(("ffn", 0, 0, l - 1, 1))
                        if do_ffn1:
                            st.append(("ffn", 1, 1, l, 0))
                            st.append(("inproj", 2))
                        return st
                    ffn_calls = []
                    for gi_, (typ_, _t) in enumerate(groups):
                        for si__, st_ in enumerate(stages_of(typ_)):
                            if st_[0] == "ffn":
                                ffn_calls.append((gi_, si__, (st_[3], st_[4])))
                    nxt_spec = {}
                    for i_ in range(len(ffn_calls) - 1):
                        nxt_spec[(ffn_calls[i_][0], ffn_calls[i_][1])] = ffn_calls[i_ + 1][2]

                    load_x(0, groups[0][0], groups[0][1])
                    for gi, (typ, tiles) in enumerate(groups):
                        nt = len(tiles)
                        w = 0 if typ == "x" else 1
                        xb = gi % NXB
                        if w == 1:
                            load_mgate(1)
                        if gi + 1 < len(groups):
                            load_x(gi + 1, groups[gi + 1][0], groups[gi + 1][1])
                        stages = stages_of(typ)
                        for tl in range(nt):
                            norm_transpose(gi, tl, nt, w, stages[0][1])
                        for si_, st in enumerate(stages):
                            if st[0] == "ffn":
                                nxt = stages[si_ + 1] if si_ + 1 < len(stages) else None
                                ffn(gi, tiles, w, st[1], st[2], st[3], st[4], nxt=nxt_spec.get((gi, si_)))
                                if nxt is not None:
                                    for tl in range(nt):
                                        norm_transpose(gi, tl, nt, w, nxt[1])
                            else:
                                inproj(gi, tiles, w)
                        if l <= DEPTH - 1:
                            r0 = tiles[0] * 128
                            dma(sp, xres[r0:r0 + nt * 128, :].rearrange("(t p) d -> p t d", p=128), xg[xb][:, 0:nt, :],
                                reads=r_xg[xb][:nt], pwrites=[r_xres])
                        else:
                            r0 = tiles[0] * 128
                            dma(sp, yout[r0:r0 + nt * 128, :].rearrange("(t p) d -> p t d", p=128), xg[xb][:, 0:nt, :],
                                reads=r_xg[xb][:nt], pwrites=[r_yout])

            r_yout = Res("yout")

            def pass_B(l):
                par = l % 2
                last = l == DEPTH - 1
                groups = [("x", g) for g in lat_groups(NOUT[l])]
                if not last:
                    groups.append(("c", [CT0, CT0 + 1]))
                nkv = NKV[l]
                barrier()
                with ExitStack() as bs:
                    def bsb(name, shape, dt=F32):
                        return bs.enter_context(nc.sbuf_tensor("%s_B%d" % (name, l), list(shape), dt))
                    xgs = [bsb("bxg%d" % i, [128, GT, D]) for i in range(2)]
                    r_xgs = [[Res("bxg%d_%d" % (i, t)) for t in range(GT)] for i in range(2)]
                    QTs = [bsb("QT%d" % i, [128, GT, 8, 128], BF16) for i in range(2)]
                    r_QTs = [Res("QT0"), Res("QT1")]
                    KTR = 12
                    KT = bsb("KT", [128, KTR, 6, 128], BF16)
                    r_KTs = [Res("KT%d" % i) for i in range(KTR)]
                    VW = bsb("VW", [128, KTR, 12, 128], BF16)
                    r_VWs = [Res("VW%d" % i) for i in range(KTR)]
                    KTc = bsb("KTc", [128, 2, 6, 128], BF16)
                    VWc = bsb("VWc", [128, 2, 12, 128], BF16)
                    r_ctxkv = Res("ctxkv")
                    EB = bsb("EB", [128, 13 * 8, 128], BF16)
                    r_EB = Res("EB")
                    ebst = [bsb("ebst%d" % i, [128, 13, 128]) for i in range(2)]
                    r_ebst = [Res("ebst0"), Res("ebst1")]
                    wo = bsb("wo", [128, 8, D], BF16)
                    r_wo = Res("wo")
                    gate = bsb("bgate", [128, D])
                    r_gate = Res("bgate")
                    PTn = [bsb("PTn%d" % i, [128, 7 * 128], BF16) for i in range(3)]
                    r_PTn = [Res("PTn%d" % i) for i in range(3)]
                    NPTW = 10
                    PTw = [bsb("PTw%d" % i, [128, 512], BF16) for i in range(NPTW)]
                    r_PTw = [Res("PTw%d" % i) for i in range(NPTW)]
                    aT = bsb("aT", [128, GT, 8, 128], BF16)
                    r_aT = [Res("aT%d" % t) for t in range(GT)]
                    rden = [bsb("rden%d" % i, [128, 1024]) for i in range(2)]
                    r_rden = [Res("rden0"), Res("rden1")]
                    esk = bsb("esk", [128, 8])
                    r_esk = Res("esk")

                    def load_wo(w):
                        dma(sp, wo[:], woutb[l], reads=[r_wout[l]], writes=[r_wo])
                        dma(sp, gate[:], modtab[l, w, 5:6, :].partition_broadcast(128), reads=[r_modtab[l]], writes=[r_gate])
                        op(dve, lambda: nc.vector.tensor_tensor(out=wo[:], in0=wo[:], in1=gate[:].unsqueeze(1).to_broadcast([128, 8, D]), op=ALU.mult),
                           reads=[r_gate], writes=[r_wo])
                    load_wo(0)
                    dma(sp, KTc[:], kT_d[par, CT0:CT0 + 2].rearrange("t p k q -> p t k q"), reads=[r_qkv[par]], pwrites=[r_ctxkv])
                    dma(sp, VWc[:], v_d[par, CT0:CT0 + 2].rearrange("t p h d -> p t h d"), reads=[r_qkv[par]], pwrites=[r_ctxkv])
                    op(act, lambda: nc.scalar.activation(out=esk[:], in_=esink[:], func=AF.Exp), reads=[r_esink], writes=[r_esk])
                    for h in range(8):
                        s = h % 2
                        dma(sp, ebst[s][:], nab[l, :, h].rearrange("s k q -> k s q"), writes=[r_ebst[s]])
                        op(act, lambda: nc.scalar.activation(out=EB[:].rearrange("p (s h) q -> p s h q", h=8)[:, :, h, :], in_=ebst[s][:], func=AF.Exp),
                           reads=[r_ebst[s]], pwrites=[r_EB])
                    EBv = EB[:].rearrange("p (s h) q -> p s h q", h=8)

                    sring = [0]
                    ptn_c = [0]
                    ptw_c = [0]
                    rd_c = [0]
                    ty_c = [0]
                    odw_c = [0]

                    ring1 = [0]

                    def alloc_pair():
                        if ring1[0] % 2 == 1:
                            ring1[0] += 1
                        b = ring1[0] % 4
                        ring1[0] += 2
                        return b

                    def alloc_one():
                        b = ring1[0] % 4
                        ring1[0] += 1
                        return b

                    import collections
                    defer = collections.deque()
                    LAGB = 1

                    def sched(s1, s2):
                        if s1 is not None:
                            s1()
                        if s2 is not None:
                            defer.append(s2)
                        while len(defer) > LAGB:
                            defer.popleft()()

                    def flush():
                        while defer:
                            defer.popleft()()

                    def attn_tile2(typ, tl, tg, gpar):
                        QT = QTs[gpar]
                        r_QT = r_QTs[gpar]
                        if typ == "x":
                            if tg == 0:
                                na_units = [("l", (tg + r) % KTR, 5 + r) for r in range(0, 4)]
                            elif tg == 1:
                                na_units = [("l", (tg + r) % KTR, 9 + (r + 1)) for r in range(-1, 3)]
                            else:
                                na_units = [("l", (tg + r) % KTR, r + 2) for r in range(-2, 3)]
                            na_units += [("c", 0, None), ("c", 1, None)]
                            wa_units = []
                            for r in (-1, 0, 1):
                                if 0 <= tg + r < nkv:
                                    wa_units.append(("l", (tg + r) % KTR, r))
                            wa_units += [("c", 0, None), ("c", 1, None)]
                        else:
                            na_units = [("c", 0, None), ("c", 1, None)]
                            wa_units = [("c", 0, None), ("c", 1, None)]

                        def kt_ap(u, pi, base):
                            src = KT if u[0] == "l" else KTc
                            return src[base:base + 64, u[1], pi, :]

                        def v_ap(u, hidx):
                            src = VW if u[0] == "l" else VWc
                            return src[:, u[1], hidx, :]
                        nu = len(na_units)
                        nl = sum(1 for u in na_units if u[0] == "l")
                        rk_na = [r_KTs[u[1]] for u in na_units if u[0] == "l"] + [r_ctxkv]
                        rv_na = [r_VWs[u[1]] for u in na_units if u[0] == "l"] + [r_ctxkv]
                        rk_wa = [r_KTs[u[1]] for u in wa_units if u[0] == "l"] + [r_ctxkv]
                        rv_wa = [r_VWs[u[1]] for u in wa_units if u[0] == "l"] + [r_ctxkv]

                        def na_item(h):
                            pi, base = h // 2, 64 * (h % 2)
                            st = {}

                            def s1():
                                b0 = alloc_pair()
                                S = psum[:, b0:b0 + 2, :].rearrange("p b n -> p (b n)")

                                def mms():
                                    for ui, u in enumerate(na_units):
                                        ins = nc.tensor.matmul(S[:, ui * 128:(ui + 1) * 128], lhsT=kt_ap(u, pi, base),
                                                               rhs=QT[base:base + 64, tl, pi, :], start=True, stop=True)
                                    return ins
                                op(pe, mms, reads=rk_na + [r_QT], writes=[bankres[b0], bankres[b0 + 1]], pe_inorder=True)
                                pn = ptn_c[0] % 3
                                ptn_c[0] += 1
                                st["pn"] = pn
                                op(act, lambda: nc.scalar.activation(out=PTn[pn][:, 0:nu * 128], in_=S[:, 0:nu * 128], func=AF.Exp),
                                   reads=[bankres[b0], bankres[b0 + 1]], writes=[r_PTn[pn]])
                                if nl > 0:
                                    e0 = na_units[0][2]
                                    pv = PTn[pn][:, 0:nl * 128].rearrange("p (u q) -> p u q", u=nl)
                                    op(dve, lambda: nc.vector.tensor_tensor(out=pv, in0=pv, in1=EBv[:, e0:e0 + nl, h, :], op=ALU.mult),
                                       reads=[r_EB], pwrites=[r_PTn[pn]])

                            def s2():
                                pn = st["pn"]
                                ob = 4 + h // 4

                                def mmo():
                                    for ui, u in enumerate(na_units):
                                        ins = nc.tensor.matmul(psum[:, ob, (h % 4) * 128:(h % 4 + 1) * 128], lhsT=v_ap(u, h),
                                                               rhs=PTn[pn][:, ui * 128:(ui + 1) * 128], start=(ui == 0), stop=(ui == nu - 1))
                                    return ins
                                if h % 4 == 0:
                                    op(pe, mmo, reads=[r_PTn[pn]] + rv_na, writes=[bankres[ob]], pe_inorder=True)
                                else:
                                    op(pe, mmo, reads=[r_PTn[pn]] + rv_na, pwrites=[bankres[ob]], pe_inorder=True)
                                if h == 7:
                                    ri = rd_c[0] % 2
                                    rd_c[0] += 1
                                    od = psum[:, 4:6, :].rearrange("p b n -> p (b n)")
                                    odv = od.rearrange("p (i two q) -> p i two q", i=4, two=2)
                                    rdv = rden[ri][:, :].rearrange("p (i two q) -> p i two q", i=4, two=2)
                                    op(act, lambda: nc.scalar.activation(out=rdv[64:128, :, 0, :], in_=odv[64:128, :, 0, :], func=AF.Ln),
                                       reads=[bankres[4], bankres[5]], writes=[r_rden[ri]])
                                    op(act, lambda: nc.scalar.activation(out=rdv[0:64, :, 1, :], in_=odv[0:64, :, 1, :], func=AF.Ln),
                                       reads=[bankres[4], bankres[5]], pwrites=[r_rden[ri]])
                                    op(act, lambda: nc.scalar.activation(out=rdv[64:128, :, 0, :], in_=rdv[64:128, :, 0, :], func=AF.Exp, scale=-1.0),
                                       reads=[r_rden[ri]], pwrites=[r_rden[ri]])
                                    op(act, lambda: nc.scalar.activation(out=rdv[0:64, :, 1, :], in_=rdv[0:64, :, 1, :], func=AF.Exp, scale=-1.0),
                                       reads=[r_rden[ri]], pwrites=[r_rden[ri]])
                                    op(dve, lambda: nc.vector.tensor_tensor(out=aT[0:64, tl, 0:4, :], in0=odv[0:64, :, 0, :], in1=rdv[64:128, :, 0, :], op=ALU.mult),
                                       reads=[bankres[4], bankres[5], r_rden[ri]], pwrites=[r_aT[tl]])
                                    op(dve, lambda: nc.vector.tensor_tensor(out=aT[64:128, tl, 0:4, :], in0=odv[64:128, :, 1, :], in1=rdv[0:64, :, 1, :], op=ALU.mult),
                                       reads=[bankres[4], bankres[5], r_rden[ri]], pwrites=[r_aT[tl]])
                            return s1, s2

                        for h in range(8 if "na" in bparts else 0):
                            sched(*na_item(h))

                        nuw = len(wa_units)

                        def wa_item(kv, u0, pws):
                            us = wa_units[u0:u0 + 2]
                            lastpair = u0 + 2 >= nuw

                            def s1():
                                b0 = alloc_pair()

                                def mms():
                                    for j, u in enumerate(us):
                                        nc.tensor.matmul(psum[:, b0, j * 256:(j + 1) * 256], lhsT=kt_ap(u, 4 + kv, 0),
                                                         rhs=QT[0:64, tl, 4 + 2 * kv:6 + 2 * kv, :].rearrange("p a q -> p (a q)"), start=True, stop=True)
                                        ins = nc.tensor.matmul(psum[:, b0 + 1, j * 256:(j + 1) * 256], lhsT=kt_ap(u, 4 + kv, 64),
                                                               rhs=QT[64:128, tl, 4 + 2 * kv:6 + 2 * kv, :].rearrange("p a q -> p (a q)"), start=True, stop=True)
                                    return ins
                                op(pe, mms, reads=rk_wa + [r_QT], writes=[bankres[b0], bankres[b0 + 1]], pe_inorder=True)
                                for j, u in enumerate(us):
                                    pw = ptw_c[0] % NPTW
                                    ptw_c[0] += 1
                                    pws.append(pw)
                                    op(act, lambda: nc.scalar.activation(out=PTw[pw][:].rearrange("p (b n) -> p b n", b=2),
                                                                         in_=psum[:, b0:b0 + 2, j * 256:(j + 1) * 256], func=AF.Exp),
                                       reads=[bankres[b0], bankres[b0 + 1]], writes=[r_PTw[pw]])
                                    if u[0] == "l" and u[2] != 0:
                                        mi = 0 if u[2] < 0 else 1
                                        pv = PTw[pw][:].rearrange("p (s q) -> p s q", s=4)
                                        op(dve, lambda: nc.vector.tensor_tensor(out=pv, in0=pv, in1=wam_sb[:, mi, :].unsqueeze(1).to_broadcast([128, 4, 128]), op=ALU.mult),
                                           reads=[r_wam], pwrites=[r_PTw[pw]])

                            def s2():
                                ob = 6 + kv

                                def mmo():
                                    for ui, u in enumerate(wa_units):
                                        nc.tensor.matmul(psum[:, ob, 0:256], lhsT=v_ap(u, 8 + kv), rhs=PTw[pws[ui]][:, 0:256],
                                                         start=(ui == 0), stop=(ui == nuw - 1), skip_group_check=True)
                                        ins = nc.tensor.matmul(psum[:, ob, 256:512], lhsT=v_ap(u, 10 + kv), rhs=PTw[pws[ui]][:, 256:512],
                                                               start=False, stop=(ui == nuw - 1), skip_group_check=True)
                                    return ins
                                op(pe, mmo, reads=[r_PTw[p] for p in pws] + rv_wa, writes=[bankres[ob]], pe_inorder=True)
                                if kv == 0:
                                    return
                                ri = rd_c[0] % 2
                                rd_c[0] += 1
                                P2 = psum[:, 6:8, :]
                                rd2 = rden[ri][:, :].rearrange("p (b n) -> p b n", b=2)
                                esk4 = esk[:, 0:8].rearrange("p (k s) -> p k s", k=2)
                                op(dve, lambda: nc.vector.tensor_tensor(out=rd2[64:128, :, 0:256].rearrange("p b (s q) -> p b s q", s=2),
                                                                        in0=P2[64:128, :, 0:256].rearrange("p b (s q) -> p b s q", s=2),
                                                                        in1=esk4[64:128, :, 0:2].unsqueeze(3).to_broadcast([64, 2, 2, 128]), op=ALU.add),
                                   reads=[bankres[6], bankres[7], r_esk], writes=[r_rden[ri]])
                                op(dve, lambda: nc.vector.tensor_tensor(out=rd2[0:64, :, 256:512].rearrange("p b (s q) -> p b s q", s=2),
                                                                        in0=P2[0:64, :, 256:512].rearrange("p b (s q) -> p b s q", s=2),
                                                                        in1=esk4[0:64, :, 2:4].unsqueeze(3).to_broadcast([64, 2, 2, 128]), op=ALU.add),
                                   reads=[bankres[6], bankres[7], r_esk], pwrites=[r_rden[ri]])
                                for (p0, p1, c0, c1) in ((64, 128, 0, 256), (0, 64, 256, 512)):
                                    op(act, lambda: nc.scalar.activation(out=rd2[p0:p1, :, c0:c1], in_=rd2[p0:p1, :, c0:c1], func=AF.Ln),
                                       reads=[r_rden[ri]], pwrites=[r_rden[ri]])
                                    op(act, lambda: nc.scalar.activation(out=rd2[p0:p1, :, c0:c1], in_=rd2[p0:p1, :, c0:c1], func=AF.Exp, scale=-1.0),
                                       reads=[r_rden[ri]], pwrites=[r_rden[ri]])
                                op(dve, lambda: nc.vector.tensor_tensor(out=aT[0:64, tl, 4:8, :].rearrange("p (b s) q -> p b (s q)", b=2),
                                                                        in0=P2[0:64, :, 0:256], in1=rd2[64:128, :, 0:256], op=ALU.mult),
                                   reads=[bankres[6], bankres[7], r_rden[ri]], pwrites=[r_aT[tl]])
                                op(dve, lambda: nc.vector.tensor_tensor(out=aT[64:128, tl, 4:8, :].rearrange("p (b s) q -> p b (s q)", b=2),
                                                                        in0=P2[64:128, :, 256:512], in1=rd2[0:64, :, 256:512], op=ALU.mult),
                                   reads=[bankres[6], bankres[7], r_rden[ri]], pwrites=[r_aT[tl]])
                            return s1, (s2 if lastpair else None)

                        for kv in range(2 if "wa" in bparts else 0):
                            pws = []
                            for u0 in range(0, nuw, 2):
                                sched(*wa_item(kv, u0, pws))

                    def outproj(typ, tl, gpar):
                        xg = xgs[gpar]
                        r_xg = r_xgs[gpar]
                        for nh in range(2):
                            bk = alloc_one()

                            def mmp():
                                for h in range(8):
                                    ins = nc.tensor.matmul(psum[:, bk, :], lhsT=aT[:, tl, h, :], rhs=wo[:, h, nh * 512:(nh + 1) * 512],
                                                           start=(h == 0), stop=(h == 7))
                                return ins
                            op(pe, mmp, reads=[r_aT[tl], r_wo], writes=[bankres[bk]], pe_inorder=True)
                            xsl = xg[:, tl, nh * 512:(nh + 1) * 512]
                            op(dve, lambda: nc.vector.tensor_tensor(out=xsl, in0=psum[:, bk, :], in1=xsl, op=ALU.add),
                               reads=[bankres[bk]], pwrites=[r_xg[tl]])

                    kv_loaded = [0]

                    def load_kv_upto(t1):
                        t1 = min(t1, nkv)
                        t0 = kv_loaded[0]
                        while t0 < t1:
                            n = min(t1 - t0, KTR - (t0 % KTR))
                            sl = t0 % KTR
                            dma(sp, KT[:, sl:sl + n], kT_d[par, t0:t0 + n].rearrange("t p k q -> p t k q"), reads=[r_qkv[par]], writes=r_KTs[sl:sl + n])
                            dma(sp, VW[:, sl:sl + n], v_d[par, t0:t0 + n].rearrange("t p h d -> p t h d"), reads=[r_qkv[par]], writes=r_VWs[sl:sl + n])
                            t0 += n
                        kv_loaded[0] = max(kv_loaded[0], t1)

                    def load_group(gi):
                        typ, tiles = groups[gi]
                        gpar = gi % 2
                        nt = len(tiles)
                        r0 = tiles[0] * 128
                        dma(sp, xgs[gpar][:, 0:nt, :], xres[r0:r0 + nt * 128, :].rearrange("(t p) d -> p t d", p=128),
                            reads=[r_xres], writes=r_xgs[gpar][:nt])
                        dma(sp, QTs[gpar][:, 0:nt], qT_d[par, tiles[0]:tiles[0] + nt].rearrange("t p k q -> p t k q"),
                            reads=[r_qkv[par]], writes=[r_QTs[gpar]])
                        if typ == "x":
                            load_kv_upto(max(tiles[-1] + 3, 4))

                    load_group(0)
                    for gi, (typ, tiles) in enumerate(groups):
                        nt = len(tiles)
                        gpar = gi % 2
                        r0 = tiles[0] * 128
                        if typ == "c":
                            flush()
                            load_wo(1)
                        if gi + 1 < len(groups):
                            load_group(gi + 1)
                        for tl in range(nt):
                            attn_tile2(typ, tl, tiles[tl], gpar)
                            if "proj" in bparts:
                                sched(None, (lambda typ_=typ, tl_=tl, gp_=gpar: outproj(typ_, tl_, gp_)))
                        flush()
                        dma(sp, xres_w[r0:r0 + nt * 128, :].rearrange("(t p) d -> p t d", p=128), xgs[gpar][:, 0:nt, :],
                            reads=r_xgs[gpar][:nt], pwrites=[r_xres])

            seq = []
            for l in range(DEPTH):
                seq.append(("F", l))
                seq.append(("B", l))
            seq.append(("F", DEPTH))
            if btest is not None:
                load_layer_consts(btest)
                seq = [("B", btest)]
            for kind, l in seq:
                if kind == "F":
                    pass_F(l)
                else:
                    pass_B(l)
                if stop_after == (kind, l):
                    break

            r_dump = Res("dump")
            for name, (dst, src) in dump_out.items():
                rr = {"xres": [r_xres], "qT_d": r_qkv, "kT_d": r_qkv, "v_d": r_qkv, "modtab": r_modtab}[name]
                if len(src.shape) == 2:
                    nrow = src.shape[0]
                    step = 1024
                    for a in range(0, nrow, step):
                        b = min(nrow, a + step)
                        dma(sp, dst[a:b], src[a:b], reads=rr, pwrites=[r_dump])
                else:
                    for a in range(src.shape[0]):
                        if len(src.shape) >= 5:
                            for b in range(src.shape[1]):
                                dma(sp, dst[a, b], src[a, b], reads=rr, pwrites=[r_dump])
                        else:
                            dma(sp, dst[a], src[a], reads=rr, pwrites=[r_dump])
            final = {}
            _merge(final, r_yout.w)
            _merge(final, r_dump.w)
            _merge(final, r_xres.w)
            for p, v in final.items():
                nc.sync.wait_ge(p.sem, v)
            for s in pool.dsem:
                if s.val:
                    nc.gpsimd.wait_ge(s.sem, s.val)
            stats = dict(pe=pe.nins, act=act.nins, dve=dve.nins, waits=pe.nwait + act.nwait + dve.nwait + sp.nwait + pool.nwait,
                         dmas=sp.di + pool.di)
            print("program stats:", stats, "sbuf_remaining", getattr(nc, "sbuf_bytes_remaining", None))

        block.sync(program)
    return nc


def _rope_table(parity):
    rot = 32
    inv_freq = (10000.0 ** (-np.arange(0, rot, 2, dtype=np.float32) / np.float32(rot))).astype(np.float32)
    tl = np.arange(NLAT * 128)
    tg = tl if parity == 0 else (SEQ - 1 - tl)
    row = (tg // 64).astype(np.float32)
    col = (tg % 64).astype(np.float32)
    ang = np.stack([row[:, None] * inv_freq, col[:, None] * inv_freq], axis=1).astype(np.float32)
    c, s = np.cos(ang).astype(np.float32), np.sin(ang).astype(np.float32)
    tab = np.zeros((NLAT * 128, 128), np.float32)
    c64 = np.stack([c, c], axis=2)
    tab[:, 0:64] = c64.reshape(-1, 64)
    tab[:, 64:96] = (-s).reshape(-1, 32)
    tab[:, 96:128] = s.reshape(-1, 32)
    return tab


def _na_bias_tables(rpb, parity):
    L = rpb.shape[0]
    out = np.full((L, 13, 8, 128, 128), MASKV, np.float32)
    pats = [(10, r, r + 2) for r in range(-2, 3)] + [(0, r, 5 + r) for r in range(0, 4)] + [(1, r, 10 + r) for r in range(-1, 3)]
    idx = np.arange(128)
    for (j, rel, slot) in pats:
        ql = idx[None, :]
        kl = idx[:, None]
        qr_l = 2 * j + ql // 64
        qc_l = ql % 64
        kr_l = 2 * (j + rel) + kl // 64
        kc_l = kl % 64
        if parity == 0:
            qr, qc, kr, kc = qr_l, qc_l, kr_l, kc_l
        else:
            qr, qc, kr, kc = 127 - qr_l, 63 - qc_l, 127 - kr_l, 63 - kc_l
        r0 = np.clip(qr - 4, 0, 120)
        c0 = np.clip(qc - 8, 0, 48)
        valid = (kr >= r0) & (kr < r0 + 8) & (kc >= c0) & (kc < c0 + 16)
        valid = np.broadcast_to(valid, (128, 128))
        dr = np.clip(kr - qr + 7, 0, 14)
        dc = np.clip(kc - qc + 15, 0, 30)
        dr = np.broadcast_to(dr, (128, 128))
        dc = np.broadcast_to(dc, (128, 128))
        g = rpb[:, :, dr, dc]
        out[:, slot] = np.where(valid[None, None], g, np.float32(MASKV))
    return out


_NC_CACHE = {}


def make_in_maps(x, c, ctx, c_ctx, w_ada, b_ada, norm_g, w_ffn_up, w_ffn_down, w_in, w_out, qk_norm_g, na_rpb, wa_sink):
    f = lambda a: np.ascontiguousarray(np.asarray(a, dtype=np.float32))
    x, c, ctx, c_ctx = f(x), f(c), f(ctx), f(c_ctx)
    shared = dict(w_ada=f(w_ada), b_ada=f(b_ada), norm_g=f(norm_g), w_up=f(w_ffn_up), w_dn=f(w_ffn_down), w_in=f(w_in),
                  w_out=f(w_out), qkg=f(qk_norm_g), identf_in=np.eye(128, dtype=np.float32))
    sink = f(wa_sink)
    shared["sinkp"] = np.ascontiguousarray(sink[:, [0, 2, 1, 3, 4, 6, 5, 7]])
    idx = np.arange(128)
    wam = np.stack([(idx[:, None] >= idx[None, :]), (idx[:, None] <= idx[None, :])]).astype(np.float32).astype(ml_dtypes.bfloat16)
    shared["wam"] = wam
    rpb = f(na_rpb)
    tabs = [(_rope_table(p), _na_bias_tables(rpb, p)) for p in range(2)]
    in_maps = []
    for core in range(8):
        b, p = core // 2, core % 2
        xs = x[b] if p == 0 else x[b, ::-1]
        m = dict(shared)
        m["xin"] = np.ascontiguousarray(xs[:NLAT * 128])
        m["ctxin"] = np.ascontiguousarray(ctx[b])
        m["cvec"] = np.ascontiguousarray(np.stack([c[b], c_ctx]))
        m["rope"] = tabs[p][0]
        m["nab"] = tabs[p][1]
        in_maps.append(m)
    return in_maps


def kernel(x, c, ctx, c_ctx, w_ada, b_ada, norm_g, w_ffn_up, w_ffn_down, w_in, w_out, qk_norm_g, na_rpb, wa_sink):
    in_maps = make_in_maps(x, c, ctx, c_ctx, w_ada, b_ada, norm_g, w_ffn_up, w_ffn_down, w_in, w_out, qk_norm_g, na_rpb, wa_sink)
    if "nc" not in _NC_CACHE:
        _NC_CACHE["nc"] = build_program()
    nc = _NC_CACHE["nc"]
    res = run_bass_kernel_spmd(nc, in_maps, core_ids=list(range(8)))
    out = np.empty((4, SEQ, D), np.float32)
    for core in range(8):
        b, p = core // 2, core % 2
        y = np.asarray(res.results[core]["yout"], dtype=np.float32)
        if p == 0:
            out[b, 0:4096] = y
        else:
            out[b, 4096:] = y[::-1]
    return out
```

```python
import numpy as np
import ml_dtypes
from contextlib import ExitStack
import concourse.bass as bass
import concourse.mybir as mybir
from concourse.bass_utils import run_bass_kernel_spmd

F32 = mybir.dt.float32
BF16 = mybir.dt.bfloat16
AF = mybir.ActivationFunctionType
ALU = mybir.AluOpType
AX = mybir.AxisListType

D = 1024
DFF = 2816
NFC = 22
INW = 2304
DEPTH = 4
SEQ = 8192
NLAT = 40
NT_ALL = 42
CT0 = 40
NOUT = [38, 36, 34, 32]
NKV = [40, 38, 36, 34]
EPS = 1e-6
MASKV = -30000.0
GT = 4
SAME_ENG_SYNC = True


class Prod:
    __slots__ = ("sem", "val")

    def __init__(self, sem):
        self.sem = sem
        self.val = 0


class Res:
    __slots__ = ("name", "w", "r")

    def __init__(self, name):
        self.name = name
        self.w = {}
        self.r = {}


class Eng:
    def __init__(self, nc, name, h, ndma=0):
        self.nc = nc
        self.name = name
        self.h = h
        self.prod = Prod(nc.alloc_semaphore("e_" + name))
        self.seen = {}
        self.dsem = [Prod(nc.alloc_semaphore("d_%s_%d" % (name, i))) for i in range(ndma)]
        self.di = 0
        self.nwait = 0
        self.nins = 0

    def need(self, deps):
        for p, v in deps.items():
            if p is self.prod and not SAME_ENG_SYNC:
                continue
            if self.seen.get(p, 0) >= v:
                continue
            self.h.wait_ge(p.sem, v)
            self.nwait += 1
            self.seen[p] = v


def _merge(d, s):
    for p, v in s.items():
        if d.get(p, 0) < v:
            d[p] = v


def _deps(reads, writes, pwrites):
    deps = {}
    for r in reads:
        _merge(deps, r.w)
    for w in writes:
        _merge(deps, w.w)
        _merge(deps, w.r)
    for w in pwrites:
        _merge(deps, w.w)
        _merge(deps, w.r)
    return deps


def _commit(p, v, reads, writes, pwrites):
    for r in reads:
        if r.r.get(p, 0) < v:
            r.r[p] = v
    for w in writes:
        w.w = {p: v}
        w.r = {}
    for w in pwrites:
        if w.w.get(p, 0) < v:
            w.w[p] = v


def op(e, fn, reads=(), writes=(), pwrites=(), pe_inorder=False):
    deps = _deps(reads, writes, pwrites)
    if pe_inorder:
        deps.pop(e.prod, None)
    e.need(deps)
    ins = fn()
    e.prod.val += 1
    ins.then_inc(e.prod.sem, 1)
    e.nins += 1
    _commit(e.prod, e.prod.val, reads, writes, pwrites)


def dma(q, out, in_, reads=(), writes=(), pwrites=(), **kw):
    deps = _deps(reads, writes, pwrites)
    q.need(deps)
    s = q.dsem[q.di % len(q.dsem)]
    q.di += 1
    if s.val > 0 and q.seen.get(s, 0) < s.val:
        q.h.wait_ge(s.sem, s.val)
        q.seen[s] = s.val
    q.h.dma_start(out=out, in_=in_, **kw).then_inc(s.sem, 16)
    s.val += 16
    _commit(s, s.val, reads, writes, pwrites)
    return s


def lat_groups(n):
    gs = []
    t = 0
    while t < n:
        k = min(GT, n - t)
        gs.append(list(range(t, t + k)))
        t += k
    return gs


def build_program(stop_after=None, dumps=(), btest=None, bparts=('na', 'wa', 'proj')):
    nc = bass.Bass("TRN2", target_bir_lowering=False, dynamic_dma_scratch_size=8192)

    def din(name, shape, dt=F32, big=False):
        if big and btest is not None:
            return None
        return nc.dram_tensor(name, list(shape), dt, kind="ExternalInput").ap()

    def dscr(name, shape, dt, ext=False):
        if ext and btest is not None:
            return nc.dram_tensor("in_" + name, list(shape), dt, kind="ExternalInput").ap()
        return nc.dram_tensor(name, list(shape), dt).ap()

    xin = din("xin", [NLAT * 128, D], big=True)
    ctxin = din("ctxin", [256, D], big=True)
    cvec = din("cvec", [2, D], big=True)
    w_ada = din("w_ada", [DEPTH, D, 9 * D], big=True)
    b_ada = din("b_ada", [DEPTH, 9 * D], big=True)
    norm_g = din("norm_g", [DEPTH, 3, D], big=True)
    w_up = din("w_up", [DEPTH, 2, D, 2 * DFF], big=True)
    w_dn = din("w_dn", [DEPTH, 2, DFF, D], big=True)
    w_in = din("w_in", [DEPTH, D, INW], big=True)
    w_out = din("w_out", [DEPTH, D, D], big=True)
    qkg = din("qkg", [DEPTH, 4, 64])
    sinkp = din("sinkp", [DEPTH, 8])
    nab = din("nab", [DEPTH, 13, 8, 128, 128])
    rope = din("rope", [NLAT * 128, 128], big=True)
    wam = din("wam", [2, 128, 128], BF16)
    identf_d = din("identf_in", [128, 128])
    yout = nc.dram_tensor("yout", [32 * 128, D], F32, kind="ExternalOutput").ap()

    wupb = dscr("wupb", [DEPTH, 2, NFC, 128, 8, 256], BF16)
    wdnb = dscr("wdnb", [DEPTH, 2, NFC, 128, D], BF16)
    winb = dscr("winb", [DEPTH, 128, 8, INW], BF16)
    woutb = dscr("woutb", [DEPTH, 128, 8, D], BF16, ext=True)
    modtab = dscr("modtab", [DEPTH, 2, 9, D], F32, ext=True)
    xres = dscr("xres", [NT_ALL * 128, D], F32, ext=True)
    qT_d = dscr("qT_d", [2, NT_ALL, 128, 8, 128], BF16, ext=True)
    kT_d = dscr("kT_d", [2, NT_ALL, 128, 6, 128], BF16, ext=True)
    v_d = dscr("v_d", [2, NT_ALL, 128, 12, 128], BF16, ext=True)

    xres_w = xres
    if btest is not None:
        xres_w = nc.dram_tensor("xres_out", [NT_ALL * 128, D], F32, kind="ExternalOutput").ap()
    dump_out = {}
    for name in dumps:
        src = {"xres": xres, "qT_d": qT_d, "kT_d": kT_d, "v_d": v_d, "modtab": modtab}[name]
        dump_out[name] = (nc.dram_tensor("dump_" + name, list(src.shape), src.dtype, kind="ExternalOutput").ap(), src)

    es = ExitStack()
    with es:
        def sb(name, shape, dt=F32):
            return es.enter_context(nc.sbuf_tensor(name, list(shape), dt))

        pe = Eng(nc, "pe", nc.tensor)
        act = Eng(nc, "act", nc.scalar)
        dve = Eng(nc, "dve", nc.vector)
        pool = Eng(nc, "pool", nc.gpsimd, ndma=6)
        sp = Eng(nc, "sp", nc.sync, ndma=32)

        psum = es.enter_context(nc.psum_tensor("psum", [128, 8, 512], F32))
        bankres = [Res("bank%d" % i) for i in range(8)]

        r_wup = [[Res("wup%d%d" % (l, f)) for f in range(2)] for l in range(DEPTH)]
        r_wdn = [[Res("wdn%d%d" % (l, f)) for f in range(2)] for l in range(DEPTH)]
        r_win = [Res("win%d" % l) for l in range(DEPTH)]
        r_wout = [Res("wout%d" % l) for l in range(DEPTH)]
        r_modtab = [Res("modtab%d" % l) for l in range(DEPTH)]
        r_xres = Res("xres")
        r_qkv = [Res("qkv0"), Res("qkv1")]

        identf = sb("identf", [128, 128])
        r_ident = Res("ident")
        wam_sb = sb("wam_sb", [128, 2, 128], BF16)
        r_wam = Res("wam")
        esink = sb("esink", [128, 8])
        r_esink = Res("esink")
        gcols = sb("gcols", [128, 2])
        gbc = sb("gbc", [128, 2, 64])
        r_g = Res("g")

        block = es.enter_context(nc.Block())

        def program(_eng):
            def cast_up(l, f):
                for c in range(NFC):
                    for half in range(2):
                        col0 = half * DFF + c * 128
                        dma(pool, wupb[l, f, c][:, :, half * 128:(half + 1) * 128],
                            w_up[l, f][:, col0:col0 + 128].rearrange("(kc p) n -> p kc n", p=128),
                            pwrites=[r_wup[l][f]])

            def cast_dn(l, f):
                for c in range(NFC):
                    dma(pool, wdnb[l, f, c], w_dn[l, f][c * 128:(c + 1) * 128, :], pwrites=[r_wdn[l][f]])

            def cast_in(l):
                for kc in range(8):
                    for hh in range(2):
                        dma(pool, winb[l][:, kc, hh * 1152:(hh + 1) * 1152],
                            w_in[l][kc * 128:(kc + 1) * 128, hh * 1152:(hh + 1) * 1152], pwrites=[r_win[l]])

            def cast_out(l):
                for kc in range(8):
                    dma(pool, woutb[l][:, kc, :], w_out[l][kc * 128:(kc + 1) * 128, :], pwrites=[r_wout[l]])

            for l in range(DEPTH if btest is None else 0):
                cast_up(l, 0)
                cast_dn(l, 0)
                cast_in(l)
                cast_out(l)
                cast_up(l, 1)
                cast_dn(l, 1)

            dma(sp, identf[:], identf_d, writes=[r_ident])
            dma(sp, wam_sb[:], wam.rearrange("r k q -> k r q"), writes=[r_wam])

            scol = sb("scol", [128, 8, 2], BF16)
            r_scol = Res("scol")

            def modtab_chunks(l, CW, wst, wbf, bch, gch, och, r_wst, r_wbf, r_bch, r_gch, r_och, bankfn, extra_w=()):
                ns = len(wst)
                fns = []
                for cc in range(9 * D // CW):
                    def fn(cc=cc):
                        s = cc % ns
                        col0 = cc * CW
                        m, off = col0 // D, col0 % D
                        j, kind = m // 3, m % 3
                        dma(sp, wst[s], w_ada[l][:, col0:col0 + CW].rearrange("(kc p) n -> p kc n", p=128),
                            writes=[r_wst[s]] + list(extra_w))
                        dma(sp, bch[s], b_ada[l:l + 1, col0:col0 + CW].partition_broadcast(2), writes=[r_bch[s]])
                        if kind == 1:
                            dma(sp, gch[s], norm_g[l, j:j + 1, off:off + CW].partition_broadcast(2), writes=[r_gch[s]])
                        op(dve, lambda: nc.vector.tensor_copy(out=wbf[s], in_=wst[s]), reads=[r_wst[s]], writes=[r_wbf[s]] + list(extra_w))
                        bk = bankfn()

                        def mm():
                            for kc in range(8):
                                ins = nc.tensor.matmul(psum[0:2, bk, 0:CW], lhsT=scol[:, kc, :], rhs=wbf[s][:, kc, :],
                                                       start=(kc == 0), stop=(kc == 7))
                            return ins
                        op(pe, mm, reads=[r_scol, r_wbf[s]], writes=[bankres[bk]], pe_inorder=True)
                        P = psum[0:2, bk, 0:CW]
                        if kind == 1:
                            op(dve, lambda: nc.vector.scalar_tensor_tensor(out=och[s], in0=P, scalar=1.0, in1=bch[s], op0=ALU.add, op1=ALU.add),
                               reads=[bankres[bk], r_bch[s]], writes=[r_och[s]])
                            op(dve, lambda: nc.vector.tensor_tensor(out=och[s], in0=och[s], in1=gch[s], op=ALU.mult),
                               reads=[r_gch[s], r_och[s]], writes=[r_och[s]])
                        else:
                            op(dve, lambda: nc.vector.tensor_tensor(out=och[s], in0=P, in1=bch[s], op=ALU.add),
                               reads=[bankres[bk], r_bch[s]], writes=[r_och[s]])
                            if kind == 2 and j != 1:
                                op(dve, lambda: nc.vector.tensor_scalar(out=och[s], in0=och[s], scalar1=0.5, scalar2=None, op0=ALU.mult),
                                   reads=[r_och[s]], writes=[r_och[s]])
                        dma(sp, modtab[l, :, m, off:off + CW], och[s], reads=[r_och[s]], pwrites=[r_modtab[l]])
                    fns.append(fn)
                return fns

            def emit_modtab0():
                with ExitStack() as ms:
                    def msb(name, shape, dt=F32):
                        return ms.enter_context(nc.sbuf_tensor(name, list(shape), dt))
                    ccol = msb("ccol", [128, 8, 2])
                    r_ccol = Res("ccol")
                    wst = [msb("wst%d" % i, [128, 8, 512])[:] for i in range(2)]
                    wbf = [msb("wbf%d" % i, [128, 8, 512], BF16)[:] for i in range(2)]
                    bch = [msb("bch%d" % i, [2, 512])[:] for i in range(2)]
                    gch = [msb("gch%d" % i, [2, 512])[:] for i in range(2)]
                    och = [msb("och%d" % i, [2, 512])[:] for i in range(2)]
                    for w_ in range(2):
                        dma(sp, ccol[:, :, w_], cvec[w_, :].rearrange("(kc p) -> p kc", p=128), pwrites=[r_ccol],
                            allow_slow_non_contiguous=True)
                    op(act, lambda: nc.scalar.activation(out=scol[:], in_=ccol[:], func=AF.Silu),
                       reads=[r_ccol], writes=[r_scol])
                    cnt = [0]

                    def bankfn():
                        cnt[0] += 1
                        return cnt[0] % 8
                    R2 = lambda n: [Res(n + "0"), Res(n + "1")]
                    for fn in modtab_chunks(0, 512, wst, wbf, bch, gch, och, R2("wst"), R2("wbf"), R2("bch"), R2("gch"), R2("och"), bankfn):
                        fn()

            if btest is None:
                emit_modtab0()

            def load_layer_consts(l):
                for hh in range(2):
                    dma(sp, gcols[hh * 64:(hh + 1) * 64, 0:1], qkg[l, 0:1, :].rearrange("o d -> d o"), pwrites=[r_g],
                        allow_slow_non_contiguous=True)
                    dma(sp, gcols[hh * 64:(hh + 1) * 64, 1:2], qkg[l, 1:2, :].rearrange("o d -> d o"), pwrites=[r_g],
                        allow_slow_non_contiguous=True)
                dma(sp, gbc[:, 0, :], qkg[l, 2:3, :].partition_broadcast(128), pwrites=[r_g])
                dma(sp, gbc[:, 1, :], qkg[l, 3:4, :].partition_broadcast(128), pwrites=[r_g])
                op(dve, lambda: nc.vector.tensor_scalar(out=gcols[:, 0:1], in0=gcols[:, 0:1], scalar1=0.125, scalar2=None,
                                                        op0=ALU.mult), reads=[r_g], pwrites=[r_g])
                op(dve, lambda: nc.vector.tensor_scalar(out=gbc[:, 0, :], in0=gbc[:, 0, :], scalar1=0.125, scalar2=None,
                                                        op0=ALU.mult), reads=[r_g], pwrites=[r_g])
                dma(sp, esink[:], sinkp[l:l + 1, :].partition_broadcast(128), writes=[r_esink])

            def barrier():
                allp = {}
                for e in (pe, act, dve):
                    if e.prod.val:
                        allp[e.prod] = e.prod.val
                for sd in sp.dsem:
                    if sd.val:
                        allp[sd] = sd.val
                for e in (pe, act, dve, sp):
                    e.need(dict(allp))

            def pass_F(l):
                do_ffn2 = l >= 1
                do_ffn1 = l <= DEPTH - 1
                n_lat = NKV[l] if l <= DEPTH - 1 else NOUT[DEPTH - 1]
                groups = [("x", g) for g in lat_groups(n_lat)]
                if l <= DEPTH - 1:
                    groups.append(("c", [CT0, CT0 + 1]))
                par = l % 2
                barrier()
                with ExitStack() as fs:
                    def fsb(name, shape, dt=F32):
                        return fs.enter_context(nc.sbuf_tensor("%s_F%d" % (name, l), list(shape), dt))
                    NXB = 2
                    xg = [fsb("xg%d" % i, [128, GT, D]) for i in range(NXB)]
                    r_xg = [[Res("xg%d_%d" % (i, t)) for t in range(GT)] for i in range(NXB)]
                    hnT = fsb("hnT", [128, 8, GT * 128], BF16)
                    r_hnT = [Res("hnT%d" % t) for t in range(GT)]
                    hidT = fsb("hidT", [128, NFC, GT * 128], BF16)
                    r_hid = Res("hidT")
                    NUP = 3
                    wup = [fsb("wup%d" % i, [128, 8, 256], BF16) for i in range(NUP)]
                    r_wups = [Res("wups%d" % i) for i in range(NUP)]
                    wdn = fsb("wdn", [128, NFC, D], BF16)
                    r_wdns = Res("wdns")
                    NWI = 2
                    win = [fsb("win%d" % i, [128, 8, 512], BF16) for i in range(NWI)]
                    r_wins = [Res("wins%d" % i) for i in range(NWI)]
                    xn = [fsb("xn%d" % i, [128, D]) for i in range(2)]
                    r_xn = [Res("xn0"), Res("xn1")]
                    mt = [fsb("mt%d" % i, [128, 512]) for i in range(2)]
                    r_mt = [Res("mt0"), Res("mt1")]
                    mtc = [0]
                    stat = fsb("stat", [128, 64])
                    r_stat = Res("stat")
                    sg = [fsb("sg%d" % i, [128, 512], BF16) for i in range(2)]
                    r_sg = [Res("sg0"), Res("sg1")]
                    tmpy = [fsb("tmpy%d" % i, [128, 512]) for i in range(2)]
                    r_tmpy = [Res("tmpy0"), Res("tmpy1")]
                    mcols = fsb("mcols", [128, 2, 6, 8])
                    mgate = fsb("mgate", [128, 2, D])
                    r_mod = Res("mod")
                    r_mgate = Res("mgate")
                    sq = [fsb("sq%d" % i, [128, 512]) for i in range(1)]
                    r_sq = [Res("sq0")]
                    NQN = 3
                    qn = [fsb("qn%d" % i, [128, 512]) for i in range(NQN)]
                    r_qn = [Res("qn%d" % i) for i in range(NQN)]
                    qg = fsb("qg", [128, 512])
                    r_qg = Res("qg")
                    t1 = fsb("t1", [128, 512])
                    r_t1 = Res("t1")
                    tu = fsb("tu", [128, 256])
                    r_tu = Res("tu")
                    NQR = 3
                    qrs = [fsb("qr%d" % i, [128, 512]) for i in range(NQR)]
                    r_qrs = [Res("qr%d" % i) for i in range(NQR)]
                    qrc = [0]
                    kdups = [fsb("kdup%d" % i, [128, 2, 128]) for i in range(NQR)]
                    r_kdups = [Res("kdup%d" % i) for i in range(NQR)]
                    kdc = [0]
                    ssq = fsb("ssq", [128, 32])
                    r_ssq = Res("ssq")
                    rsq = fsb("rsq", [128, 32])
                    r_rsq = Res("rsq")
                    NST = GT
                    qTs = [fsb("qTs%d" % i, [128, 8, 128], BF16) for i in range(NST)]
                    kTs = [fsb("kTs%d" % i, [128, 6, 128], BF16) for i in range(NST)]
                    vs = [fsb("vs%d" % i, [128, 12, 128], BF16) for i in range(NST)]
                    r_qTs = [Res("qTs%d" % i) for i in range(NST)]
                    r_kTs = [Res("kTs%d" % i) for i in range(NST)]
                    r_vs = [Res("vs%d" % i) for i in range(NST)]
                    ropet = fsb("ropet", [128, GT, 128])
                    r_rope = Res("rope")

                    for i in range(NST):
                        op(dve, lambda: nc.vector.memset(vs[i][:], 1.0), writes=[r_vs[i]])

                    for w in range(2):
                        specs = []
                        if do_ffn2:
                            specs.append((0, l - 1, 2))
                        if do_ffn1:
                            specs.append((1, l, 0))
                            specs.append((2, l, 1))
                        for slot, ll, j in specs:
                            dma(sp, mcols[:, w, 2 * slot, :], modtab[ll, w, 3 * j + 1, :].rearrange("(kc p) -> p kc", p=128),
                                reads=[r_modtab[ll]], pwrites=[r_mod], allow_slow_non_contiguous=True)
                            dma(sp, mcols[:, w, 2 * slot + 1, :], modtab[ll, w, 3 * j, :].rearrange("(kc p) -> p kc", p=128),
                                reads=[r_modtab[ll]], pwrites=[r_mod], allow_slow_non_contiguous=True)

                    def load_mgate(w):
                        if do_ffn2:
                            dma(sp, mgate[:, 0, :], modtab[l - 1, w, 8:9, :].partition_broadcast(128),
                                reads=[r_modtab[l - 1]], pwrites=[r_mgate])
                        if do_ffn1:
                            dma(sp, mgate[:, 1, :], modtab[l, w, 2:3, :].partition_broadcast(128),
                                reads=[r_modtab[l]], pwrites=[r_mgate])
                    load_mgate(0)
                    if do_ffn1:
                        load_layer_consts(l)

                    ringpos = [0]

                    def nextbank():
                        b = ringpos[0] % 8
                        ringpos[0] += 1
                        return b

                    upc = [0]
                    winc = [0]
                    stc = [0]
                    xnc = [0]
                    sgc = [0]
                    tyc = [0]
                    sqc = [0]
                    qnc = [0]

                    def norm_transpose(gi, tl, nt, w, slot):
                        xb = gi % NXB
                        xt = xg[xb][:, tl, :]
                        rx = r_xg[xb][tl]
                        c0 = tl * 2
                        op(act, lambda: nc.scalar.activation(out=sq[0][:].bitcast(BF16), in_=xt, func=AF.Square, accum_out=stat[:, c0:c0 + 1]),
                           reads=[rx], writes=[r_sq[0]], pwrites=[r_stat])
                        op(act, lambda: nc.scalar.activation(out=stat[:, c0 + 1:c0 + 2], in_=stat[:, c0:c0 + 1], func=AF.Ln,
                                                             scale=1.0 / D, bias=EPS),
                           reads=[r_stat], pwrites=[r_stat])
                        op(act, lambda: nc.scalar.activation(out=stat[:, c0 + 1:c0 + 2], in_=stat[:, c0 + 1:c0 + 2], func=AF.Exp,
                                                             scale=-0.5),
                           reads=[r_stat], pwrites=[r_stat])
                        xs = xnc[0] % 2
                        xnc[0] += 1
                        op(dve, lambda: nc.vector.tensor_scalar(out=xn[xs][:], in0=xt, scalar1=stat[:, c0 + 1:c0 + 2], scalar2=None,
                                                                op0=ALU.mult),
                           reads=[rx, r_stat], writes=[r_xn[xs]])
                        for hb in range(2):
                            bk = nextbank()

                            def tr():
                                for i in range(4):
                                    kc = hb * 4 + i
                                    ins = nc.tensor.transpose(psum[:, bk, i * 128:(i + 1) * 128], xn[xs][:, kc * 128:(kc + 1) * 128], identf[:])
                                return ins
                            op(pe, tr, reads=[r_xn[xs], r_ident], writes=[bankres[bk]], pe_inorder=True)
                            pv = psum[:, bk, :].rearrange("p (k t) -> p k t", k=4)
                            ov = hnT[:, hb * 4:(hb + 1) * 4, tl * 128:(tl + 1) * 128]
                            gcol = mcols[:, w, 2 * slot, hb * 4:(hb + 1) * 4].unsqueeze(2).to_broadcast([128, 4, 128])
                            scolb = mcols[:, w, 2 * slot + 1, hb * 4:(hb + 1) * 4].unsqueeze(2).to_broadcast([128, 4, 128])
                            mi = mtc[0] % 2
                            mtc[0] += 1
                            xv = mt[mi][:].rearrange("p (k t) -> p k t", k=4)
                            op(dve, lambda: nc.vector.tensor_tensor(out=xv, in0=pv, in1=gcol, op=ALU.mult),
                               reads=[bankres[bk], r_mod], writes=[r_mt[mi]])
                            op(dve, lambda: nc.vector.tensor_tensor(out=ov, in0=xv, in1=scolb, op=ALU.add),
                               reads=[r_mt[mi], r_mod], pwrites=[r_hnT[tl]])

                    up_pref = {}

                    def load_up(ll, f, c):
                        s = upc[0] % NUP
                        upc[0] += 1
                        dma(sp, wup[s][:], wupb[ll, f, c], reads=[r_wup[ll][f]], writes=[r_wups[s]])
                        return s

                    def ffn(gi, tiles, w, slot, gslot, ll, f, nxt=None, after_tile=None):
                        nt = len(tiles)
                        ntok = nt * 128
                        xb = gi % NXB
                        pre = up_pref.pop((ll, f), None)
                        if pre is None:
                            pre = [load_up(ll, f, c) for c in range(min(NUP - 1, NFC))]
                        slots = list(pre)
                        for fc in range(NFC):
                            if fc + NUP - 1 < NFC:
                                slots.append(load_up(ll, f, fc + NUP - 1))
                            if fc < 11:
                                c0, c1 = 2 * fc, 2 * fc + 2
                                dma(sp, wdn[:, c0:c1, :], wdnb[ll, f, c0:c1].rearrange("c p n -> p c n"),
                                    reads=[r_wdn[ll][f]], pwrites=[r_wdns])
                            s = slots[fc]
                            bg, bu = nextbank(), nextbank()

                            def mmg():
                                for kc in range(8):
                                    ins = nc.tensor.matmul(psum[:, bg, 0:ntok], lhsT=wup[s][:, kc, 0:128], rhs=hnT[:, kc, 0:ntok],
                                                           start=(kc == 0), stop=(kc == 7))
                                return ins

                            def mmu():
                                for kc in range(8):
                                    ins = nc.tensor.matmul(psum[:, bu, 0:ntok], lhsT=wup[s][:, kc, 128:256], rhs=hnT[:, kc, 0:ntok],
                                                           start=(kc == 0), stop=(kc == 7))
                                return ins
                            op(pe, mmg, reads=[r_wups[s]] + r_hnT[:nt], writes=[bankres[bg]], pe_inorder=True)
                            op(pe, mmu, reads=[r_wups[s]] + r_hnT[:nt], writes=[bankres[bu]], pe_inorder=True)
                            si = sgc[0] % 2
                            sgc[0] += 1
                            op(act, lambda: nc.scalar.activation(out=sg[si][:, 0:ntok], in_=psum[:, bg, 0:ntok], func=AF.Silu),
                               reads=[bankres[bg]], writes=[r_sg[si]])
                            op(dve, lambda: nc.vector.tensor_tensor(out=hidT[:, fc, 0:ntok], in0=psum[:, bu, 0:ntok], in1=sg[si][:, 0:ntok],
                                                                    op=ALU.mult),
                               reads=[bankres[bu], r_sg[si]], pwrites=[r_hid])
                        if nxt is not None:
                            up_pref[nxt] = [load_up(nxt[0], nxt[1], c) for c in range(min(NUP - 1, NFC))]
                        for tl in range(nt):
                            for nh in range(2):
                                bk = nextbank()

                                def mmd():
                                    for fc in range(NFC):
                                        ins = nc.tensor.matmul(psum[:, bk, :], lhsT=hidT[:, fc, tl * 128:(tl + 1) * 128],
                                                               rhs=wdn[:, fc, nh * 512:(nh + 1) * 512],
                                                               start=(fc == 0), stop=(fc == NFC - 1))
                                    return ins
                                op(pe, mmd, reads=[r_hid, r_wdns], writes=[bankres[bk]], pe_inorder=True)
                                ti = tyc[0] % 2
                                tyc[0] += 1
                                op(dve, lambda: nc.vector.tensor_tensor(out=tmpy[ti][:], in0=psum[:, bk, :],
                                                                        in1=mgate[:, gslot, nh * 512:(nh + 1) * 512], op=ALU.mult),
                                   reads=[bankres[bk], r_mgate], writes=[r_tmpy[ti]])
                                xsl = xg[xb][:, tl, nh * 512:(nh + 1) * 512]
                                op(dve, lambda: nc.vector.tensor_tensor(out=xsl, in0=xsl, in1=tmpy[ti][:], op=ALU.add),
                                   reads=[r_tmpy[ti]], pwrites=[r_xg[xb][tl]])
                            if after_tile is not None and tl >= 1:
                                after_tile(tl - 1)
                        if after_tile is not None:
                            after_tile(nt - 1)

                    def inproj(gi, tiles, w):
                        nt = len(tiles)
                        xb = gi % NXB
                        if w == 0:
                            r0 = tiles[0] * 128
                            dma(sp, ropet[:, 0:nt, :], rope[r0:r0 + nt * 128, :].rearrange("(t p) c -> p t c", p=128), writes=[r_rope])
                        import collections
                        tails = collections.deque()
                        LAGF = 2
                        sts = []
                        for tl in range(nt):
                            sti = stc[0] % NST
                            stc[0] += 1
                            sts.append(sti)
                        for cc in range(5):
                            ncol = 512 if cc < 4 else 256
                            wsl = winc[0] % NWI
                            winc[0] += 1
                            dma(sp, win[wsl][:, :, 0:ncol], winb[l][:, :, cc * 512:cc * 512 + ncol], reads=[r_win[l]], writes=[r_wins[wsl]])
                            for tl in range(nt):
                                sti = sts[tl]
                                bk = nextbank()

                                def mmi():
                                    for kc in range(8):
                                        ins = nc.tensor.matmul(psum[:, bk, 0:ncol], lhsT=hnT[:, kc, tl * 128:(tl + 1) * 128],
                                                               rhs=win[wsl][:, kc, 0:ncol], start=(kc == 0), stop=(kc == 7))
                                    return ins
                                op(pe, mmi, reads=[r_hnT[tl], r_wins[wsl]], writes=[bankres[bk]], pe_inorder=True)
                                P = psum[:, bk, :]
                                if cc == 3:
                                    Pv = P.rearrange("p (i two d) -> p i two d", i=4, two=2)
                                    vv = vs[sti][:, 0:8, :].rearrange("p (i two) d -> p i two d", two=2)
                                    op(act, lambda: nc.scalar.activation(out=vv[:, :, 0, 0:64], in_=Pv[:, :, 0, :], func=AF.Copy),
                                       reads=[bankres[bk]], pwrites=[r_vs[sti]])
                                    op(act, lambda: nc.scalar.activation(out=vv[:, :, 1, 64:128], in_=Pv[:, :, 1, :], func=AF.Copy),
                                       reads=[bankres[bk]], pwrites=[r_vs[sti]])
                                    continue
                                nh = 8 if cc < 4 else 2
                                ncn = nh * 64
                                so = cc * 8 if cc < 3 else 24
                                if cc == 4:
                                    op(act, lambda: nc.scalar.activation(out=vs[sti][:, 8:10, 0:64], in_=P[:, 128:256].rearrange("p (h d) -> p h d", h=2), func=AF.Copy),
                                       reads=[bankres[bk]], pwrites=[r_vs[sti]])
                                    op(act, lambda: nc.scalar.activation(out=vs[sti][:, 10:12, 64:128], in_=P[:, 128:256].rearrange("p (h d) -> p h d", h=2), func=AF.Copy),
                                       reads=[bankres[bk]], pwrites=[r_vs[sti]])
                                sqi = 0
                                sqc[0] += 1
                                op(act, lambda: nc.scalar.activation(out=sq[sqi][:, 0:ncn], in_=P[:, 0:ncn], func=AF.Square),
                                   reads=[bankres[bk]], writes=[r_sq[sqi]])
                                op(dve, lambda: nc.vector.tensor_reduce(out=ssq[:, so:so + nh], in_=sq[sqi][:, 0:ncn].rearrange("p (h d) -> p h d", h=nh),
                                                                        axis=AX.X, op=ALU.add),
                                   reads=[r_sq[sqi]], pwrites=[r_ssq])
                                op(act, lambda: nc.scalar.activation(out=rsq[:, so:so + nh], in_=ssq[:, so:so + nh], func=AF.Ln, scale=1.0 / 64, bias=EPS),
                                   reads=[r_ssq], pwrites=[r_rsq])
                                op(act, lambda: nc.scalar.activation(out=rsq[:, so:so + nh], in_=rsq[:, so:so + nh], func=AF.Exp, scale=-0.5),
                                   reads=[r_rsq], pwrites=[r_rsq])
                                qi = qnc[0] % NQN
                                qnc[0] += 1
                                op(dve, lambda: nc.vector.tensor_tensor(out=qn[qi][:, 0:ncn].rearrange("p (h d) -> p h d", h=nh),
                                                                        in0=P[:, 0:ncn].rearrange("p (h d) -> p h d", h=nh),
                                                                        in1=rsq[:, so:so + nh].unsqueeze(2).to_broadcast([128, nh, 64]), op=ALU.mult),
                                   reads=[bankres[bk], r_rsq], writes=[r_qn[qi]])
                                if cc in (0, 2):
                                    def tail_a(qi=qi, sti=sti, cc=cc):
                                        b2 = nextbank()

                                        def tr():
                                            for i in range(4):
                                                ins = nc.tensor.transpose(psum[:, b2, i * 128:(i + 1) * 128], qn[qi][:, i * 128:(i + 1) * 128], identf[:])
                                            return ins
                                        op(pe, tr, reads=[r_qn[qi], r_ident], writes=[bankres[b2]], pe_inorder=True)
                                        if cc == 0:
                                            op(act, lambda: nc.scalar.activation(out=qTs[sti][:, 0:4, :], in_=psum[:, b2, :].rearrange("p (k t) -> p k t", k=4),
                                                                                 func=AF.Identity, scale=gcols[:, 0:1]),
                                               reads=[bankres[b2], r_g], pwrites=[r_qTs[sti]])
                                        else:
                                            op(act, lambda: nc.scalar.activation(out=kTs[sti][:, 0:4, :], in_=psum[:, b2, :].rearrange("p (k t) -> p k t", k=4),
                                                                                 func=AF.Identity, scale=gcols[:, 1:2]),
                                               reads=[bankres[b2], r_g], pwrites=[r_kTs[sti]])
                                    tails.append(tail_a)
                                    while len(tails) > LAGF:
                                        tails.popleft()()
                                    continue
                                gi_ = 0 if cc == 1 else 1
                                qri = qrc[0] % NQR
                                qrc[0] += 1
                                qr = qrs[qri]
                                r_qr = r_qrs[qri]
                                if w == 0:
                                    gout, r_gout = qg, r_qg
                                else:
                                    gout, r_gout = qr, r_qr
                                qgv = gout[:, 0:ncn].rearrange("p (h d) -> p h d", h=nh)
                                op(dve, lambda: nc.vector.tensor_tensor(out=qgv, in0=qn[qi][:, 0:ncn].rearrange("p (h d) -> p h d", h=nh),
                                                                        in1=gbc[:, gi_, :].unsqueeze(1).to_broadcast([128, nh, 64]), op=ALU.mult),
                                   reads=[r_qn[qi], r_g], writes=[r_gout])
                                if w == 0:
                                    cosb = ropet[:, tl, 0:64].unsqueeze(1).to_broadcast([128, nh, 64])
                                    op(dve, lambda: nc.vector.tensor_tensor(out=t1[:, 0:ncn].rearrange("p (h d) -> p h d", h=nh), in0=qgv, in1=cosb, op=ALU.mult),
                                       reads=[r_qg, r_rope], writes=[r_t1])
                                    q5 = qg[:, 0:ncn].rearrange("p (h a t f) -> p h a t f", h=nh, a=2, t=2, f=16)
                                    t5 = t1[:, 0:ncn].rearrange("p (h a t f) -> p h a t f", h=nh, a=2, t=2, f=16)
                                    r5 = qr[:, 0:ncn].rearrange("p (h a t f) -> p h a t f", h=nh, a=2, t=2, f=16)
                                    u4 = tu[:, 0:ncn // 2].rearrange("p (h a f) -> p h a f", h=nh, a=2, f=16)
                                    for half in range(2):
                                        sinb = ropet[:, tl, 64 + 32 * half:96 + 32 * half].rearrange("p (a f) -> p a f", a=2).unsqueeze(1).to_broadcast([128, nh, 2, 16])
                                        op(dve, lambda: nc.vector.tensor_tensor(out=u4, in0=q5[:, :, :, 1 - half, :], in1=sinb, op=ALU.mult),
                                           reads=[r_qg, r_rope], writes=[r_tu])
                                        op(dve, lambda: nc.vector.tensor_tensor(out=r5[:, :, :, half, :], in0=t5[:, :, :, half, :], in1=u4, op=ALU.add),
                                           reads=[r_t1, r_tu], pwrites=[r_qr])
                                src = qr
                                r_src = r_qr
                                if cc == 1:
                                    def tail_b(src=src, r_src=r_src, sti=sti):
                                        b2 = nextbank()

                                        def tr():
                                            for i in range(4):
                                                ins = nc.tensor.transpose(psum[:, b2, i * 128:(i + 1) * 128], src[:, i * 128:(i + 1) * 128], identf[:])
                                            return ins
                                        op(pe, tr, reads=[r_src, r_ident], writes=[bankres[b2]], pe_inorder=True)
                                        op(act, lambda: nc.scalar.activation(out=qTs[sti][:, 4:8, :], in_=psum[:, b2, :].rearrange("p (k t) -> p k t", k=4), func=AF.Copy),
                                           reads=[bankres[b2]], pwrites=[r_qTs[sti]])
                                    tails.append(tail_b)
                                else:
                                    kdi = kdc[0] % NQR
                                    kdc[0] += 1
                                    kdup = kdups[kdi]
                                    r_kdup = r_kdups[kdi]
                                    for dd in range(2):
                                        op(dve, lambda: nc.vector.tensor_copy(out=kdup[:, :, dd * 64:(dd + 1) * 64], in_=src[:, 0:128].rearrange("p (h d) -> p h d", h=2)),
                                           reads=[r_src], pwrites=[r_kdup])

                                    def tail_k(kdup=kdup, r_kdup=r_kdup, sti=sti):
                                        b2 = nextbank()

                                        def tr():
                                            for i in range(2):
                                                ins = nc.tensor.transpose(psum[:, b2, i * 128:(i + 1) * 128], kdup[:, i, :], identf[:])
                                            return ins
                                        op(pe, tr, reads=[r_kdup, r_ident], writes=[bankres[b2]], pe_inorder=True)
                                        op(act, lambda: nc.scalar.activation(out=kTs[sti][:, 4:6, :], in_=psum[:, b2, 0:256].rearrange("p (k t) -> p k t", k=2), func=AF.Copy),
                                           reads=[bankres[b2]], pwrites=[r_kTs[sti]])
                                    tails.append(tail_k)
                                while len(tails) > LAGF:
                                    tails.popleft()()
                        while tails:
                            tails.popleft()()
                        for tl in range(nt):
                            sti = sts[tl]
                            tg = tiles[tl]
                            dma(sp, qT_d[par, tg], qTs[sti][:], reads=[r_qTs[sti]], pwrites=[r_qkv[par]])
                            dma(sp, kT_d[par, tg], kTs[sti][:], reads=[r_kTs[sti]], pwrites=[r_qkv[par]])
                            dma(sp, v_d[par, tg], vs[sti][:], reads=[r_vs[sti]], pwrites=[r_qkv[par]])

                    def load_x(gi, typ, tiles):
                        xb = gi % NXB
                        nt = len(tiles)
                        if l == 0:
                            src = xin if typ == "x" else ctxin
                            r0 = tiles[0] * 128 if typ == "x" else 0
                            dma(sp, xg[xb][:, 0:nt, :], src[r0:r0 + nt * 128, :].rearrange("(t p) d -> p t d", p=128), writes=r_xg[xb][:nt])
                        else:
                            r0 = tiles[0] * 128
                            dma(sp, xg[xb][:, 0:nt, :], xres[r0:r0 + nt * 128, :].rearrange("(t p) d -> p t d", p=128),
                                reads=[r_xres], writes=r_xg[xb][:nt])

                    def stages_of(typ):
                        st = []
                        if do_ffn2 and not (typ == "c" and l - 1 >= DEPTH - 1):
                            st.append(("ffn", 0, 0, l - 1, 1))
                        if do_ffn1:
                            st.append(("ffn", 1, 1, l, 0))
                            st.append(("inproj", 2))
                        return st
                    ffn_calls = []
                    for gi_, (typ_, _t) in enumerate(groups):
                        for si__, st_ in enumerate(stages_of(typ_)):
                            if st_[0] == "ffn":
                                ffn_calls.append((gi_, si__, (st_[3], st_[4])))
                    nxt_spec = {}
                    for i_ in range(len(ffn_calls) - 1):
                        nxt_spec[(ffn_calls[i_][0], ffn_calls[i_][1])] = ffn_calls[i_ + 1][2]

                    load_x(0, groups[0][0], groups[0][1])
                    for gi, (typ, tiles) in enumerate(groups):
                        nt = len(tiles)
                        w = 0 if typ == "x" else 1
                        xb = gi % NXB
                        if w == 1:
                            load_mgate(1)
                        if gi + 1 < len(groups):
                            load_x(gi + 1, groups[gi + 1][0], groups[gi + 1][1])
                        stages = stages_of(typ)
                        for tl in range(nt):
                            norm_transpose(gi, tl, nt, w, stages[0][1])
                        for si_, st in enumerate(stages):
                            if st[0] == "ffn":
                                nxt = stages[si_ + 1] if si_ + 1 < len(stages) else None
                                at = None
                                if nxt is not None:
                                    at = (lambda tl_, slot_=nxt[1]: norm_transpose(gi, tl_, nt, w, slot_))
                                ffn(gi, tiles, w, st[1], st[2], st[3], st[4], nxt=nxt_spec.get((gi, si_)), after_tile=at)
                            else:
                                inproj(gi, tiles, w)
                        if l <= DEPTH - 1:
                            r0 = tiles[0] * 128
                            dma(sp, xres[r0:r0 + nt * 128, :].rearrange("(t p) d -> p t d", p=128), xg[xb][:, 0:nt, :],
                                reads=r_xg[xb][:nt], pwrites=[r_xres])
                        else:
                            r0 = tiles[0] * 128
                            dma(sp, yout[r0:r0 + nt * 128, :].rearrange("(t p) d -> p t d", p=128), xg[xb][:, 0:nt, :],
                                reads=r_xg[xb][:nt], pwrites=[r_yout])

            r_yout = Res("yout")

            def pass_B(l):
                par = l % 2
                last = l == DEPTH - 1
                groups = [("x", g) for g in lat_groups(NOUT[l])]
                if not last:
                    groups.append(("c", [CT0, CT0 + 1]))
                nkv = NKV[l]
                barrier()
                with ExitStack() as bs:
                    def bsb(name, shape, dt=F32):
                        return bs.enter_context(nc.sbuf_tensor("%s_B%d" % (name, l), list(shape), dt))
                    xgs = [bsb("bxg%d" % i, [128, GT, D]) for i in range(2)]
                    r_xgs = [[Res("bxg%d_%d" % (i, t)) for t in range(GT)] for i in range(2)]
                    QTs = [bsb("QT%d" % i, [128, GT, 8, 128], BF16) for i in range(2)]
                    r_QTs = [Res("QT0"), Res("QT1")]
                    KTR = 12
                    KT = bsb("KT", [128, KTR, 6, 128], BF16)
                    r_KTs = [Res("KT%d" % i) for i in range(KTR)]
                    VW = bsb("VW", [128, KTR, 12, 128], BF16)
                    r_VWs = [Res("VW%d" % i) for i in range(KTR)]
                    KTc = bsb("KTc", [128, 2, 6, 128], BF16)
                    VWc = bsb("VWc", [128, 2, 12, 128], BF16)
                    r_ctxkv = Res("ctxkv")
                    EB = bsb("EB", [128, 13 * 8, 128], BF16)
                    r_EB = Res("EB")
                    ebflat = bsb("ebflat", [128, 2 * 13 * 128])
                    ebst = [ebflat[:, i * 1664:(i + 1) * 1664].rearrange("p (s q) -> p s q", s=13) for i in range(2)]
                    r_ebst = [Res("ebst0"), Res("ebst1")]
                    mt_fns = []
                    if l + 1 < DEPTH and btest is None:
                        m_wst = [ebflat[:, 0:2048].rearrange("p (k n) -> p k n", k=8)]
                        m_wbf = [ebflat[:, 2048:3072].bitcast(BF16).rearrange("p (k n) -> p k n", k=8)]
                        m_b = bsb("m_b", [2, 3, 256])
                        mt_fns = modtab_chunks(l + 1, 256, m_wst, m_wbf, [m_b[:, 0, :]], [m_b[:, 1, :]], [m_b[:, 2, :]],
                                               [Res("mwst")], [Res("mwbf")], [Res("mbch")], [Res("mgch")], [Res("moch")],
                                               lambda: alloc_one(), extra_w=r_ebst)
                    wo = bsb("wo", [128, 8, D], BF16)
                    r_wo = Res("wo")
                    gate = bsb("bgate", [128, D])
                    r_gate = Res("bgate")
                    PTn = [bsb("PTn%d" % i, [128, 7 * 128], BF16) for i in range(3)]
                    r_PTn = [Res("PTn%d" % i) for i in range(3)]
                    NPTW = 10
                    PTw = [bsb("PTw%d" % i, [128, 512], BF16) for i in range(NPTW)]
                    r_PTw = [Res("PTw%d" % i) for i in range(NPTW)]
                    aT = bsb("aT", [128, GT, 8, 128], BF16)
                    r_aT = [Res("aT%d" % t) for t in range(GT)]
                    rden = [bsb("rden%d" % i, [128, 1024]) for i in range(2)]
                    r_rden = [Res("rden0"), Res("rden1")]
                    esk = bsb("esk", [128, 8])
                    r_esk = Res("esk")

                    def load_wo(w):
                        dma(sp, wo[:], woutb[l], reads=[r_wout[l]], writes=[r_wo])
                        dma(sp, gate[:], modtab[l, w, 5:6, :].partition_broadcast(128), reads=[r_modtab[l]], writes=[r_gate])
                        op(dve, lambda: nc.vector.tensor_tensor(out=wo[:], in0=wo[:], in1=gate[:].unsqueeze(1).to_broadcast([128, 8, D]), op=ALU.mult),
                           reads=[r_gate], writes=[r_wo])
                    load_wo(0)
                    dma(sp, KTc[:], kT_d[par, CT0:CT0 + 2].rearrange("t p k q -> p t k q"), reads=[r_qkv[par]], pwrites=[r_ctxkv])
                    dma(sp, VWc[:], v_d[par, CT0:CT0 + 2].rearrange("t p h d -> p t h d"), reads=[r_qkv[par]], pwrites=[r_ctxkv])
                    op(act, lambda: nc.scalar.activation(out=esk[:], in_=esink[:], func=AF.Exp), reads=[r_esink], writes=[r_esk])
                    for h in range(8):
                        s = h % 2
                        dma(sp, ebst[s], nab[l, :, h].rearrange("s k q -> k s q"), writes=[r_ebst[s]])
                        op(act, lambda: nc.scalar.activation(out=EB[:].rearrange("p (s h) q -> p s h q", h=8)[:, :, h, :], in_=ebst[s], func=AF.Exp),
                           reads=[r_ebst[s]], pwrites=[r_EB])
                    EBv = EB[:].rearrange("p (s h) q -> p s h q", h=8)

                    sring = [0]
                    ptn_c = [0]
                    ptw_c = [0]
                    rd_c = [0]
                    ty_c = [0]
                    odw_c = [0]

                    ring1 = [0]

                    def alloc_pair():
                        if ring1[0] % 2 == 1:
                            ring1[0] += 1
                        b = ring1[0] % 4
                        ring1[0] += 2
                        return b

                    def alloc_one():
                        b = ring1[0] % 4
                        ring1[0] += 1
                        return b

                    import collections
                    defer = collections.deque()
                    LAGB = 1

                    def sched(s1, s2):
                        if s1 is not None:
                            s1()
                        if s2 is not None:
                            defer.append(s2)
                        while len(defer) > LAGB:
                            defer.popleft()()

                    def flush():
                        while defer:
                            defer.popleft()()

                    def attn_tile2(typ, tl, tg, gpar):
                        QT = QTs[gpar]
                        r_QT = r_QTs[gpar]
                        if typ == "x":
                            if tg == 0:
                                na_units = [("l", (tg + r) % KTR, 5 + r) for r in range(0, 4)]
                            elif tg == 1:
                                na_units = [("l", (tg + r) % KTR, 9 + (r + 1)) for r in range(-1, 3)]
                            else:
                                na_units = [("l", (tg + r) % KTR, r + 2) for r in range(-2, 3)]
                            na_units += [("c", 0, None), ("c", 1, None)]
                            wa_units = []
                            for r in (-1, 0, 1):
                                if 0 <= tg + r < nkv:
                                    wa_units.append(("l", (tg + r) % KTR, r))
                            wa_units += [("c", 0, None), ("c", 1, None)]
                        else:
                            na_units = [("c", 0, None), ("c", 1, None)]
                            wa_units = [("c", 0, None), ("c", 1, None)]

                        def kt_ap(u, pi, base):
                            src = KT if u[0] == "l" else KTc
                            return src[base:base + 64, u[1], pi, :]

                        def v_ap(u, hidx):
                            src = VW if u[0] == "l" else VWc
                            return src[:, u[1], hidx, :]
                        nu = len(na_units)
                        nl = sum(1 for u in na_units if u[0] == "l")
                        rk_na = [r_KTs[u[1]] for u in na_units if u[0] == "l"] + [r_ctxkv]
                        rv_na = [r_VWs[u[1]] for u in na_units if u[0] == "l"] + [r_ctxkv]
                        rk_wa = [r_KTs[u[1]] for u in wa_units if u[0] == "l"] + [r_ctxkv]
                        rv_wa = [r_VWs[u[1]] for u in wa_units if u[0] == "l"] + [r_ctxkv]

                        def na_item(h):
                            pi, base = h // 2, 64 * (h % 2)
                            st = {}

                            def s1():
                                b0 = alloc_pair()
                                S = psum[:, b0:b0 + 2, :].rearrange("p b n -> p (b n)")

                                def mms():
                                    for ui, u in enumerate(na_units):
                                        ins = nc.tensor.matmul(S[:, ui * 128:(ui + 1) * 128], lhsT=kt_ap(u, pi, base),
                                                               rhs=QT[base:base + 64, tl, pi, :], start=True, stop=True)
                                    return ins
                                op(pe, mms, reads=rk_na + [r_QT], writes=[bankres[b0], bankres[b0 + 1]], pe_inorder=True)
                                pn = ptn_c[0] % 3
                                ptn_c[0] += 1
                                st["pn"] = pn
                                op(act, lambda: nc.scalar.activation(out=PTn[pn][:, 0:nu * 128], in_=S[:, 0:nu * 128], func=AF.Exp),
                                   reads=[bankres[b0], bankres[b0 + 1]], writes=[r_PTn[pn]])
                                if nl > 0:
                                    e0 = na_units[0][2]
                                    pv = PTn[pn][:, 0:nl * 128].rearrange("p (u q) -> p u q", u=nl)
                                    op(dve, lambda: nc.vector.tensor_tensor(out=pv, in0=pv, in1=EBv[:, e0:e0 + nl, h, :], op=ALU.mult),
                                       reads=[r_EB], pwrites=[r_PTn[pn]])

                            def s2():
                                pn = st["pn"]
                                ob = 4 + h // 4

                                def mmo():
                                    for ui, u in enumerate(na_units):
                                        ins = nc.tensor.matmul(psum[:, ob, (h % 4) * 128:(h % 4 + 1) * 128], lhsT=v_ap(u, h),
                                                               rhs=PTn[pn][:, ui * 128:(ui + 1) * 128], start=(ui == 0), stop=(ui == nu - 1))
                                    return ins
                                if h % 4 == 0:
                                    op(pe, mmo, reads=[r_PTn[pn]] + rv_na, writes=[bankres[ob]], pe_inorder=True)
                                else:
                                    op(pe, mmo, reads=[r_PTn[pn]] + rv_na, pwrites=[bankres[ob]], pe_inorder=True)
                                if h == 7:
                                    ri = rd_c[0] % 2
                                    rd_c[0] += 1
                                    od = psum[:, 4:6, :].rearrange("p b n -> p (b n)")
                                    odv = od.rearrange("p (i two q) -> p i two q", i=4, two=2)
                                    rdv = rden[ri][:, :].rearrange("p (i two q) -> p i two q", i=4, two=2)
                                    op(act, lambda: nc.scalar.activation(out=rdv[64:128, :, 0, :], in_=odv[64:128, :, 0, :], func=AF.Ln),
                                       reads=[bankres[4], bankres[5]], writes=[r_rden[ri]])
                                    op(act, lambda: nc.scalar.activation(out=rdv[0:64, :, 1, :], in_=odv[0:64, :, 1, :], func=AF.Ln),
                                       reads=[bankres[4], bankres[5]], pwrites=[r_rden[ri]])
                                    op(act, lambda: nc.scalar.activation(out=rdv[64:128, :, 0, :], in_=rdv[64:128, :, 0, :], func=AF.Exp, scale=-1.0),
                                       reads=[r_rden[ri]], pwrites=[r_rden[ri]])
                                    op(act, lambda: nc.scalar.activation(out=rdv[0:64, :, 1, :], in_=rdv[0:64, :, 1, :], func=AF.Exp, scale=-1.0),
                                       reads=[r_rden[ri]], pwrites=[r_rden[ri]])
                                    op(dve, lambda: nc.vector.tensor_tensor(out=aT[0:64, tl, 0:4, :], in0=odv[0:64, :, 0, :], in1=rdv[64:128, :, 0, :], op=ALU.mult),
                                       reads=[bankres[4], bankres[5], r_rden[ri]], pwrites=[r_aT[tl]])
                                    op(dve, lambda: nc.vector.tensor_tensor(out=aT[64:128, tl, 0:4, :], in0=odv[64:128, :, 1, :], in1=rdv[0:64, :, 1, :], op=ALU.mult),
                                       reads=[bankres[4], bankres[5], r_rden[ri]], pwrites=[r_aT[tl]])
                            return s1, s2

                        for h in range(8 if "na" in bparts else 0):
                            sched(*na_item(h))

                        nuw = len(wa_units)

                        def wa_item(kv, u0, pws):
                            us = wa_units[u0:u0 + 2]
                            lastpair = u0 + 2 >= nuw

                            def s1():
                                b0 = alloc_pair()

                                def mms():
                                    for j, u in enumerate(us):
                                        nc.tensor.matmul(psum[:, b0, j * 256:(j + 1) * 256], lhsT=kt_ap(u, 4 + kv, 0),
                                                         rhs=QT[0:64, tl, 4 + 2 * kv:6 + 2 * kv, :].rearrange("p a q -> p (a q)"), start=True, stop=True)
                                        ins = nc.tensor.matmul(psum[:, b0 + 1, j * 256:(j + 1) * 256], lhsT=kt_ap(u, 4 + kv, 64),
                                                               rhs=QT[64:128, tl, 4 + 2 * kv:6 + 2 * kv, :].rearrange("p a q -> p (a q)"), start=True, stop=True)
                                    return ins
                                op(pe, mms, reads=rk_wa + [r_QT], writes=[bankres[b0], bankres[b0 + 1]], pe_inorder=True)
                                for j, u in enumerate(us):
                                    pw = ptw_c[0] % NPTW
                                    ptw_c[0] += 1
                                    pws.append(pw)
                                    op(act, lambda: nc.scalar.activation(out=PTw[pw][:].rearrange("p (b n) -> p b n", b=2),
                                                                         in_=psum[:, b0:b0 + 2, j * 256:(j + 1) * 256], func=AF.Exp),
                                       reads=[bankres[b0], bankres[b0 + 1]], writes=[r_PTw[pw]])
                                    if u[0] == "l" and u[2] != 0:
                                        mi = 0 if u[2] < 0 else 1
                                        pv = PTw[pw][:].rearrange("p (s q) -> p s q", s=4)
                                        op(dve, lambda: nc.vector.tensor_tensor(out=pv, in0=pv, in1=wam_sb[:, mi, :].unsqueeze(1).to_broadcast([128, 4, 128]), op=ALU.mult),
                                           reads=[r_wam], pwrites=[r_PTw[pw]])

                            def s2():
                                ob = 6 + kv

                                def mmo():
                                    for ui, u in enumerate(wa_units):
                                        nc.tensor.matmul(psum[:, ob, 0:256], lhsT=v_ap(u, 8 + kv), rhs=PTw[pws[ui]][:, 0:256],
                                                         start=(ui == 0), stop=(ui == nuw - 1), skip_group_check=True)
                                        ins = nc.tensor.matmul(psum[:, ob, 256:512], lhsT=v_ap(u, 10 + kv), rhs=PTw[pws[ui]][:, 256:512],
                                                               start=False, stop=(ui == nuw - 1), skip_group_check=True)
                                    return ins
                                op(pe, mmo, reads=[r_PTw[p] for p in pws] + rv_wa, writes=[bankres[ob]], pe_inorder=True)
                                if kv == 0:
                                    return
                                ri = rd_c[0] % 2
                                rd_c[0] += 1
                                P2 = psum[:, 6:8, :]
                                rd2 = rden[ri][:, :].rearrange("p (b n) -> p b n", b=2)
                                esk4 = esk[:, 0:8].rearrange("p (k s) -> p k s", k=2)
                                op(dve, lambda: nc.vector.tensor_tensor(out=rd2[64:128, :, 0:256].rearrange("p b (s q) -> p b s q", s=2),
                                                                        in0=P2[64:128, :, 0:256].rearrange("p b (s q) -> p b s q", s=2),
                                                                        in1=esk4[64:128, :, 0:2].unsqueeze(3).to_broadcast([64, 2, 2, 128]), op=ALU.add),
                                   reads=[bankres[6], bankres[7], r_esk], writes=[r_rden[ri]])
                                op(dve, lambda: nc.vector.tensor_tensor(out=rd2[0:64, :, 256:512].rearrange("p b (s q) -> p b s q", s=2),
                                                                        in0=P2[0:64, :, 256:512].rearrange("p b (s q) -> p b s q", s=2),
                                                                        in1=esk4[0:64, :, 2:4].unsqueeze(3).to_broadcast([64, 2, 2, 128]), op=ALU.add),
                                   reads=[bankres[6], bankres[7], r_esk], pwrites=[r_rden[ri]])
                                for (p0, p1, c0, c1) in ((64, 128, 0, 256), (0, 64, 256, 512)):
                                    op(act, lambda: nc.scalar.activation(out=rd2[p0:p1, :, c0:c1], in_=rd2[p0:p1, :, c0:c1], func=AF.Ln),
                                       reads=[r_rden[ri]], pwrites=[r_rden[ri]])
                                    op(act, lambda: nc.scalar.activation(out=rd2[p0:p1, :, c0:c1], in_=rd2[p0:p1, :, c0:c1], func=AF.Exp, scale=-1.0),
                                       reads=[r_rden[ri]], pwrites=[r_rden[ri]])
                                op(dve, lambda: nc.vector.tensor_tensor(out=aT[0:64, tl, 4:8, :].rearrange("p (b s) q -> p b (s q)", b=2),
                                                                        in0=P2[0:64, :, 0:256], in1=rd2[64:128, :, 0:256], op=ALU.mult),
                                   reads=[bankres[6], bankres[7], r_rden[ri]], pwrites=[r_aT[tl]])
                                op(dve, lambda: nc.vector.tensor_tensor(out=aT[64:128, tl, 4:8, :].rearrange("p (b s) q -> p b (s q)", b=2),
                                                                        in0=P2[64:128, :, 256:512], in1=rd2[0:64, :, 256:512], op=ALU.mult),
                                   reads=[bankres[6], bankres[7], r_rden[ri]], pwrites=[r_aT[tl]])
                            return s1, (s2 if lastpair else None)

                        for kv in range(2 if "wa" in bparts else 0):
                            pws = []
                            for u0 in range(0, nuw, 2):
                                sched(*wa_item(kv, u0, pws))

                    def outproj(typ, tl, gpar):
                        xg = xgs[gpar]
                        r_xg = r_xgs[gpar]
                        for nh in range(2):
                            bk = alloc_one()

                            def mmp():
                                for h in range(8):
                                    ins = nc.tensor.matmul(psum[:, bk, :], lhsT=aT[:, tl, h, :], rhs=wo[:, h, nh * 512:(nh + 1) * 512],
                                                           start=(h == 0), stop=(h == 7))
                                return ins
                            op(pe, mmp, reads=[r_aT[tl], r_wo], writes=[bankres[bk]], pe_inorder=True)
                            xsl = xg[:, tl, nh * 512:(nh + 1) * 512]
                            op(dve, lambda: nc.vector.tensor_tensor(out=xsl, in0=psum[:, bk, :], in1=xsl, op=ALU.add),
                               reads=[bankres[bk]], pwrites=[r_xg[tl]])

                    kv_loaded = [0]

                    def load_kv_upto(t1):
                        t1 = min(t1, nkv)
                        t0 = kv_loaded[0]
                        while t0 < t1:
                            n = min(t1 - t0, KTR - (t0 % KTR))
                            sl = t0 % KTR
                            dma(sp, KT[:, sl:sl + n], kT_d[par, t0:t0 + n].rearrange("t p k q -> p t k q"), reads=[r_qkv[par]], writes=r_KTs[sl:sl + n])
                            dma(sp, VW[:, sl:sl + n], v_d[par, t0:t0 + n].rearrange("t p h d -> p t h d"), reads=[r_qkv[par]], writes=r_VWs[sl:sl + n])
                            t0 += n
                        kv_loaded[0] = max(kv_loaded[0], t1)

                    def load_group(gi):
                        typ, tiles = groups[gi]
                        gpar = gi % 2
                        nt = len(tiles)
                        r0 = tiles[0] * 128
                        dma(sp, xgs[gpar][:, 0:nt, :], xres[r0:r0 + nt * 128, :].rearrange("(t p) d -> p t d", p=128),
                            reads=[r_xres], writes=r_xgs[gpar][:nt])
                        dma(sp, QTs[gpar][:, 0:nt], qT_d[par, tiles[0]:tiles[0] + nt].rearrange("t p k q -> p t k q"),
                            reads=[r_qkv[par]], writes=[r_QTs[gpar]])
                        if typ == "x":
                            load_kv_upto(max(tiles[-1] + 3, 4))

                    load_group(0)
                    for gi, (typ, tiles) in enumerate(groups):
                        nt = len(tiles)
                        gpar = gi % 2
                        r0 = tiles[0] * 128
                        if typ == "c":
                            flush()
                            load_wo(1)
                        if gi + 1 < len(groups):
                            load_group(gi + 1)
                        for tl in range(nt):
                            attn_tile2(typ, tl, tiles[tl], gpar)
                            if "proj" in bparts:
                                sched(None, (lambda typ_=typ, tl_=tl, gp_=gpar: outproj(typ_, tl_, gp_)))
                            if mt_fns:
                                mt_fns.pop(0)()
                        flush()
                        if gi == len(groups) - 1:
                            while mt_fns:
                                mt_fns.pop(0)()
                        dma(sp, xres_w[r0:r0 + nt * 128, :].rearrange("(t p) d -> p t d", p=128), xgs[gpar][:, 0:nt, :],
                            reads=r_xgs[gpar][:nt], pwrites=[r_xres])

            seq = []
            for l in range(DEPTH):
                seq.append(("F", l))
                seq.append(("B", l))
            seq.append(("F", DEPTH))
            if btest is not None:
                load_layer_consts(btest)
                seq = [("B", btest)]
            for kind, l in seq:
                if kind == "F":
                    pass_F(l)
                else:
                    pass_B(l)
                if stop_after == (kind, l):
                    break

            r_dump = Res("dump")
            for name, (dst, src) in dump_out.items():
                rr = {"xres": [r_xres], "qT_d": r_qkv, "kT_d": r_qkv, "v_d": r_qkv, "modtab": r_modtab}[name]
                if len(src.shape) == 2:
                    nrow = src.shape[0]
                    step = 1024
                    for a in range(0, nrow, step):
                        b = min(nrow, a + step)
                        dma(sp, dst[a:b], src[a:b], reads=rr, pwrites=[r_dump])
                else:
                    for a in range(src.shape[0]):
                        if len(src.shape) >= 5:
                            for b in range(src.shape[1]):
                                dma(sp, dst[a, b], src[a, b], reads=rr, pwrites=[r_dump])
                        else:
                            dma(sp, dst[a], src[a], reads=rr, pwrites=[r_dump])
            final = {}
            _merge(final, r_yout.w)
            _merge(final, r_dump.w)
            _merge(final, r_xres.w)
            for p, v in final.items():
                nc.sync.wait_ge(p.sem, v)
            for s in pool.dsem:
                if s.val:
                    nc.gpsimd.wait_ge(s.sem, s.val)
            stats = dict(pe=pe.nins, act=act.nins, dve=dve.nins, waits=pe.nwait + act.nwait + dve.nwait + sp.nwait + pool.nwait,
                         dmas=sp.di + pool.di)
            print("program stats:", stats, "sbuf_remaining", getattr(nc, "sbuf_bytes_remaining", None))

        block.sync(program)
    return nc


def _rope_table(parity):
    rot = 32
    inv_freq = (10000.0 ** (-np.arange(0, rot, 2, dtype=np.float32) / np.float32(rot))).astype(np.float32)
    tl = np.arange(NLAT * 128)
    tg = tl if parity == 0 else (SEQ - 1 - tl)
    row = (tg // 64).astype(np.float32)
    col = (tg % 64).astype(np.float32)
    ang = np.stack([row[:, None] * inv_freq, col[:, None] * inv_freq], axis=1).astype(np.float32)
    c, s = np.cos(ang).astype(np.float32), np.sin(ang).astype(np.float32)
    tab = np.zeros((NLAT * 128, 128), np.float32)
    c64 = np.stack([c, c], axis=2)
    tab[:, 0:64] = c64.reshape(-1, 64)
    tab[:, 64:96] = (-s).reshape(-1, 32)
    tab[:, 96:128] = s.reshape(-1, 32)
    return tab


def _na_bias_tables(rpb, parity):
    L = rpb.shape[0]
    out = np.full((L, 13, 8, 128, 128), MASKV, np.float32)
    pats = [(10, r, r + 2) for r in range(-2, 3)] + [(0, r, 5 + r) for r in range(0, 4)] + [(1, r, 10 + r) for r in range(-1, 3)]
    idx = np.arange(128)
    for (j, rel, slot) in pats:
        ql = idx[None, :]
        kl = idx[:, None]
        qr_l = 2 * j + ql // 64
        qc_l = ql % 64
        kr_l = 2 * (j + rel) + kl // 64
        kc_l = kl % 64
        if parity == 0:
            qr, qc, kr, kc = qr_l, qc_l, kr_l, kc_l
        else:
            qr, qc, kr, kc = 127 - qr_l, 63 - qc_l, 127 - kr_l, 63 - kc_l
        r0 = np.clip(qr - 4, 0, 120)
        c0 = np.clip(qc - 8, 0, 48)
        valid = (kr >= r0) & (kr < r0 + 8) & (kc >= c0) & (kc < c0 + 16)
        valid = np.broadcast_to(valid, (128, 128))
        dr = np.clip(kr - qr + 7, 0, 14)
        dc = np.clip(kc - qc + 15, 0, 30)
        dr = np.broadcast_to(dr, (128, 128))
        dc = np.broadcast_to(dc, (128, 128))
        g = rpb[:, :, dr, dc]
        out[:, slot] = np.where(valid[None, None], g, np.float32(MASKV))
    return out


_NC_CACHE = {}


def make_in_maps(x, c, ctx, c_ctx, w_ada, b_ada, norm_g, w_ffn_up, w_ffn_down, w_in, w_out, qk_norm_g, na_rpb, wa_sink):
    f = lambda a: np.ascontiguousarray(np.asarray(a, dtype=np.float32))
    x, c, ctx, c_ctx = f(x), f(c), f(ctx), f(c_ctx)
    shared = dict(w_ada=f(w_ada), b_ada=f(b_ada), norm_g=f(norm_g), w_up=f(w_ffn_up), w_dn=f(w_ffn_down), w_in=f(w_in),
                  w_out=f(w_out), qkg=f(qk_norm_g), identf_in=np.eye(128, dtype=np.float32))
    sink = f(wa_sink)
    shared["sinkp"] = np.ascontiguousarray(sink[:, [0, 2, 1, 3, 4, 6, 5, 7]])
    idx = np.arange(128)
    wam = np.stack([(idx[:, None] >= idx[None, :]), (idx[:, None] <= idx[None, :])]).astype(np.float32).astype(ml_dtypes.bfloat16)
    shared["wam"] = wam
    rpb = f(na_rpb)
    tabs = [(_rope_table(p), _na_bias_tables(rpb, p)) for p in range(2)]
    in_maps = []
    for core in range(8):
        b, p = core // 2, core % 2
        xs = x[b] if p == 0 else x[b, ::-1]
        m = dict(shared)
        m["xin"] = np.ascontiguousarray(xs[:NLAT * 128])
        m["ctxin"] = np.ascontiguousarray(ctx[b])
        m["cvec"] = np.ascontiguousarray(np.stack([c[b], c_ctx]))
        m["rope"] = tabs[p][0]
        m["nab"] = tabs[p][1]
        in_maps.append(m)
    return in_maps


def kernel(x, c, ctx, c_ctx, w_ada, b_ada, norm_g, w_ffn_up, w_ffn_down, w_in, w_out, qk_norm_g, na_rpb, wa_sink):
    in_maps = make_in_maps(x, c, ctx, c_ctx, w_ada, b_ada, norm_g, w_ffn_up, w_ffn_down, w_in, w_out, qk_norm_g, na_rpb, wa_sink)
    if "nc" not in _NC_CACHE:
        _NC_CACHE["nc"] = build_program()
    nc = _NC_CACHE["nc"]
    res = run_bass_kernel_spmd(nc, in_maps, core_ids=list(range(8)))
    out = np.empty((4, SEQ, D), np.float32)
    for core in range(8):
        b, p = core // 2, core % 2
        y = np.asarray(res.results[core]["yout"], dtype=np.float32)
        if p == 0:
            out[b, 0:4096] = y
        else:
            out[b, 4096:] = y[::-1]
    return out
```
